# Optimizing a Trainium2 kernel written in Bass

```python
import jax
import jax.numpy as jnp
from jax import lax
import numpy as np

D_MODEL = 1024
BATCH = 8
SEQ = 4096
DEPTH = 4

FFN_DIM = 2816
NORM_EPS = 1e-6
A_HEADS = 8
A_HEAD_DIM = 64
A_WIDTH = A_HEADS * A_HEAD_DIM
ROT_DIM = A_HEAD_DIM // 4
ROPE_THETA = 500000.0
DILATED_PATTERNS = ((128, 1), (512, 4), (2048, 16))
WIN_BLOCK = 128
CONV_CHANNELS = D_MODEL - A_WIDTH
CONV_WIDTH = 31
HYB_IN = 3 * A_WIDTH + 2 * CONV_CHANNELS
GDN_HEADS = 8
GDN_KEY_DIM = 128
GDN_VALUE_DIM = 128
GDN_QK_W = GDN_HEADS * GDN_KEY_DIM
GDN_V_W = GDN_HEADS * GDN_VALUE_DIM
GDN_QKV_W = 2 * GDN_QK_W + GDN_V_W
GDN_SHORT_CONV = 4
GDN_CHUNK = 64
GDN_IN = GDN_QKV_W + GDN_V_W + 2 * GDN_HEADS
N_EVEN = (DEPTH + 1) // 2
N_ODD = DEPTH // 2

kernel_name = "hybrid_dilated_conformer_gdn_trunk"


def rms_norm(x, g):
    xf = x.astype(jnp.float32)
    y = xf * lax.rsqrt(jnp.mean(xf * xf, axis=-1, keepdims=True) + NORM_EPS)
    return (y * g.astype(jnp.float32)).astype(x.dtype)


def layer_norm(x, g, b):
    xf = x.astype(jnp.float32)
    mu = jnp.mean(xf, axis=-1, keepdims=True)
    xc = xf - mu
    y = xc * lax.rsqrt(jnp.mean(xc * xc, axis=-1, keepdims=True) + NORM_EPS)
    return (y * g.astype(jnp.float32) + b.astype(jnp.float32)).astype(x.dtype)


def l2_normalize(x):
    xf = x.astype(jnp.float32)
    return xf * lax.rsqrt(jnp.sum(xf * xf, axis=-1, keepdims=True) + NORM_EPS)


def swiglu_ffn(h, w_in, w_out):
    gate, up = jnp.split(h @ w_in, 2, axis=-1)
    return (jax.nn.silu(gate) * up) @ w_out


def causal_depthwise_conv(x, w):
    width, ch = w.shape
    return lax.conv_general_dilated(
        x, w[:, None, :].astype(x.dtype), window_strides=(1,), padding=[(width - 1, 0)],
        dimension_numbers=("NWC", "WIO", "NWC"), feature_group_count=ch)


def rotary_angles(positions):
    inv_freq = jnp.power(jnp.float32(ROPE_THETA),
                         -jnp.arange(0, ROT_DIM, 2, dtype=jnp.float32) / ROT_DIM)
    ang = positions.astype(jnp.float32)[..., None] * inv_freq
    return jnp.cos(ang)[:, :, None, :], jnp.sin(ang)[:, :, None, :]


def apply_partial_rotary(x, cos, sin):
    half = ROT_DIM // 2
    xr = x[..., :ROT_DIM].astype(jnp.float32)
    x1, x2 = xr[..., :half], xr[..., half:]
    rot = jnp.concatenate([x1 * cos - x2 * sin, x2 * cos + x1 * sin], axis=-1).astype(x.dtype)
    return jnp.concatenate([rot, x[..., ROT_DIM:]], axis=-1)


def dilated_window_branch(q, k, v, window, dilation):
    B, S, H, hd = q.shape
    span = window // dilation
    L = S // dilation
    nb = -(-L // WIN_BLOCK)
    Lp = nb * WIN_BLOCK

    def to_blocks(t):
        t = jnp.moveaxis(t.reshape(B, L, dilation, H, hd), 2, 1)
        t = jnp.pad(t, ((0, 0), (0, 0), (0, Lp - L), (0, 0), (0, 0)))
        return t.reshape(B, dilation, nb, WIN_BLOCK, H, hd)

    def with_prev(t):
        prev = jnp.pad(t, ((0, 0), (0, 0), (1, 0), (0, 0), (0, 0), (0, 0)))[:, :, :nb]
        return jnp.concatenate([prev, t], axis=3)

    qb = to_blocks(q)
    kb = with_prev(to_blocks(k))
    vb = with_prev(to_blocks(v))
    s = jnp.einsum("brnqhd,brnkhd->brnhqk", qb, kb).astype(jnp.float32) * (hd ** -0.5)
    blk = jnp.arange(nb)[:, None, None] * WIN_BLOCK
    q_idx = blk + jnp.arange(WIN_BLOCK)[None, :, None]
    k_idx = blk - WIN_BLOCK + jnp.arange(2 * WIN_BLOCK)[None, None, :]
    dist = q_idx - k_idx
    allowed = (dist >= 0) & (dist <= span) & (k_idx >= 0)
    s = jnp.where(allowed[:, None], s, -jnp.inf)
    m = jnp.max(s, axis=-1, keepdims=True)
    p = jnp.exp(s - m)
    den = jnp.sum(p, axis=-1, keepdims=True)
    o = jnp.einsum("brnhqk,brnkhd->brnqhd", (p / den).astype(v.dtype), vb)
    lse = jnp.swapaxes((m + jnp.log(den))[..., 0], -1, -2)

    def from_blocks(t):
        t = t.reshape((B, dilation, Lp) + t.shape[4:])[:, :, :L]
        return jnp.moveaxis(t, 1, 2).reshape((B, S) + t.shape[3:])

    return from_blocks(o), from_blocks(lse)


def hybrid_mixer(h, cos, sin, w_in, dw_w, dw_b, ln_g, ln_b, w_out):
    B, S, _ = h.shape
    proj = h @ w_in
    heads = lambda t: t.reshape(B, S, A_HEADS, A_HEAD_DIM)
    q = apply_partial_rotary(heads(proj[..., :A_WIDTH]), cos, sin)
    k = apply_partial_rotary(heads(proj[..., A_WIDTH:2 * A_WIDTH]), cos, sin)
    v = heads(proj[..., 2 * A_WIDTH:3 * A_WIDTH])
    outs, lses = [], []
    for window, dilation in DILATED_PATTERNS:
        o_g, lse_g = dilated_window_branch(q, k, v, window, dilation)
        outs.append(o_g)
        lses.append(lse_g)
    mix_w = jax.nn.softmax(jnp.stack(lses), axis=0)
    attn = jnp.einsum("gbsh,gbshd->bshd", mix_w, jnp.stack(outs).astype(jnp.float32))
    attn = attn.reshape(B, S, A_WIDTH).astype(h.dtype)
    u = proj[..., 3 * A_WIDTH:]
    glu = u[..., :CONV_CHANNELS] * jax.nn.sigmoid(u[..., CONV_CHANNELS:])
    c = causal_depthwise_conv(glu, dw_w) + dw_b
    c = jax.nn.silu(layer_norm(c, ln_g, ln_b))
    return jnp.concatenate([attn, c], axis=-1) @ w_out


def gated_delta_rule(q, k, v, g, beta):
    B, S, H, dk = q.shape
    C = GDN_CHUNK
    N = S // C

    def chunked(t):
        return jnp.moveaxis(t.reshape((B, N, C, H) + t.shape[3:]), 3, 1)

    q = chunked(q) * (dk ** -0.5)
    k = chunked(k)
    v = chunked(v)
    beta = chunked(beta)
    g = jnp.cumsum(chunked(g), axis=-1)
    causal = jnp.tril(jnp.ones((C, C), dtype=bool))
    strict = jnp.tril(jnp.ones((C, C), dtype=bool), -1)
    decay = jnp.exp(jnp.where(causal, g[..., :, None] - g[..., None, :], -jnp.inf))
    k_beta = k * beta[..., None]
    l_mat = jnp.where(strict, jnp.einsum("bhnik,bhnjk->bhnij", k_beta, k) * decay, 0.0)
    eye = jnp.eye(C, dtype=q.dtype)
    t_inv = lax.linalg.triangular_solve(l_mat + eye, jnp.broadcast_to(eye, l_mat.shape),
                                        left_side=True, lower=True, unit_diagonal=True)
    u = jnp.einsum("bhnij,bhnjv->bhniv", t_inv, v * beta[..., None])
    w = jnp.einsum("bhnij,bhnjk->bhnik", t_inv, k_beta * jnp.exp(g)[..., None])
    attn = jnp.where(causal, jnp.einsum("bhnik,bhnjk->bhnij", q, k) * decay, 0.0)
    q_dec = q * jnp.exp(g)[..., None]
    g_last = g[..., -1]
    k_dec = k * jnp.exp(g_last[..., None] - g)[..., None]
    xs = tuple(jnp.moveaxis(t, 2, 0) for t in (w, u, q_dec, k_dec, attn, g_last))

    def step(state, inp):
        w_c, u_c, q_c, k_c, a_c, gl_c = inp
        v_new = u_c - jnp.einsum("bhik,bhkv->bhiv", w_c, state)
        o_c = jnp.einsum("bhik,bhkv->bhiv", q_c, state) + jnp.einsum("bhij,bhjv->bhiv", a_c, v_new)
        state = state * jnp.exp(gl_c)[..., None, None] + jnp.einsum("bhik,bhiv->bhkv", k_c, v_new)
        return state, o_c

    state0 = jnp.zeros((B, H, dk, v.shape[-1]), q.dtype)
    _, o = lax.scan(step, state0, xs)
    return jnp.transpose(o, (1, 0, 3, 2, 4)).reshape(B, S, H, -1)


def gated_deltanet_mixer(h, w_in, conv_w, a_log, dt_bias, norm_g, w_out):
    B, S, _ = h.shape
    proj = h @ w_in
    qkv = jax.nn.silu(causal_depthwise_conv(proj[..., :GDN_QKV_W], conv_w))
    z = proj[..., GDN_QKV_W:GDN_QKV_W + GDN_V_W]
    b = proj[..., GDN_QKV_W + GDN_V_W:GDN_QKV_W + GDN_V_W + GDN_HEADS]
    a = proj[..., GDN_QKV_W + GDN_V_W + GDN_HEADS:]
    q = qkv[..., :GDN_QK_W].reshape(B, S, GDN_HEADS, GDN_KEY_DIM)
    k = qkv[..., GDN_QK_W:2 * GDN_QK_W].reshape(B, S, GDN_HEADS, GDN_KEY_DIM)
    v = qkv[..., 2 * GDN_QK_W:].reshape(B, S, GDN_HEADS, GDN_VALUE_DIM)
    beta = jax.nn.sigmoid(b.astype(jnp.float32))
    g = -jnp.exp(a_log.astype(jnp.float32)) * jax.nn.softplus(
        a.astype(jnp.float32) + dt_bias.astype(jnp.float32))
    o = gated_delta_rule(l2_normalize(q), l2_normalize(k), v.astype(jnp.float32), g, beta)
    o = rms_norm(o, norm_g) * jax.nn.silu(
        z.reshape(B, S, GDN_HEADS, GDN_VALUE_DIM).astype(jnp.float32))
    return o.reshape(B, S, GDN_V_W).astype(h.dtype) @ w_out


def setup_inputs(seed: int = 0) -> dict:
    key = jax.random.key(seed)
    ks = jax.random.split(key, 24)
    f32 = jnp.float32

    def dense(kk, shape, fan_in):
        return jax.random.normal(kk, shape, f32) * (fan_in ** -0.5)

    def gain(kk, shape):
        return 1.0 + 0.02 * jax.random.normal(kk, shape, f32)

    x = jax.random.normal(ks[0], (BATCH, SEQ, D_MODEL), f32)
    offsets = jax.random.randint(ks[1], (BATCH, 1), 0, 1024, dtype=jnp.int32)
    positions = offsets + jnp.arange(SEQ, dtype=jnp.int32)[None, :]
    dt = jnp.exp(jax.random.uniform(ks[18], (N_ODD, GDN_HEADS), f32,
                                    float(np.log(1e-3)), float(np.log(1e-1))))
    return {
        "x": x,
        "positions": positions,
        "ffn1_norm": gain(ks[2], (DEPTH, D_MODEL)),
        "ffn1_w_in": dense(ks[3], (DEPTH, D_MODEL, 2 * FFN_DIM), D_MODEL),
        "ffn1_w_out": dense(ks[4], (DEPTH, FFN_DIM, D_MODEL), FFN_DIM),
        "mix_norm": gain(ks[5], (DEPTH, D_MODEL)),
        "ffn2_norm": gain(ks[6], (DEPTH, D_MODEL)),
        "ffn2_w_in": dense(ks[7], (DEPTH, D_MODEL, 2 * FFN_DIM), D_MODEL),
        "ffn2_w_out": dense(ks[8], (DEPTH, FFN_DIM, D_MODEL), FFN_DIM),
        "hyb_w_in": dense(ks[9], (N_EVEN, D_MODEL, HYB_IN), D_MODEL),
        "hyb_dw_w": dense(ks[10], (N_EVEN, CONV_WIDTH, CONV_CHANNELS), CONV_WIDTH),
        "hyb_dw_b": 0.02 * jax.random.normal(ks[11], (N_EVEN, CONV_CHANNELS), f32),
        "hyb_ln_g": gain(ks[12], (N_EVEN, CONV_CHANNELS)),
        "hyb_ln_b": 0.02 * jax.random.normal(ks[13], (N_EVEN, CONV_CHANNELS), f32),
        "hyb_w_out": dense(ks[14], (N_EVEN, A_WIDTH + CONV_CHANNELS, D_MODEL), A_WIDTH + CONV_CHANNELS),
        "gdn_w_in": dense(ks[15], (N_ODD, D_MODEL, GDN_IN), D_MODEL),
        "gdn_conv_w": dense(ks[16], (N_ODD, GDN_SHORT_CONV, GDN_QKV_W), GDN_SHORT_CONV),
        "gdn_A_log": jnp.log(jax.random.uniform(ks[17], (N_ODD, GDN_HEADS), f32, 1.0, 16.0)),
        "gdn_dt_bias": dt + jnp.log(-jnp.expm1(-dt)),
        "gdn_norm_g": gain(ks[19], (N_ODD, GDN_VALUE_DIM)),
        "gdn_w_out": dense(ks[20], (N_ODD, GDN_V_W, D_MODEL), GDN_V_W),
        "final_norm": gain(ks[21], (D_MODEL,)),
    }


def reference(x, positions, ffn1_norm, ffn1_w_in, ffn1_w_out, mix_norm, ffn2_norm, ffn2_w_in,
              ffn2_w_out, hyb_w_in, hyb_dw_w, hyb_dw_b, hyb_ln_g, hyb_ln_b, hyb_w_out,
              gdn_w_in, gdn_conv_w, gdn_A_log, gdn_dt_bias, gdn_norm_g, gdn_w_out, final_norm):
    cos, sin = rotary_angles(positions)
    h = x
    for layer in range(DEPTH):
        h = h + 0.5 * swiglu_ffn(rms_norm(h, ffn1_norm[layer]), ffn1_w_in[layer], ffn1_w_out[layer])
        hn = rms_norm(h, mix_norm[layer])
        i = layer // 2
        if layer % 2 == 0:
            mix = hybrid_mixer(hn, cos, sin, hyb_w_in[i], hyb_dw_w[i], hyb_dw_b[i],
                               hyb_ln_g[i], hyb_ln_b[i], hyb_w_out[i])
        else:
            mix = gated_deltanet_mixer(hn, gdn_w_in[i], gdn_conv_w[i], gdn_A_log[i],
                                       gdn_dt_bias[i], gdn_norm_g[i], gdn_w_out[i])
        h = h + mix
        h = h + 0.5 * swiglu_ffn(rms_norm(h, ffn2_norm[layer]), ffn2_w_in[layer], ffn2_w_out[layer])
    return rms_norm(h, final_norm)
```

```python
from contextlib import ExitStack
import concourse.bass as bass
import concourse.mybir as mybir

F32 = mybir.dt.float32
BF16 = mybir.dt.bfloat16
I32 = mybir.dt.int32
AF = mybir.ActivationFunctionType
ALU = mybir.AluOpType
AX = mybir.AxisListType

ENGS = ("pe", "act", "dve", "pool", "sp")
EP = 30000
NSLOT = 8
DMA_EP = 1800


class Op:
    __slots__ = ("eng", "idx", "fn", "deps", "signal", "sig", "dma", "slot", "epoch", "val")

    def __init__(self, eng, fn, dma):
        self.eng = eng
        self.fn = fn
        self.dma = dma
        self.deps = []
        self.signal = False
        self.sig = 0


class Buf:
    def __init__(self, name, t):
        self.name = name
        self.t = t
        self.e = {}

    def __getitem__(self, k):
        return self.t[k]

    def _ents(self, key):
        if key is None:
            return list(self.e.values())
        return [self.e[k] for k in (key, None) if k in self.e]

    def read(self, op, key):
        deps = [en[0] for en in self._ents(key) if en[0] is not None]
        en = self.e.setdefault(key, [None, {}, []])
        if op.dma:
            en[2].append(op)
        else:
            en[1][op.eng] = op
        return deps

    def write(self, op, key):
        deps = []
        for en in self._ents(key):
            if en[0] is not None:
                deps.append(en[0])
            deps.extend(en[1].values())
            deps.extend(en[2])
        if key is None:
            self.e = {None: [op, {}, []]}
        else:
            self.e[key] = [op, {}, []]
        return deps


class Prog:
    def __init__(self, nc):
        self.nc = nc
        self.ops = {e: [] for e in ENGS}
        self.stack = ExitStack()
        self.nsem = 0
        self.dq = {e: {"n": 0, "last": [None] * NSLOT, "uses": [0] * NSLOT, "epoch": [0] * NSLOT} for e in ENGS}
        self._esem = {}
        self._dsem = {}
        self.ntile = 0

    SB_LO = 16512
    SB_HI = 229376

    def sb(self, name, shape, dtype):
        self.ntile += 1
        isz = 2 if dtype == BF16 else 4
        n = 1
        for d in shape[1:]:
            n *= d
        nbytes = (n * isz + 63) // 64 * 64
        off = getattr(self, "sb_ptr", self.SB_LO)
        assert off + nbytes <= self.SB_HI, f"SBUF overflow allocating {name}: {off}+{nbytes}"
        self.sb_ptr = off + nbytes
        self.sb_peak = max(getattr(self, "sb_peak", 0), self.sb_ptr)
        t = self.nc.alloc_sbuf_tensor_at(f"{name}_{self.ntile}", list(shape), dtype, offset=off)
        return Buf(name, t)

    def sb_mark(self):
        return getattr(self, "sb_ptr", self.SB_LO)

    def sb_release(self, mark):
        self.sb_ptr = mark

    def barrier(self):
        lasts = []
        for e in ENGS:
            for op in reversed(self.ops[e]):
                if not op.dma and op.fn is not None:
                    lasts.append(op)
                    break
            q = self.dq[e]
            for s_ in range(NSLOT):
                if q["last"][s_] is not None:
                    lasts.append(q["last"][s_])
        for e in ENGS:
            op = Op(e, None, False)
            op.idx = len(self.ops[e])
            for d in lasts:
                if (not d.dma) and d.eng == e:
                    continue
                op.deps.append(d)
                if not d.dma:
                    d.signal = True
            self.ops[e].append(op)

    def ps(self, name, shape, dtype=F32):
        self.ntile += 1
        t = self.stack.enter_context(self.nc.psum_tensor(f"{name}_{self.ntile}", list(shape), dtype))
        return Buf(name, t)

    def dram(self, name, shape, dtype, kind="Internal"):
        t = self.nc.dram_tensor(name, list(shape), dtype, kind=kind)
        return Buf(name, t)

    def _sem(self, name):
        self.nsem += 1
        return self.stack.enter_context(self.nc.semaphore(f"{name}_{self.nsem}"))

    def add(self, eng, fn, reads=(), writes=(), dma=False):
        op = Op(eng, fn, dma)
        op.idx = len(self.ops[eng])
        deps = []
        for r in reads:
            b, k = r if isinstance(r, tuple) else (r, None)
            if b is not None:
                deps.extend(b.read(op, k))
        for w in writes:
            b, k = w if isinstance(w, tuple) else (w, None)
            if b is not None:
                deps.extend(b.write(op, k))
        if dma:
            q = self.dq[eng]
            s = q["n"] % NSLOT
            q["n"] += 1
            if q["last"][s] is not None:
                deps.append(q["last"][s])
            if q["uses"][s] >= DMA_EP:
                q["uses"][s] = 0
                q["epoch"][s] += 1
            q["uses"][s] += 1
            op.slot = s
            op.epoch = q["epoch"][s]
            op.val = 16 * q["uses"][s]
            q["last"][s] = op
        seen = set()
        for d in deps:
            if d is op or id(d) in seen:
                continue
            seen.add(id(d))
            if (not d.dma) and d.eng == "pe" and eng == "pe" and not dma:
                continue
            op.deps.append(d)
            if not d.dma:
                d.signal = True
        self.ops[eng].append(op)
        return op

    def dma(self, out, in_, reads=(), writes=(), eng="sp", **kw):
        return self.add(eng, lambda h: h.dma_start(out=out, in_=in_, **kw), reads, writes, dma=True)

    def mm(self, out, lhsT, rhs, start=True, stop=True, reads=(), writes=(), **kw):
        return self.add("pe", lambda h: h.matmul(out, lhsT, rhs, start=start, stop=stop, **kw), reads, writes)

    def tr(self, out, in_, ident, reads=(), writes=()):
        return self.add("pe", lambda h: h.transpose(out, in_, ident), reads, writes)

    def act(self, out, in_, func, reads=(), writes=(), eng="act", **kw):
        return self.add(eng, lambda h: h.activation(out=out, in_=in_, func=func, **kw), reads, writes)

    def tt(self, eng, out, in0, in1, op, reads=(), writes=()):
        return self.add(eng, lambda h: h.tensor_tensor(out=out, in0=in0, in1=in1, op=op), reads, writes)

    def ts(self, eng, out, in0, s1, s2, op0, op1=None, reads=(), writes=()):
        if op1 is None:
            return self.add(eng, lambda h: h.tensor_scalar(out=out, in0=in0, scalar1=s1, scalar2=None, op0=op0), reads, writes)
        return self.add(eng, lambda h: h.tensor_scalar(out=out, in0=in0, scalar1=s1, scalar2=s2, op0=op0, op1=op1), reads, writes)

    def stt(self, eng, out, in0, scalar, in1, op0, op1, reads=(), writes=()):
        return self.add(eng, lambda h: h.scalar_tensor_tensor(out=out, in0=in0, scalar=scalar, in1=in1, op0=op0, op1=op1), reads, writes)

    def copy(self, eng, out, in_, reads=(), writes=()):
        if eng == "act":
            return self.add(eng, lambda h: h.copy(out=out, in_=in_), reads, writes)
        return self.add(eng, lambda h: h.tensor_copy(out=out, in_=in_), reads, writes)

    def memset(self, eng, ap, val, writes=()):
        return self.add(eng, lambda h: h.memset(ap, val), (), writes)

    def esem(self, e, ep):
        k = (e, ep)
        if k not in self._esem:
            self._esem[k] = self._sem(f"e_{e}{ep}")
        return self._esem[k]

    def dsem(self, e, slot, ep):
        k = (e, slot, ep)
        if k not in self._dsem:
            self._dsem[k] = self._sem(f"d_{e}{slot}_{ep}")
        return self._dsem[k]

    def emit(self):
        nc = self.nc
        for e in ENGS:
            n = 0
            for op in self.ops[e]:
                if (not op.dma) and op.signal:
                    n += 1
                    op.sig = n
        for e in ENGS:
            for op in self.ops[e]:
                if op.dma:
                    self.dsem(e, op.slot, op.epoch)
                elif op.signal:
                    self.esem(e, (op.sig - 1) // EP)
        final = []
        for e in ENGS:
            q = self.dq[e]
            for s in range(NSLOT):
                if q["last"][s] is not None:
                    o = q["last"][s]
                    final.append((self.dsem(e, o.slot, o.epoch), o.val))
        P = self

        def run(e, h):
            weng = {}
            wdma = {}
            for op in P.ops[e]:
                for d in op.deps:
                    if d.dma:
                        k = (d.eng, d.slot, d.epoch)
                        if wdma.get(k, 0) < d.val:
                            h.wait_ge(P.dsem(*k), d.val)
                            wdma[k] = d.val
                    else:
                        if weng.get(d.eng, 0) < d.sig:
                            ep, v = divmod(d.sig - 1, EP)
                            h.wait_ge(P.esem(d.eng, ep), v + 1)
                            weng[d.eng] = d.sig
                if op.fn is None:
                    continue
                ins = op.fn(h)
                if op.dma:
                    ins.then_inc(P.dsem(e, op.slot, op.epoch), 16)
                elif op.signal:
                    ins.then_inc(P.esem(e, (op.sig - 1) // EP), 1)
            if e == "sp":
                for sem, v in final:
                    h.wait_ge(sem, v)

        with nc.Block() as block:
            @block.tensor
            def _(h):
                run("pe", h)

            @block.scalar
            def _(h):
                run("act", h)

            @block.vector
            def _(h):
                run("dve", h)

            @block.gpsimd
            def _(h):
                run("pool", h)

            @block.sync
            def _(h):
                run("sp", h)

    def close(self):
        self.stack.close()

import numpy as np
from concourse.bass_utils import run_bass_kernel_spmd

D = 1024
FF = 2816
NFC = FF // 128
EPS = 1e-6
NCORES = 8


class Ctx:
    pass


def cast_weight(P, C, src_ap, dst, R, Cc, rowkey):
    CB = 1024
    i = 0
    for r0 in range(0, R, 128):
        rr = min(128, R - r0)
        for c0 in range(0, Cc, CB):
            cc = min(CB, Cc - c0)
            a = C.cv32[i % 2]
            b = C.cv16[i % 2]
            P.dma(a.t[0:rr, 0:cc], src_ap[r0:r0 + rr, c0:c0 + cc], writes=[a])
            P.copy("pool", b.t[0:rr, 0:cc], a.t[0:rr, 0:cc], reads=[a], writes=[b])
            P.dma(dst.t[r0:r0 + rr, c0:c0 + cc], b.t[0:rr, 0:cc], reads=[b], writes=[(dst, (rowkey, r0, c0))])
            i += 1


def rmsnorm_tile(P, C, src, gcol, dst, ntok, srckey=None, doff=0):
    for s0 in range(0, ntok, 512):
        n = min(512, ntok - s0)
        sl = slice(s0, s0 + n)
        ps = C.PS[7]
        for c in range(8):
            P.act(C.sq.t[:, c, 0:n], src.t[:, c, sl], AF.Square, reads=[(src, srckey)], writes=[(C.sq, c)])
        for c in range(8):
            P.mm(ps.t[:, 0:n], C.ones_bf.t[:, :], C.sq.t[:, c, 0:n], start=(c == 0), stop=(c == 7),
                 reads=[(C.sq, c), C.ones_bf], writes=[ps])
        P.act(C.rstd.t[:, 0:n], ps.t[:, 0:n], AF.Ln, reads=[ps, C.cst], writes=[C.rstd], scale=1.0 / D, bias=C.cst.t[:, 0:1])
        P.act(C.rstd.t[:, 0:n], C.rstd.t[:, 0:n], AF.Exp, reads=[C.rstd], writes=[C.rstd], scale=-0.5)
        for c in range(8):
            P.stt("dve", dst.t[:, c, doff + s0:doff + s0 + n], src.t[:, c, sl], gcol[:, c:c + 1], C.rstd.t[:, 0:n], ALU.mult, ALU.mult,
                  reads=[(src, srckey), C.rstd, C.gains], writes=[(dst, (c, doff + s0))])


def ffn_phase(P, C, S, T, gcol, w_in, w_out):
    hview = C.hT.t.rearrange("(c p) t -> p c t", p=128)
    WB = 512
    nwb = FF // WB + (1 if FF % WB else 0)
    wi = 0
    wo_i = 0
    for tt in range(S // T):
        tsl = slice(tt * T, (tt + 1) * T)
        P.dma(C.htile.t[:, :, :], hview[:, :, tsl], reads=[(C.hT, tt)], writes=[C.htile])
        rmsnorm_tile(P, C, C.htile, gcol, C.xn, T)
        for wb in range(nwb):
            c0 = wb * WB
            cw = min(WB, FF - c0)
            wg = C.wg[wi % 2]
            wu = C.wu[wi % 2]
            wi += 1
            P.dma(wg.t[:, :, 0:cw], w_in.t[:, c0:c0 + cw].rearrange("(c p) f -> p c f", p=128), reads=[w_in], writes=[wg])
            P.dma(wu.t[:, :, 0:cw], w_in.t[:, FF + c0:FF + c0 + cw].rearrange("(c p) f -> p c f", p=128), reads=[w_in], writes=[wu])
            for fi in range(cw // 128):
                f = c0 // 128 + fi
                fs = slice(fi * 128, (fi + 1) * 128)
                for s0 in range(0, T, 512):
                    sl = slice(s0, s0 + 512)
                    k = (f * (T // 512) + s0 // 512) % 2
                    pg = C.PS[k]
                    pu = C.PS[2 + k]
                    sg = C.sg[k]
                    for c in range(8):
                        P.mm(pg.t[:, :], wg.t[:, c, fs], C.xn.t[:, c, sl], start=(c == 0), stop=(c == 7),
                             reads=[wg, C.xn], writes=[pg])
                    for c in range(8):
                        P.mm(pu.t[:, :], wu.t[:, c, fs], C.xn.t[:, c, sl], start=(c == 0), stop=(c == 7),
                             reads=[wu, C.xn], writes=[pu])
                    P.act(sg.t[:, :], pg.t[:, :], AF.Silu, reads=[pg], writes=[sg])
                    P.tt("dve", C.gT.t[:, f, sl], sg.t[:, :], pu.t[:, :], ALU.mult, reads=[sg, pu], writes=[(C.gT, (f, s0))])
        OB = 256
        for ob in range(D // OB):
            wo = C.wo[wo_i % 2]
            wo_i += 1
            P.dma(wo.t[:, :, :], w_out.t[:, ob * OB:(ob + 1) * OB].rearrange("(f p) d -> p f d", p=128), reads=[w_out], writes=[wo])
            for di in range(OB // 128):
                dc = ob * (OB // 128) + di
                for s0 in range(0, T, 512):
                    sl = slice(s0, s0 + 512)
                    k = (dc * (T // 512) + s0 // 512) % 2
                    py = C.PS[4 + k]
                    for f in range(NFC):
                        P.mm(py.t[:, :], wo.t[:, f, di * 128:(di + 1) * 128], C.gT.t[:, f, sl], start=(f == 0), stop=(f == NFC - 1),
                             reads=[wo, C.gT], writes=[py])
                    P.stt("dve", C.htile.t[:, dc, sl], py.t[:, :], 0.5, C.htile.t[:, dc, sl], ALU.mult, ALU.add,
                          reads=[py, (C.htile, (dc, s0))], writes=[(C.htile, (dc, s0))])
        P.dma(hview[:, :, tsl], C.htile.t[:, :, :], reads=[C.htile], writes=[(C.hT, tt)])


def final_phase(P, C, S, T, gcol, outT):
    hview = C.hT.t.rearrange("(c p) t -> p c t", p=128)
    oview = outT.t.rearrange("(c p) t -> p c t", p=128)
    for tt in range(S // T):
        tsl = slice(tt * T, (tt + 1) * T)
        P.dma(C.htile.t[:, :, :], hview[:, :, tsl], reads=[(C.hT, tt)], writes=[C.htile])
        rmsnorm_tile(P, C, C.htile, gcol, C.fo, T)
        P.dma(oview[:, :, tsl], C.fo.t[:, :, :], reads=[C.fo], writes=[(outT, tt)])


AW = 512
HYB_IN = 2560
CONVW = 31
PADL = 32
DIL = ((128, 1), (512, 4), (2048, 16))


def cast_perm_qk(P, C, src_ap, dst):
    for r0 in range(0, D, 128):
        a = C.cv32[(r0 // 128) % 2]
        b = C.cv16[(r0 // 128) % 2]
        P.dma(a.t[:, 0:1024], src_ap[r0:r0 + 128, 0:1024], writes=[a])
        av = a.t[:, 0:1024].rearrange("p (h e) -> p h e", e=64)
        bv = b.t[:, 0:1024].rearrange("p (h e) -> p h e", e=64)
        P.memset("pool", b.t[:, 0:1024], 0.0, writes=[b])
        P.ts("pool", bv[:, :, 0:8], av[:, :, 8:16], -1.0, None, ALU.mult, reads=[a], writes=[b])
        P.copy("pool", bv[:, :, 8:16], av[:, :, 0:8], reads=[a], writes=[b])
        P.dma(dst.t[r0:r0 + 128, :], b.t[:, 0:1024], reads=[b], writes=[(dst, ("r", r0, 0))])


def rope_tables(P, C, S, pos_ap):
    m = P.sb_mark()
    posi = P.sb("posi", [128, S], I32)
    ang = P.sb("ang", [128, S], F32)
    tmp = P.sb("tmp", [128, S], F32)
    res = P.sb("res", [128, S], F32)
    P.dma(posi.t[:, :], pos_ap[0:1, :].to_broadcast([128, S]), writes=[posi])
    P.copy("dve", ang.t[:, :], posi.t[:, :], reads=[posi], writes=[ang])
    P.ts("dve", ang.t[:, :], ang.t[:, :], C.consts.t[:, 640:641], None, ALU.mult, reads=[ang, C.consts], writes=[ang])
    PI = float(np.pi)
    ki = posi
    for shift, dst in ((0.5 * PI, C.cosT), (0.0, C.sinT)):
        P.ts("dve", tmp.t[:, :], ang.t[:, :], shift, None, ALU.add, reads=[ang], writes=[tmp])
        P.ts("dve", ki.t[:, :], tmp.t[:, :], 1.0 / (2 * PI), 0.5, ALU.mult, ALU.add, reads=[tmp], writes=[ki])
        P.copy("dve", res.t[:, :], ki.t[:, :], reads=[ki], writes=[res])
        P.stt("dve", tmp.t[:, :], res.t[:, :], -2 * PI, tmp.t[:, :], ALU.mult, ALU.add, reads=[res, tmp], writes=[tmp])
        P.ts("dve", res.t[:, :], tmp.t[:, :], -PI, 2 * PI, ALU.is_lt, ALU.mult, reads=[tmp], writes=[res])
        P.tt("dve", tmp.t[:, :], tmp.t[:, :], res.t[:, :], ALU.add, reads=[tmp, res], writes=[tmp])
        P.ts("dve", res.t[:, :], tmp.t[:, :], PI, -2 * PI, ALU.is_gt, ALU.mult, reads=[tmp], writes=[res])
        P.tt("dve", tmp.t[:, :], tmp.t[:, :], res.t[:, :], ALU.add, reads=[tmp, res], writes=[tmp])
        P.ts("dve", tmp.t[:, :], tmp.t[:, :], PI, -PI, ALU.min, ALU.max, reads=[tmp], writes=[tmp])
        P.act(res.t[:, :], tmp.t[:, :], AF.Sin, reads=[tmp], writes=[res])
        P.dma(dst.t[:, :], res.t[:, :], reads=[res], writes=[dst])
    P.barrier()
    P.sb_release(m)


def hyb_proj_phase(P, C, S, T, gcol, Wh, Whp, hv, l):
    m = P.sb_mark()
    C.htile = P.sb("htile", [128, 8, T], F32)
    C.xn = P.sb("xn", [128, 8, T], BF16)
    C.sq = P.sb("sq", [128, 8, 512], BF16)
    C.rstd = P.sb("rstd", [128, 512], F32)
    cos = P.sb("cos", [128, T], F32)
    sin = P.sb("sin", [128, T], F32)
    wa = [P.sb(f"wa{i}", [128, 8, 128], BF16) for i in range(2)]
    wb = [P.sb(f"wb{i}", [128, 8, 128], BF16) for i in range(2)]
    wv = P.sb("wv", [128, 8, 512], BF16)
    t1 = [P.sb(f"t1{i}", [128, 512], F32) for i in range(2)]
    t2 = [P.sb(f"t2{i}", [128, 512], F32) for i in range(2)]
    ob = [P.sb(f"ob{i}", [128, T], BF16) for i in range(2)]
    vt = P.sb("vt", [128, T // 128, 512], BF16)
    hview = C.hT.t.rearrange("(c p) t -> p c t", p=128)
    n = 0
    oi = 0
    for tt in range(S // T):
        tsl = slice(tt * T, (tt + 1) * T)
        P.dma(C.htile.t[:, :, :], hview[:, :, tsl], reads=[(C.hT, tt)], writes=[C.htile])
        rmsnorm_tile(P, C, C.htile, gcol, C.xn, T)
        P.dma(cos.t[:, :], C.cosT.t[:, tsl], reads=[C.cosT], writes=[cos])
        P.dma(sin.t[:, :], C.sinT.t[:, tsl], reads=[C.sinT], writes=[sin])
        for qk in range(2):
            for ch in range(4):
                col0 = qk * 512 + ch * 128
                w1 = wa[n % 2]
                w2 = wb[n % 2]
                P.dma(w1.t[:, :, :], Wh.t[:, col0:col0 + 128].rearrange("(c p) f -> p c f", p=128), reads=[Wh], writes=[w1])
                P.dma(w2.t[:, :, :], Whp.t[:, col0:col0 + 128].rearrange("(c p) f -> p c f", p=128), reads=[Whp], writes=[w2])
                o = ob[oi % 2]
                oi += 1
                for s0 in range(0, T, 512):
                    sl = slice(s0, s0 + 512)
                    k = n % 2
                    n += 1
                    pa, pb = C.PS[k], C.PS[2 + k]
                    for c in range(8):
                        P.mm(pa.t[:, :], w1.t[:, c, :], C.xn.t[:, c, sl], start=(c == 0), stop=(c == 7), reads=[w1, C.xn], writes=[pa])
                    for c in range(8):
                        P.mm(pb.t[:, :], w2.t[:, c, :], C.xn.t[:, c, sl], start=(c == 0), stop=(c == 7), reads=[w2, C.xn], writes=[pb])
                    P.tt("dve", t1[k].t[:, :], pa.t[:, :], cos.t[:, sl], ALU.mult, reads=[pa, cos], writes=[t1[k]])
                    P.tt("dve", t2[k].t[:, :], pb.t[:, :], sin.t[:, sl], ALU.mult, reads=[pb, sin], writes=[t2[k]])
                    P.tt("pool", o.t[:, sl], t1[k].t[:, :], t2[k].t[:, :], ALU.add, reads=[t1[k], t2[k]], writes=[(o, s0)])
                dst = C.qT_d if qk == 0 else C.kT_d
                P.dma(dst.t[ch * 128:(ch + 1) * 128, tsl], o.t[:, :], reads=[o], writes=[(dst, (ch, tt))])
        P.dma(wv.t[:, :, :], Wh.t[:, 1024:1536].rearrange("(c p) f -> p c f", p=128), reads=[Wh], writes=[wv])
        for blk in range(T // 128):
            pv = C.PS[4 + blk % 2]
            bs = slice(blk * 128, (blk + 1) * 128)
            for c in range(8):
                P.mm(pv.t[:, :], C.xn.t[:, c, bs], wv.t[:, c, :], start=(c == 0), stop=(c == 7), reads=[wv, C.xn], writes=[pv])
            P.copy("act", vt.t[:, blk, :], pv.t[:, :], reads=[pv], writes=[(vt, blk)])
        P.dma(C.v_d.t[tsl, :].rearrange("(b p) f -> p b f", p=128), vt.t[:, :, :], reads=[vt], writes=[(C.v_d, tt)])
        for ch in range(4):
            w1 = wa[n % 2]
            w2 = wb[n % 2]
            P.dma(w1.t[:, :, :], Wh.t[:, 1536 + ch * 128:1536 + (ch + 1) * 128].rearrange("(c p) f -> p c f", p=128), reads=[Wh], writes=[w1])
            P.dma(w2.t[:, :, :], Wh.t[:, 2048 + ch * 128:2048 + (ch + 1) * 128].rearrange("(c p) f -> p c f", p=128), reads=[Wh], writes=[w2])
            o = ob[oi % 2]
            oi += 1
            for s0 in range(0, T, 512):
                sl = slice(s0, s0 + 512)
                k = n % 2
                n += 1
                pa, pb = C.PS[k], C.PS[2 + k]
                for c in range(8):
                    P.mm(pa.t[:, :], w1.t[:, c, :], C.xn.t[:, c, sl], start=(c == 0), stop=(c == 7), reads=[w1, C.xn], writes=[pa])
                for c in range(8):
                    P.mm(pb.t[:, :], w2.t[:, c, :], C.xn.t[:, c, sl], start=(c == 0), stop=(c == 7), reads=[w2, C.xn], writes=[pb])
                P.act(t1[k].t[:, :], pb.t[:, :], AF.Sigmoid, reads=[pb], writes=[t1[k]])
                P.tt("dve", o.t[:, sl], pa.t[:, :], t1[k].t[:, :], ALU.mult, reads=[pa, t1[k]], writes=[(o, s0)])
            P.dma(C.gluT_d.t[ch * 128:(ch + 1) * 128, PADL + tt * T:PADL + (tt + 1) * T], o.t[:, :], reads=[o], writes=[(C.gluT_d, (ch, tt))])
    P.barrier()
    P.sb_release(m)


def hyb_conv_phase(P, C, S, hvcol, dww):
    m = P.sb_mark()
    diag = P.sb("diag", [128, 4, CONVW, 128], BF16)
    glu = [P.sb(f"glu{i}", [128, PADL + 512], BF16) for i in range(2)]
    csb = P.sb("csb", [128, 4, 512], F32)
    cbf = P.sb("cbf", [128, 4, 512], BF16)
    csq = P.sb("csq", [128, 4, 512], BF16)
    mean = P.sb("mean", [128, 512], F32)
    msq = P.sb("msq", [128, 512], F32)
    rstd = P.sb("rstdc", [128, 512], F32)
    tmp = [P.sb(f"ctmp{i}", [128, 512], F32) for i in range(2)]
    co = [P.sb(f"co{i}", [128, 512], BF16) for i in range(2)]
    for cc in range(4):
        for k in range(CONVW):
            P.ts("pool", diag.t[:, cc, k, :], C.ident_bf.t[:, :], dww[:, cc * CONVW + k:cc * CONVW + k + 1], None, ALU.mult,
                 reads=[C.ident_bf, C.hyb_dww], writes=[(diag, (cc, k))])
    gi = 0
    oi = 0
    off = PADL - (CONVW - 1)
    for t0 in range(0, S, 512):
        for cc in range(4):
            g = glu[gi % 2]
            gi += 1
            P.dma(g.t[:, :], C.gluT_d.t[cc * 128:(cc + 1) * 128, t0:t0 + PADL + 512], reads=[C.gluT_d], writes=[g])
            pc = C.PS[cc % 2]
            for k in range(CONVW):
                P.mm(pc.t[:, :], diag.t[:, cc, k, :], g.t[:, off + k:off + k + 512], start=(k == 0), stop=(k == CONVW - 1),
                     reads=[diag, g], writes=[pc])
            P.act(csb.t[:, cc, :], pc.t[:, :], AF.Identity, reads=[pc, C.hvec], writes=[(csb, cc)], bias=hvcol[:, cc:cc + 1])
            P.act(csq.t[:, cc, :], pc.t[:, :], AF.Square, reads=[pc, C.hvec], writes=[(csq, cc)], bias=hvcol[:, cc:cc + 1])
            P.copy("pool", cbf.t[:, cc, :], csb.t[:, cc, :], reads=[(csb, cc)], writes=[(cbf, cc)])
        p1, p2 = C.PS[4], C.PS[5]
        for cc in range(4):
            P.mm(p1.t[:, :], C.ones_bf.t[:, :], cbf.t[:, cc, :], start=(cc == 0), stop=(cc == 3), reads=[(cbf, cc), C.ones_bf], writes=[p1])
        for cc in range(4):
            P.mm(p2.t[:, :], C.ones_bf.t[:, :], csq.t[:, cc, :], start=(cc == 0), stop=(cc == 3), reads=[(csq, cc), C.ones_bf], writes=[p2])
        P.ts("dve", mean.t[:, :], p1.t[:, :], 1.0 / 512, None, ALU.mult, reads=[p1], writes=[mean])
        P.tt("dve", msq.t[:, :], mean.t[:, :], mean.t[:, :], ALU.mult, reads=[mean], writes=[msq])
        P.stt("dve", msq.t[:, :], p2.t[:, :], 1.0 / 512, msq.t[:, :], ALU.mult, ALU.subtract, reads=[p2, msq], writes=[msq])
        P.act(rstd.t[:, :], msq.t[:, :], AF.Ln, reads=[msq, C.cst], writes=[rstd], bias=C.cst.t[:, 0:1])
        P.act(rstd.t[:, :], rstd.t[:, :], AF.Exp, reads=[rstd], writes=[rstd], scale=-0.5)
        for cc in range(4):
            tm = tmp[cc % 2]
            o = co[oi % 2]
            oi += 1
            P.tt("dve", tm.t[:, :], csb.t[:, cc, :], mean.t[:, :], ALU.subtract, reads=[(csb, cc), mean], writes=[tm])
            P.tt("dve", tm.t[:, :], tm.t[:, :], rstd.t[:, :], ALU.mult, reads=[tm, rstd], writes=[tm])
            P.act(o.t[:, :], tm.t[:, :], AF.Silu, reads=[tm, C.hvec], writes=[o], scale=hvcol[:, 4 + cc:5 + cc], bias=hvcol[:, 8 + cc:9 + cc])
            P.dma(C.catT_d.t[512 + cc * 128:512 + (cc + 1) * 128, t0:t0 + 512], o.t[:, :], reads=[o], writes=[(C.catT_d, (4 + cc, t0))])
    P.barrier()
    P.sb_release(m)


def hyb_attn_phase(P, C, S):
    m = P.sb_mark()
    qT = P.sb("qTa", [128, S], BF16)
    kT = P.sb("kTa", [128, S], BF16)
    acc = [P.sb(f"acc{i}", [65, S], F32) for i in range(2)]
    bc = P.sb("bc", [64, S], F32)
    rden = P.sb("rden", [65, S], F32)
    ao = P.sb("ao", [64, S], BF16)
    vts = [P.sb(f"vts{i}", [128, 32, 2, 65], BF16) for i in range(2)]
    pts = [P.sb(f"pt{i}", [128, 512], BF16) for i in range(4)]
    for v_ in vts:
        P.memset("pool", v_.t[:, :, :, 64:65], 1.0, writes=[(v_, "ones")])
    vi = 0
    pi_ = 0
    sti = 0
    for hp in range(4):
        P.dma(qT.t[:, :], C.qT_d.t[hp * 128:(hp + 1) * 128, :], reads=[C.qT_d], writes=[qT])
        P.dma(kT.t[:, :], C.kT_d.t[hp * 128:(hp + 1) * 128, :], reads=[C.kT_d], writes=[kT])
        first = [True, True]
        for (_, r) in DIL:
            L = S // r
            nb = L // 128
            for c in range(r):
                vt = vts[vi % 2]
                vi += 1
                for hh in range(2):
                    src = C.v_d.t[:, hp * 128 + hh * 64:hp * 128 + (hh + 1) * 64]
                    src = src[c::r, :].rearrange("(j p) e -> p j e", p=128)
                    for jb in range(0, nb, 8):
                        je = min(nb, jb + 8)
                        P.dma(vt.t[:, jb:je, hh, 0:64], src[:, jb:je, :], reads=[C.v_d], writes=[(vt, (hh, jb))])
                for hh in range(2):
                    hb = hh * 64
                    A = acc[hh]

                    def tok(j0, nblk):
                        return slice(c + r * 128 * j0, c + r * (128 * (j0 + nblk) - 1) + 1, r)

                    ptl = {}
                    for j0 in range(0, nb, 2):
                        nkb = min(2, nb - j0)
                        pt = pts[pi_ % 4]
                        pi_ += 1
                        ps = C.PS[pi_ % 2]
                        widths = []
                        for jj in range(nkb):
                            j = j0 + jj
                            nq = 2 if j + 1 < nb else 1
                            P.mm(ps.t[:, jj * 256:jj * 256 + nq * 128], kT.t[hb:hb + 64, tok(j, 1)], qT.t[hb:hb + 64, tok(j, nq)],
                                 reads=[kT, qT], writes=[ps])
                            widths.append(nq)
                        wtot = 256 * (nkb - 1) + 128 * widths[-1]
                        P.act(pt.t[:, 0:wtot], ps.t[:, 0:wtot], AF.Exp, reads=[ps], writes=[pt], scale=0.125)
                        P.tt("pool", pt.t[:, 0:wtot], pt.t[:, 0:wtot], C.mask4.t[:, 0:wtot], ALU.mult, reads=[pt, C.mask4], writes=[pt])
                        for jj in range(nkb):
                            ptl[j0 + jj] = (pt, jj * 256)
                        for jj in range(nkb):
                            n_ = j0 + jj
                            q4 = n_ % 4
                            po = C.PS[4 + (n_ // 4) % 2]
                            ptc, oc = ptl[n_]
                            has_prev = n_ > 0
                            P.mm(po.t[0:65, q4 * 128:(q4 + 1) * 128], vt.t[:, n_, hh, :], ptc.t[:, oc:oc + 128], start=True, stop=not has_prev,
                                 reads=[vt, ptc], writes=[(po, q4)])
                            if has_prev:
                                ptp, op_ = ptl[n_ - 1]
                                P.mm(po.t[0:65, q4 * 128:(q4 + 1) * 128], vt.t[:, n_ - 1, hh, :], ptp.t[:, op_ + 128:op_ + 256], start=False, stop=True,
                                     reads=[vt, ptp], writes=[(po, q4)])
                            if q4 == 3 or n_ == nb - 1:
                                nq4 = q4 + 1
                                b0 = n_ - q4
                                dsts = A.t[:, tok(b0, nq4)]
                                if first[hh]:
                                    P.copy("act", dsts, po.t[0:65, 0:nq4 * 128], reads=[po], writes=[A])
                                else:
                                    P.tt("dve", dsts, dsts, po.t[0:65, 0:nq4 * 128], ALU.add, reads=[po, A], writes=[A])
                    first[hh] = False
        for hh in range(2):
            A = acc[hh]
            h = hp * 2 + hh
            P.add("dve", lambda e, A=A: e.reciprocal(out=rden.t[64:65, :], in_=A.t[64:65, :]), reads=[A], writes=[rden])
            P.dma(C.den_d.t[0:1, :], rden.t[64:65, :], reads=[rden], writes=[C.den_d])
            P.dma(bc.t[:, :], C.den_d.t[0:1, :].to_broadcast([64, S]), reads=[C.den_d], writes=[bc])
            P.tt("dve", ao.t[:, :], A.t[0:64, :], bc.t[:, :], ALU.mult, reads=[A, bc], writes=[ao])
            P.dma(C.catT_d.t[h * 64:(h + 1) * 64, :], ao.t[:, :], reads=[ao], writes=[(C.catT_d, ("a", h))])
    P.barrier()
    P.sb_release(m)


def mix_out_phase(P, C, S, T, Wo):
    m = P.sb_mark()
    htile = P.sb("htile", [128, 8, T], F32)
    cat = P.sb("cat", [128, 8, T], BF16)
    wo = P.sb("wo_m", [128, 8, D], BF16)
    hview = C.hT.t.rearrange("(c p) t -> p c t", p=128)
    cview = C.catT_d.t.rearrange("(c p) t -> p c t", p=128)
    P.dma(wo.t[:, :, :], Wo.t.rearrange("(c p) d -> p c d", p=128), reads=[Wo], writes=[wo])
    n = 0
    for tt in range(S // T):
        tsl = slice(tt * T, (tt + 1) * T)
        P.dma(htile.t[:, :, :], hview[:, :, tsl], reads=[(C.hT, tt)], writes=[htile])
        P.dma(cat.t[:, :, :], cview[:, :, tsl], reads=[C.catT_d], writes=[cat])
        for dc in range(8):
            for s0 in range(0, T, 512):
                sl = slice(s0, s0 + 512)
                py = C.PS[n % 2]
                n += 1
                for f in range(8):
                    P.mm(py.t[:, :], wo.t[:, f, dc * 128:(dc + 1) * 128], cat.t[:, f, sl], start=(f == 0), stop=(f == 7), reads=[wo, cat], writes=[py])
                P.tt("dve", htile.t[:, dc, sl], py.t[:, :], htile.t[:, dc, sl], ALU.add, reads=[py, (htile, (dc, s0))], writes=[(htile, (dc, s0))])
        P.dma(hview[:, :, tsl], htile.t[:, :, :], reads=[htile], writes=[(C.hT, tt)])
    P.barrier()
    P.sb_release(m)


GDN_IN = 4112
import os as _os
GDN_STOP = int(_os.environ.get('GDN_STOP', '0'))
QSCALE = 128.0 ** -0.5


def gdn_alloc_scalars(P, C, S):
    NB = S // 128
    sc = Ctx()
    sc.beta = P.sb("g_beta", [128, NB, 8], F32)
    sc.gc = P.sb("g_gc", [128, NB, 8], F32)
    sc.kd = P.sb("g_kd", [128, NB, 8], F32)
    sc.kbg = P.sb("g_kbg", [128, NB, 8], F32)
    sc.egl = P.sb("g_egl", [128, 2, NB, 8], F32)
    return sc


def gdn_proj_phase(P, C, S, gcol, i, sc, Wg):
    m = P.sb_mark()
    NB = S // 128
    xn = P.sb("xnf", [128, 8, S], BF16)
    mA = P.sb_mark()
    ht = [P.sb(f"ht{j}", [128, 8, 512], F32) for j in range(1)]
    C.sq = P.sb("sq", [128, 8, 512], BF16)
    C.rstd = P.sb("rstd", [128, 512], F32)
    hview = C.hT.t.rearrange("(c p) t -> p c t", p=128)
    for t0 in range(0, S, 512):
        h = ht[0]
        P.dma(h.t[:, :, :], hview[:, :, t0:t0 + 512], reads=[C.hT], writes=[h])
        rmsnorm_tile(P, C, h, gcol, xn, 512, doff=t0)
    dg = P.sb("dg", [128, 24, 4, 128], BF16)
    cw = C.gcw.t[:, i * 96:(i + 1) * 96]
    for ch in range(24):
        for k in range(4):
            P.ts("pool", dg.t[:, ch, k, :], C.ident_bf.t[:, :], cw[:, ch * 4 + k:ch * 4 + k + 1], None, ALU.mult,
                 reads=[C.ident_bf, C.gcw], writes=[(dg, (ch, k))])
    PADG = 4
    pre = [P.sb(f"pre{j}", [128, PADG + S], BF16) for j in range(2)]
    for pr in pre:
        P.memset("pool", pr.t[:, 0:PADG], 0.0, writes=[(pr, "pad")])
    wch = [P.sb(f"wch{j}", [128, 8, 128], BF16) for j in range(2)]
    yb = [P.sb(f"yb{j}", [128, 512], F32) for j in range(2)]
    sqb = [P.sb(f"sqb{j}", [128, 512], BF16) for j in range(2)]
    rn = [P.sb(f"rn{j}", [128, 512], F32) for j in range(2)]
    ob = [P.sb(f"gob{j}", [128, S], BF16) for j in range(2)]
    n = 0
    for ch in range(32):
        w = wch[ch % 2]
        P.dma(w.t[:, :, :], Wg.t[:, ch * 128:(ch + 1) * 128].rearrange("(c p) f -> p c f", p=128), reads=[Wg], writes=[w])
        o = ob[ch % 2]
        if ch < 24:
            pr = pre[ch % 2]
            for s0 in range(0, S, 512):
                k = n % 2
                n += 1
                ps = C.PS[k]
                for c in range(8):
                    P.mm(ps.t[:, :], w.t[:, c, :], xn.t[:, c, s0:s0 + 512], start=(c == 0), stop=(c == 7), reads=[w, xn], writes=[ps])
                P.copy("act", pr.t[:, PADG + s0:PADG + s0 + 512], ps.t[:, :], reads=[ps], writes=[(pr, s0)])
            for s0 in range(0, S, 512):
                k = n % 2
                n += 1
                pc = C.PS[2 + k]
                for kk in range(4):
                    P.mm(pc.t[:, :], dg.t[:, ch, kk, :], pr.t[:, s0 + 1 + kk:s0 + 1 + kk + 512], start=(kk == 0), stop=(kk == 3),
                         reads=[dg, pr], writes=[pc])
                if ch >= 16:
                    P.act(o.t[:, s0:s0 + 512], pc.t[:, :], AF.Silu, reads=[pc], writes=[(o, s0)])
                else:
                    y, sq_, r_ = yb[k], sqb[k], rn[k]
                    pn = C.PS[4 + k]
                    P.act(y.t[:, :], pc.t[:, :], AF.Silu, reads=[pc], writes=[y])
                    P.tt("dve", sq_.t[:, :], y.t[:, :], y.t[:, :], ALU.mult, reads=[y], writes=[sq_])
                    P.mm(pn.t[:, :], C.ones_bf.t[:, :], sq_.t[:, :], reads=[sq_, C.ones_bf], writes=[pn])
                    P.act(r_.t[:, :], pn.t[:, :], AF.Ln, reads=[pn, C.cst], writes=[r_], bias=C.cst.t[:, 0:1])
                    P.act(r_.t[:, :], r_.t[:, :], AF.Exp, reads=[r_], writes=[r_], scale=-0.5)
                    if ch < 8:
                        P.stt("dve", o.t[:, s0:s0 + 512], y.t[:, :], QSCALE, r_.t[:, :], ALU.mult, ALU.mult, reads=[y, r_], writes=[(o, s0)])
                    else:
                        P.tt("dve", o.t[:, s0:s0 + 512], y.t[:, :], r_.t[:, :], ALU.mult, reads=[y, r_], writes=[(o, s0)])
            dst = (C.qT_g, C.kT_g, C.vT_g)[ch // 8]
        else:
            for s0 in range(0, S, 512):
                k = n % 2
                n += 1
                ps = C.PS[k]
                for c in range(8):
                    P.mm(ps.t[:, :], w.t[:, c, :], xn.t[:, c, s0:s0 + 512], start=(c == 0), stop=(c == 7), reads=[w, xn], writes=[ps])
                P.act(o.t[:, s0:s0 + 512], ps.t[:, :], AF.Silu, reads=[ps], writes=[(o, s0)])
            dst = C.szT_g
        r0 = (ch % 8) * 128
        P.dma(dst.t[r0:r0 + 128, :], o.t[:, :], reads=[o], writes=[(dst, ch % 8)])

    P.barrier()
    P.sb_release(mA)
    if GDN_STOP == 1:
        P.sb_release(m)
        return
    wba = P.sb("wba", [128, 8, 16], BF16)
    P.dma(wba.t[:, :, :], Wg.t[:, 4096:4112].rearrange("(c p) f -> p c f", p=128), reads=[Wg], writes=[wba])
    ba = P.sb("ba", [128, NB, 16], F32)
    x_ = P.sb("gx", [128, NB, 8], F32)
    e_ = P.sb("ge", [128, NB, 8], F32)
    g_ = P.sb("gg", [128, NB, 8], F32)
    gls = P.sb("gls", [128, NB, 8], F32)
    gsm = P.sb("gsm", [128, 16], F32)
    nA = P.sb("nA", [128, 8], F32)
    P.dma(gsm.t[:, :], C.gsm_row_in[0:1, i * 16:(i + 1) * 16].to_broadcast([128, 16]), writes=[gsm])
    P.act(nA.t[:, :], gsm.t[:, 0:8], AF.Exp, reads=[gsm], writes=[nA])
    P.ts("dve", nA.t[:, :], nA.t[:, :], -1.0, None, ALU.mult, reads=[nA], writes=[nA])
    if GDN_STOP == 21:
        P.barrier()
        P.sb_release(m)
        return
    pba = C.PS[6]
    for blk in range(NB):
        for c in range(8):
            P.mm(pba.t[:, blk * 16:(blk + 1) * 16], xn.t[:, c, blk * 128:(blk + 1) * 128], wba.t[:, c, :], start=(c == 0), stop=(c == 7),
                 reads=[wba, xn], writes=[(pba, blk)])
    P.copy("dve", ba.t[:, :, :], pba.t[:, 0:NB * 16].rearrange("p (b f) -> p b f", f=16), reads=[pba], writes=[ba])

    if GDN_STOP == 22:
        P.barrier()
        P.sb_release(m)
        return
    def bc8(ap):
        return ap.unsqueeze(1).to_broadcast([128, NB, 8])

    P.act(sc.beta.t[:, :, :], ba.t[:, :, 0:8], AF.Sigmoid, reads=[ba], writes=[sc.beta])
    P.tt("dve", x_.t[:, :, :], ba.t[:, :, 8:16], bc8(gsm.t[:, 8:16]), ALU.add, reads=[ba, gsm], writes=[x_])
    P.ts("dve", e_.t[:, :, :], x_.t[:, :, :], 30.0, None, ALU.min, reads=[x_], writes=[e_])
    P.act(e_.t[:, :, :], e_.t[:, :, :], AF.Exp, reads=[e_], writes=[e_])
    P.act(e_.t[:, :, :], e_.t[:, :, :], AF.Ln, reads=[e_, C.cst], writes=[e_], bias=C.cst.t[:, 2:3])
    P.tt("dve", e_.t[:, :, :], e_.t[:, :, :], x_.t[:, :, :], ALU.max, reads=[e_, x_], writes=[e_])
    P.tt("dve", g_.t[:, :, :], e_.t[:, :, :], bc8(nA.t[:, :]), ALU.mult, reads=[e_, nA], writes=[g_])
    if GDN_STOP == 23:
        P.barrier()
        P.sb_release(m)
        return
    gflat = g_.t[:, :, :].rearrange("p b h -> p (b h)")
    W8 = NB * 8
    pgc, pl0, pl1 = C.PS[5], C.PS[4], C.PS[3]
    P.mm(pgc.t[:, 0:W8], C.consts.t[:, 1024:1152], gflat, reads=[g_, C.consts], writes=[pgc])
    P.mm(pl0.t[:, 0:W8], C.consts.t[:, 1152:1280], gflat, reads=[g_, C.consts], writes=[pl0])
    P.mm(pl1.t[:, 0:W8], C.consts.t[:, 1280:1408], gflat, reads=[g_, C.consts], writes=[pl1])

    def v3(ap):
        return ap.rearrange("p (b h) -> p b h", h=8)
    if GDN_STOP == 24:
        P.barrier()
        P.sb_release(m)
        return

    P.copy("dve", sc.gc.t[:, :, :], v3(pgc.t[:, 0:W8]), reads=[pgc], writes=[sc.gc])
    P.copy("dve", gls.t[0:64, :, :], v3(pl0.t[0:64, 0:W8]), reads=[pl0], writes=[(gls, 0)])
    P.copy("dve", gls.t[64:128, :, :], v3(pl1.t[64:128, 0:W8]), reads=[pl1], writes=[(gls, 1)])
    if GDN_STOP == 25:
        P.barrier()
        P.sb_release(m)
        return
    P.copy("dve", sc.egl.t[:, 0, :, :], v3(pl0.t[:, 0:W8]), reads=[pl0], writes=[(sc.egl, 0)])
    P.copy("dve", sc.egl.t[:, 1, :, :], v3(pl1.t[:, 0:W8]), reads=[pl1], writes=[(sc.egl, 1)])
    P.act(sc.egl.t[:, :, :, :], sc.egl.t[:, :, :, :], AF.Exp, reads=[sc.egl], writes=[sc.egl])
    if GDN_STOP == 26:
        P.barrier()
        P.sb_release(m)
        return
    P.tt("dve", gls.t[:, :, :], gls.t[:, :, :], sc.gc.t[:, :, :], ALU.subtract, reads=[gls, sc.gc], writes=[gls])
    P.act(sc.kd.t[:, :, :], gls.t[:, :, :], AF.Exp, reads=[gls], writes=[sc.kd])
    P.act(sc.kbg.t[:, :, :], sc.gc.t[:, :, :], AF.Exp, reads=[sc.gc], writes=[sc.kbg])
    P.tt("dve", sc.kbg.t[:, :, :], sc.kbg.t[:, :, :], sc.beta.t[:, :, :], ALU.mult, reads=[sc.kbg, sc.beta], writes=[sc.kbg])

    if GDN_STOP == 2:
        P.barrier()
        P.sb_release(m)
        return
    aT = P.sb("aT", [8, S], F32)
    eT = P.sb("eT", [8, S], F32)
    mk = P.sb("mk", [8, S], F32)
    gcol8 = P.sb("gcol8", [8, 4], F32)
    P.dma(gcol8.t[:, 0:2], C.gsm_col_in[:, i * 2:(i + 1) * 2], writes=[gcol8])
    P.act(gcol8.t[:, 2:3], gcol8.t[:, 0:1], AF.Exp, reads=[gcol8], writes=[gcol8])
    P.ts("dve", gcol8.t[:, 2:3], gcol8.t[:, 2:3], -1.0, None, ALU.mult, reads=[gcol8], writes=[gcol8])
    for s0 in range(0, S, 512):
        k = n % 2
        n += 1
        ps = C.PS[k]
        for c in range(8):
            P.mm(ps.t[0:8, :], wba.t[:, c, 8:16], xn.t[:, c, s0:s0 + 512], start=(c == 0), stop=(c == 7), reads=[wba, xn], writes=[ps])
        P.ts("dve", aT.t[:, s0:s0 + 512], ps.t[0:8, :], gcol8.t[:, 1:2], None, ALU.add, reads=[ps, gcol8], writes=[(aT, s0)])
    P.ts("dve", eT.t[:, :], aT.t[:, :], 30.0, None, ALU.min, reads=[aT], writes=[eT])
    P.act(eT.t[:, :], eT.t[:, :], AF.Exp, reads=[eT], writes=[eT])
    P.act(eT.t[:, :], eT.t[:, :], AF.Ln, reads=[eT, C.cst], writes=[eT], bias=C.cst.t[0:8, 2:3])
    P.tt("dve", eT.t[:, :], eT.t[:, :], aT.t[:, :], ALU.max, reads=[eT, aT], writes=[eT])
    P.ts("dve", eT.t[:, :], eT.t[:, :], gcol8.t[:, 2:3], None, ALU.mult, reads=[eT, gcol8], writes=[eT])
    P.memset("dve", mk.t[:, :], 1.0, writes=[mk])
    P.memset("dve", mk.t[:, 0:S:64], 0.0, writes=[mk])
    P.add("dve", lambda e: e.tensor_tensor_scan(out=aT.t[:, :], data0=mk.t[:, :], data1=eT.t[:, :], initial=0.0, op0=ALU.mult, op1=ALU.add),
          reads=[mk, eT], writes=[aT])
    P.dma(C.gc_d.t[:, :], aT.t[:, :], reads=[aT], writes=[C.gc_d])
    P.barrier()
    P.sb_release(m)


def gdn_core_phase(P, C, S, i, sc):
    m = P.sb_mark()
    NB = S // 128
    LT = 256
    BPL = LT // 128

    def ld(name, dt):
        return [P.sb(f"{name}{j}", [128, 8, LT], dt) for j in range(2)]

    qT, kT, vT, szT, gcB = ld("gq", BF16), ld("gk", BF16), ld("gv", BF16), ld("gsz", BF16), ld("gcB", F32)
    Sst = P.sb("Sst", [128, 8, 128], F32)
    Sbf = P.sb("Sbf", [128, 8, 128], BF16)
    P.memset("pool", Sst.t[:, :, :], 0.0, writes=[Sst])
    P.memset("pool", Sbf.t[:, :, :], 0.0, writes=[Sbf])

    class Wset:
        pass

    W = Wset()
    for nm, dt in (("egc", F32), ("qd", BF16), ("kbg", BF16), ("kd", BF16), ("vb", BF16), ("E", F32), ("En", F32),
                   ("attnT", BF16), ("X", BF16), ("XT", BF16), ("Pa", BF16), ("Pb", BF16), ("PTa", BF16), ("PTb", BF16),
                   ("TTa", BF16), ("TTb", BF16), ("u", F32), ("wT", BF16), ("vn", BF16), ("oT", F32), ("osq", BF16),
                   ("rn", F32), ("ob", BF16)):
        setattr(W, nm, P.sb("w_" + nm, [128, 8, 128], dt))

    maskU = C.consts.t[:, 1024:1152].unsqueeze(1).to_broadcast([128, 8, 128])
    maskLn = C.consts.t[:, 1408:1536].unsqueeze(1).to_broadcast([128, 8, 128])
    identb = C.ident_bf.t[:, :].unsqueeze(1).to_broadcast([128, 8, 128])
    ngcol = C.gng.t[:, i:i + 1]

    def bcf(ap):
        return ap.unsqueeze(2).to_broadcast([128, 8, 128])

    def h4(ps, lo=0, hi=128):
        return ps.t[lo:hi, :].rearrange("p (h d) -> p h d", h=4)

    for b in range(NB):
        lt = b // BPL
        li = lt % 2
        off = (b % BPL) * 128
        bsl = slice(off, off + 128)
        if b % BPL == 0:
            tsl = slice(lt * LT, (lt + 1) * LT)
            for dstl, srcd in ((qT, C.qT_g), (kT, C.kT_g), (vT, C.vT_g), (szT, C.szT_g)):
                P.dma(dstl[li].t[:, :, :], srcd.t[:, tsl].rearrange("(h p) t -> p h t", p=128), reads=[srcd], writes=[dstl[li]])
            P.dma(gcB[li].t[:, :, :], C.gc_d.t[:, tsl].unsqueeze(0).to_broadcast([128, 8, LT]), reads=[C.gc_d], writes=[gcB[li]])
        q_, k_, v_, sz_, gb_ = qT[li], kT[li], vT[li], szT[li], gcB[li]
        P.act(W.egc.t[:, :, :], gb_.t[:, :, bsl], AF.Exp, reads=[gb_], writes=[W.egc])
        P.tt("dve", W.qd.t[:, :, :], q_.t[:, :, bsl], W.egc.t[:, :, :], ALU.mult, reads=[q_, W.egc], writes=[W.qd])
        pk = C.PS[0].t[:, :].bitcast(BF16)
        for h in range(8):
            P.tr(pk[:, h * 128:(h + 1) * 128], k_.t[:, h, bsl], C.ident_bf.t[:, :], reads=[k_, C.ident_bf], writes=[(C.PS[0], h)])
        pk3 = pk.rearrange("p (h d) -> p h d", h=8)
        P.tt("dve", W.kbg.t[:, :, :], pk3, bcf(sc.kbg.t[:, b, :]), ALU.mult, reads=[C.PS[0], sc.kbg], writes=[W.kbg])
        P.tt("dve", W.kd.t[:, :, :], pk3, bcf(sc.kd.t[:, b, :]), ALU.mult, reads=[C.PS[0], sc.kd], writes=[W.kd])
        pv = C.PS[1].t[:, :].bitcast(BF16)
        for h in range(8):
            P.tr(pv[:, h * 128:(h + 1) * 128], v_.t[:, h, bsl], C.ident_bf.t[:, :], reads=[v_, C.ident_bf], writes=[(C.PS[1], h)])
        P.tt("dve", W.vb.t[:, :, :], pv.rearrange("p (h d) -> p h d", h=8), bcf(sc.beta.t[:, b, :]), ALU.mult,
             reads=[C.PS[1], sc.beta], writes=[W.vb])
        for half in range(2):
            pg, pq = C.PS[2 + half], C.PS[4 + half]
            for hh in range(4):
                h = half * 4 + hh
                P.mm(pg.t[:, hh * 128:(hh + 1) * 128], k_.t[:, h, bsl], k_.t[:, h, bsl], reads=[k_], writes=[(pg, hh)])
            for hh in range(4):
                h = half * 4 + hh
                P.mm(pq.t[:, hh * 128:(hh + 1) * 128], k_.t[:, h, bsl], q_.t[:, h, bsl], reads=[k_, q_], writes=[(pq, hh)])
        P.tt("dve", W.E.t[:, :, :], gb_.t[:, :, bsl], bcf(sc.gc.t[:, b, :]), ALU.subtract, reads=[gb_, sc.gc], writes=[W.E])
        P.ts("dve", W.En.t[:, :, :], W.E.t[:, :, :], -1.0, 0.0, ALU.mult, ALU.min, reads=[W.E], writes=[W.En])
        P.ts("dve", W.E.t[:, :, :], W.E.t[:, :, :], 0.0, None, ALU.min, reads=[W.E], writes=[W.E])
        P.act(W.E.t[:, :, :], W.E.t[:, :, :], AF.Exp, reads=[W.E], writes=[W.E])
        P.act(W.En.t[:, :, :], W.En.t[:, :, :], AF.Exp, reads=[W.En], writes=[W.En])
        P.tt("pool", W.E.t[:, :, :], W.E.t[:, :, :], maskU, ALU.mult, reads=[W.E, C.consts], writes=[W.E])
        P.tt("pool", W.En.t[:, :, :], W.En.t[:, :, :], maskLn, ALU.mult, reads=[W.En, C.consts], writes=[W.En])
        P.tt("pool", W.En.t[:, :, :], W.En.t[:, :, :], bcf(sc.beta.t[:, b, :]), ALU.mult, reads=[W.En, sc.beta], writes=[W.En])
        for half in range(2):
            hs = slice(half * 4, half * 4 + 4)
            P.tt("dve", W.attnT.t[:, hs, :], h4(C.PS[4 + half]), W.E.t[:, hs, :], ALU.mult, reads=[C.PS[4 + half], W.E], writes=[(W.attnT, half)])
            P.tt("dve", W.X.t[:, hs, :], h4(C.PS[2 + half]), W.En.t[:, hs, :], ALU.mult, reads=[C.PS[2 + half], W.En], writes=[(W.X, half)])
        px = C.PS[0].t[:, :].bitcast(BF16)
        for h in range(8):
            P.tr(px[:, h * 128:(h + 1) * 128], W.X.t[:, h, :], C.ident_bf.t[:, :], reads=[W.X, C.ident_bf], writes=[(C.PS[0], h)])
        P.copy("act", W.XT.t[:, :, :], px.rearrange("p (h d) -> p h d", h=8), reads=[C.PS[0]], writes=[W.XT])
        P.tt("pool", W.TTa.t[:, :, :], W.XT.t[:, :, :], identb, ALU.add, reads=[W.XT, C.ident_bf], writes=[W.TTa])
        Pc, PTc, TTc = W.X, W.XT, W.TTa
        pbufs = [(W.Pa, W.PTa), (W.Pb, W.PTb)]
        ttbufs = [W.TTb, W.TTa]
        for kk in range(1, 6):
            Pn, PTn = pbufs[kk % 2]
            TTn = ttbufs[(kk - 1) % 2]
            for half in range(2):
                hs = slice(half * 4, half * 4 + 4)
                pp = C.PS[2 + half]
                for hh in range(4):
                    h = half * 4 + hh
                    P.mm(pp.t[:, hh * 128:(hh + 1) * 128], PTc.t[:, h, :], Pc.t[:, h, :], reads=[PTc, Pc], writes=[(pp, hh)])
                P.copy("act", Pn.t[:, hs, :], h4(pp), reads=[pp], writes=[(Pn, half)])
                if kk < 5:
                    ppt = C.PS[4 + half]
                    for hh in range(4):
                        h = half * 4 + hh
                        P.mm(ppt.t[:, hh * 128:(hh + 1) * 128], Pc.t[:, h, :], PTc.t[:, h, :], reads=[PTc, Pc], writes=[(ppt, hh)])
                    P.copy("dve", PTn.t[:, hs, :], h4(ppt), reads=[ppt], writes=[(PTn, half)])
            for half in range(2):
                hs = slice(half * 4, half * 4 + 4)
                pt = C.PS[6 + half]
                for hh in range(4):
                    h = half * 4 + hh
                    P.mm(pt.t[:, hh * 128:(hh + 1) * 128], Pn.t[:, h, :], TTc.t[:, h, :], start=True, stop=False, reads=[Pn, TTc], writes=[(pt, hh)])
                    P.mm(pt.t[:, hh * 128:(hh + 1) * 128], C.ident_bf.t[:, :], TTc.t[:, h, :], start=False, stop=True, reads=[C.ident_bf, TTc], writes=[(pt, hh)])
                P.copy("act" if half == 0 else "dve", TTn.t[:, hs, :], h4(pt), reads=[pt], writes=[(TTn, half)])
            Pc, PTc, TTc = Pn, PTn, TTn
        for half in range(2):
            hs = slice(half * 4, half * 4 + 4)
            pu = C.PS[6 + half]
            for hh in range(4):
                h = half * 4 + hh
                P.mm(pu.t[:, hh * 128:(hh + 1) * 128], TTc.t[:, h, :], W.vb.t[:, h, :], reads=[TTc, W.vb], writes=[(pu, hh)])
            P.copy("act", W.u.t[:, hs, :], h4(pu), reads=[pu], writes=[(W.u, half)])
        for half in range(2):
            hs = slice(half * 4, half * 4 + 4)
            pw = C.PS[6 + half]
            for hh in range(4):
                h = half * 4 + hh
                P.mm(pw.t[:, hh * 128:(hh + 1) * 128], W.kbg.t[:, h, :], TTc.t[:, h, :], reads=[TTc, W.kbg], writes=[(pw, hh)])
            P.copy("dve", W.wT.t[:, hs, :], h4(pw), reads=[pw], writes=[(W.wT, half)])
        for cix in range(2):
            lo, hi = cix * 64, cix * 64 + 64
            cs = slice(lo, hi)
            for half in range(2):
                hs = slice(half * 4, half * 4 + 4)
                pW = C.PS[2 + half]
                for hh in range(4):
                    h = half * 4 + hh
                    P.mm(pW.t[cs, hh * 128:(hh + 1) * 128], W.wT.t[:, h, cs], Sbf.t[:, h, :], reads=[W.wT, Sbf], writes=[(pW, hh)])
                P.tt("dve", W.vn.t[cs, hs, :], W.u.t[cs, hs, :], h4(pW, lo, hi), ALU.subtract, reads=[pW, W.u], writes=[(W.vn, (cix, half))])
            pO = C.PS[1]
            for h in range(8):
                P.mm(pO.t[:, h * 64:(h + 1) * 64], Sbf.t[:, h, :], W.qd.t[:, h, cs], start=True, stop=False, reads=[Sbf, W.qd], writes=[(pO, h)])
                P.mm(pO.t[:, h * 64:(h + 1) * 64], W.vn.t[cs, h, :], W.attnT.t[cs, h, cs], start=False, stop=True, reads=[W.vn, W.attnT], writes=[(pO, h)])
            P.copy("act", W.oT.t[:, :, cs], pO.t[:, :].rearrange("p (h t) -> p h t", h=8), reads=[pO], writes=[(W.oT, cix)])
            for half in range(2):
                pS = C.PS[4 + half]
                for hh in range(4):
                    h = half * 4 + hh
                    P.mm(pS.t[:, hh * 128:(hh + 1) * 128], W.kd.t[cs, h, :], W.vn.t[cs, h, :], reads=[W.kd, W.vn], writes=[(pS, hh)])
            P.tt("dve", Sst.t[:, :, :], Sst.t[:, :, :], bcf(sc.egl.t[:, cix, b, :]), ALU.mult, reads=[Sst, sc.egl], writes=[Sst])
            for half in range(2):
                hs = slice(half * 4, half * 4 + 4)
                P.tt("dve", Sst.t[:, hs, :], Sst.t[:, hs, :], h4(C.PS[4 + half]), ALU.add, reads=[C.PS[4 + half], Sst], writes=[Sst])
            P.copy("act", Sbf.t[:, :, :], Sst.t[:, :, :], reads=[Sst], writes=[Sbf])
        P.tt("dve", W.osq.t[:, :, :], W.oT.t[:, :, :], W.oT.t[:, :, :], ALU.mult, reads=[W.oT], writes=[W.osq])
        for half in range(2):
            hs = slice(half * 4, half * 4 + 4)
            pn = C.PS[6 + half]
            P.mm(pn.t[:, :], C.ones_bf.t[:, :], W.osq.t[:, hs, :], reads=[W.osq, C.ones_bf], writes=[pn])
            P.act(W.rn.t[:, hs, :], h4(pn), AF.Ln, reads=[pn, C.cst], writes=[(W.rn, half)], scale=1.0 / 128, bias=C.cst.t[:, 0:1])
        P.act(W.rn.t[:, :, :], W.rn.t[:, :, :], AF.Exp, reads=[W.rn], writes=[W.rn], scale=-0.5)
        P.stt("dve", W.rn.t[:, :, :], W.oT.t[:, :, :], ngcol, W.rn.t[:, :, :], ALU.mult, ALU.mult, reads=[W.oT, W.rn, C.gng], writes=[W.rn])
        P.tt("dve", W.ob.t[:, :, :], W.rn.t[:, :, :], sz_.t[:, :, bsl], ALU.mult, reads=[W.rn, sz_], writes=[W.ob])
        P.dma(C.catT_d.t[:, b * 128:(b + 1) * 128].rearrange("(h p) t -> p h t", p=128), W.ob.t[:, :, :], reads=[W.ob], writes=[(C.catT_d, ("g", b))])
    P.barrier()
    P.sb_release(m)


def gdn_layer(P, C, S, T, gmix, i):
    m = P.sb_mark()
    sc = gdn_alloc_scalars(P, C, S)
    gdn_proj_phase(P, C, S, gmix, i, sc, C.Wg[i])
    if GDN_STOP == 0 or GDN_STOP > 3:
        gdn_core_phase(P, C, S, i, sc)
    P.sb_release(m)
    mix_out_phase(P, C, S, T, C.Wgo[i])

def build(S=4096, depth=4, T=1024, mixers=True, dbg=None):
    nc = bass.Bass("TRN2", target_bir_lowering=False)
    P = Prog(nc)
    C = Ctx()
    C.S = S
    NE = (depth + 1) // 2
    NO = depth // 2

    def din(name, shape, dt=F32):
        return nc.dram_tensor(name, list(shape), dt, kind="ExternalInput")

    xT = din("xT", [D, S])
    pos = din("pos", [1, S], I32)
    gains_in = din("gains", [128, depth * 24 + 8])
    consts_in = din("consts", [128, 2048])
    ffn_w_in = [[din(f"ffn{j + 1}_w_in_{l}", [D, 2 * FF]) for j in range(2)] for l in range(depth)]
    ffn_w_out = [[din(f"ffn{j + 1}_w_out_{l}", [FF, D]) for j in range(2)] for l in range(depth)]
    if mixers:
        hyb_w_in = [din(f"hyb_w_in_{i}", [D, HYB_IN]) for i in range(NE)]
        hyb_w_out = [din(f"hyb_w_out_{i}", [D, D]) for i in range(NE)]
        hyb_dww_in = din("hyb_dww", [128, NE * 4 * CONVW])
        hvec_in = din("hvec", [128, NE * 12])
        gdn_w_in = [din(f"gdn_w_in_{i}", [D, GDN_IN]) for i in range(NO)]
        gdn_w_out = [din(f"gdn_w_out_{i}", [D, D]) for i in range(NO)]
        gcw_in = din("gcw", [128, max(NO, 1) * 96])
        gng_in = din("gng", [128, max(NO, 1)])
        C.gsm_row_in = din("gsm_row", [1, max(NO, 1) * 16])
        C.gsm_col_in = din("gsm_col", [8, max(NO, 1) * 2])
    C.outT = P.dram("outT", [D, S], F32, kind="ExternalOutput")
    C.hT = P.dram("hT", [D, S], F32)
    C.w_in_bf = [[P.dram(f"w_in_bf_{l}_{j}", [D, 2 * FF], BF16) for j in range(2)] for l in range(depth)]
    C.w_out_bf = [[P.dram(f"w_out_bf_{l}_{j}", [FF, D], BF16) for j in range(2)] for l in range(depth)]
    if mixers:
        C.Wh = [P.dram(f"Wh_{i}", [D, HYB_IN], BF16) for i in range(NE)]
        C.Whp = [P.dram(f"Whp_{i}", [D, 1024], BF16) for i in range(NE)]
        C.Who = [P.dram(f"Who_{i}", [D, D], BF16) for i in range(NE)]
        C.qT_d = P.dram("qT_d", [AW, S], BF16)
        C.kT_d = P.dram("kT_d", [AW, S], BF16)
        C.v_d = P.dram("v_d", [S, AW], BF16)
        C.gluT_d = P.dram("gluT_d", [512, PADL + S], BF16)
        C.catT_d = P.dram("catT_d", [D, S], BF16)
        C.cosT = P.dram("cosT", [128, S], F32)
        C.sinT = P.dram("sinT", [128, S], F32)
        C.den_d = P.dram("den_d", [1, S], F32)
        C.Wg = [P.dram(f"Wg_{i}", [D, GDN_IN], BF16) for i in range(NO)]
        C.Wgo = [P.dram(f"Wgo_{i}", [D, D], BF16) for i in range(NO)]
        C.qT_g = P.dram("qT_g", [D, S], BF16)
        C.kT_g = P.dram("kT_g", [D, S], BF16)
        C.vT_g = P.dram("vT_g", [D, S], BF16)
        C.szT_g = P.dram("szT_g", [D, S], BF16)
        C.gc_d = P.dram("gc_d", [8, S], F32)
    if dbg:
        C.dbg = {k: P.dram("dbg_" + k, shp, dt, kind="ExternalOutput") for k, (shp, dt) in dbg.items()}

    C.PS = [P.ps(f"ps{i}", [128, 512], F32) for i in range(8)]
    C.gains = P.sb("gains", [128, depth * 24 + 8], F32)
    C.consts = P.sb("consts", [128, 2048], F32)
    C.ones_bf = P.sb("ones_bf", [128, 128], BF16)
    C.ident_bf = P.sb("ident_bf", [128, 128], BF16)
    C.mask4 = P.sb("mask4", [128, 512], BF16)
    C.cst = P.sb("cst", [128, 8], F32)
    C.cv32 = [P.sb(f"cv32_{i}", [128, 1024], F32) for i in range(2)]
    C.cv16 = [P.sb(f"cv16_{i}", [128, 1024], BF16) for i in range(2)]
    P.dma(C.gains.t[:, :], gains_in[:, :], writes=[C.gains])
    P.dma(C.consts.t[:, :], consts_in[:, :], writes=[C.consts])
    P.memset("pool", C.cst.t[:, 0:1], EPS, writes=[C.cst])
    P.memset("pool", C.cst.t[:, 1:2], -float(np.pi), writes=[C.cst])
    P.memset("pool", C.cst.t[:, 2:3], 1.0, writes=[C.cst])
    P.memset("pool", C.ones_bf.t[:, :], 1.0, writes=[C.ones_bf])
    P.copy("pool", C.ident_bf.t[:, :], C.consts.t[:, 0:128], reads=[C.consts], writes=[C.ident_bf])
    P.copy("pool", C.mask4.t[:, :], C.consts.t[:, 128:640], reads=[C.consts], writes=[C.mask4])
    if mixers:
        C.hyb_dww = P.sb("hyb_dww", [128, NE * 4 * CONVW], F32)
        C.hvec = P.sb("hvec", [128, NE * 12], F32)
        P.dma(C.hyb_dww.t[:, :], hyb_dww_in[:, :], writes=[C.hyb_dww])
        P.dma(C.hvec.t[:, :], hvec_in[:, :], writes=[C.hvec])
        C.gcw = P.sb("gcw", [128, max(NO, 1) * 96], F32)
        C.gng = P.sb("gng", [128, max(NO, 1)], F32)
        P.dma(C.gcw.t[:, :], gcw_in[:, :], writes=[C.gcw])
        P.dma(C.gng.t[:, :], gng_in[:, :], writes=[C.gng])
    persist = P.sb_mark()

    P.dma(C.hT.t[:, :], xT[:, :], writes=[C.hT])

    def conv_layer(l):
        cast_weight(P, C, ffn_w_in[l][0], C.w_in_bf[l][0], D, 2 * FF, "r")
        cast_weight(P, C, ffn_w_out[l][0], C.w_out_bf[l][0], FF, D, "r")
        if mixers and l % 2 == 0:
            i = l // 2
            cast_weight(P, C, hyb_w_in[i], C.Wh[i], D, HYB_IN, "r")
            cast_perm_qk(P, C, hyb_w_in[i], C.Whp[i])
            cast_weight(P, C, hyb_w_out[i], C.Who[i], D, D, "r")
        if mixers and l % 2 == 1:
            i = l // 2
            cast_weight(P, C, gdn_w_in[i], C.Wg[i], D, GDN_IN, "r")
            cast_weight(P, C, gdn_w_out[i], C.Wgo[i], D, D, "r")
        cast_weight(P, C, ffn_w_in[l][1], C.w_in_bf[l][1], D, 2 * FF, "r")
        cast_weight(P, C, ffn_w_out[l][1], C.w_out_bf[l][1], FF, D, "r")

    for l in range(depth):
        conv_layer(l)

    if mixers and NE > 0:
        zt = C.cv16[0]
        P.memset("pool", zt.t[:, 0:PADL], 0.0, writes=[zt])
        for cc in range(4):
            P.dma(C.gluT_d.t[cc * 128:(cc + 1) * 128, 0:PADL], zt.t[:, 0:PADL], reads=[zt], writes=[(C.gluT_d, ("pad", cc))])
        P.barrier()
        rope_tables(P, C, S, pos)

    def alloc_ffn():
        C.htile = P.sb("htile", [128, 8, T], F32)
        C.xn = P.sb("xn", [128, 8, T], BF16)
        C.gT = P.sb("gT", [128, NFC, T], BF16)
        C.sq = P.sb("sq", [128, 8, 512], BF16)
        C.rstd = P.sb("rstd", [128, 512], F32)
        C.sg = [P.sb(f"sg{i}", [128, 512], F32) for i in range(2)]
        C.wg = [P.sb(f"wg{i}", [128, 8, 512], BF16) for i in range(2)]
        C.wu = [P.sb(f"wu{i}", [128, 8, 512], BF16) for i in range(2)]
        C.wo = [P.sb(f"wo{i}", [128, NFC, 256], BF16) for i in range(2)]

    for l in range(depth):
        P.barrier()
        P.sb_release(persist)
        alloc_ffn()
        ffn_phase(P, C, S, T, C.gains.t[:, l * 24:l * 24 + 8], C.w_in_bf[l][0], C.w_out_bf[l][0])
        if mixers:
            P.barrier()
            P.sb_release(persist)
            gmix = C.gains.t[:, l * 24 + 8:l * 24 + 16]
            if l % 2 == 0:
                i = l // 2
                hyb_proj_phase(P, C, S, T, gmix, C.Wh[i], C.Whp[i], None, l)
                hyb_conv_phase(P, C, S, C.hvec.t[:, i * 12:(i + 1) * 12], C.hyb_dww.t[:, i * 4 * CONVW:(i + 1) * 4 * CONVW])
                hyb_attn_phase(P, C, S)
                mix_out_phase(P, C, S, T, C.Who[i])
            else:
                gdn_layer(P, C, S, T, gmix, l // 2)
            P.sb_release(persist)
            alloc_ffn()
        ffn_phase(P, C, S, T, C.gains.t[:, l * 24 + 16:l * 24 + 24], C.w_in_bf[l][1], C.w_out_bf[l][1])

    P.barrier()
    P.sb_release(persist)
    C.htile = P.sb("htile", [128, 8, T], F32)
    C.fo = P.sb("fo", [128, 8, T], F32)
    C.sq = P.sb("sq", [128, 8, 512], BF16)
    C.rstd = P.sb("rstd", [128, 512], F32)
    final_phase(P, C, S, T, C.gains.t[:, depth * 24:depth * 24 + 8], C.outT)
    P.emit()
    return nc, P


def make_consts():
    f32 = np.float32
    c = np.zeros((128, 2048), f32)
    c[:, 0:128] = np.eye(128, dtype=f32)
    k = np.arange(128)[:, None]
    q = np.arange(128)[None, :]
    cur = (q >= k).astype(f32)
    prev = (q <= k).astype(f32)
    c[:, 128:640] = np.concatenate([cur, prev, cur, prev], axis=1)
    inv = np.power(f32(500000.0), -np.arange(0, 16, 2, dtype=f32) / f32(16)).astype(f32)
    for p in range(128):
        i = p % 64
        c[p, 640] = inv[i % 8] if i < 16 else 0.0
    same = (k // 64) == (q // 64)
    c[:, 1024:1152] = ((q >= k) & same).astype(f32)
    c[:, 1152:1280] = (k < 64).astype(f32) * np.ones((1, 128), f32)
    c[:, 1280:1408] = (k >= 64).astype(f32) * np.ones((1, 128), f32)
    c[:, 1408:1536] = -((k > q) & same).astype(f32)
    return c


def host_inputs(inputs, S=4096, depth=4, mixers=True):
    f32 = np.float32
    NE = (depth + 1) // 2
    x = np.asarray(inputs["x"], dtype=f32)
    B = x.shape[0]
    gains = np.zeros((128, depth * 24 + 8), f32)
    for l in range(depth):
        for j, nm in enumerate(("ffn1_norm", "mix_norm", "ffn2_norm")):
            gains[:, l * 24 + j * 8:l * 24 + j * 8 + 8] = np.asarray(inputs[nm][l], f32).reshape(8, 128).T
    gains[:, depth * 24:] = np.asarray(inputs["final_norm"], f32).reshape(8, 128).T
    shared = {"gains": gains, "consts": make_consts()}
    for l in range(depth):
        for j in range(2):
            shared[f"ffn{j + 1}_w_in_{l}"] = np.ascontiguousarray(inputs[f"ffn{j + 1}_w_in"][l], dtype=f32)
            shared[f"ffn{j + 1}_w_out_{l}"] = np.ascontiguousarray(inputs[f"ffn{j + 1}_w_out"][l], dtype=f32)
    if mixers:
        dww = np.zeros((128, NE * 4 * CONVW), f32)
        hvec = np.zeros((128, NE * 12), f32)
        for i in range(NE):
            shared[f"hyb_w_in_{i}"] = np.ascontiguousarray(inputs["hyb_w_in"][i], dtype=f32)
            shared[f"hyb_w_out_{i}"] = np.ascontiguousarray(inputs["hyb_w_out"][i], dtype=f32)
            w = np.asarray(inputs["hyb_dw_w"][i], f32)
            for cc in range(4):
                dww[:, (i * 4 + cc) * CONVW:(i * 4 + cc + 1) * CONVW] = w[:, cc * 128:(cc + 1) * 128].T
            for j, nm in enumerate(("hyb_dw_b", "hyb_ln_g", "hyb_ln_b")):
                hvec[:, i * 12 + j * 4:i * 12 + j * 4 + 4] = np.asarray(inputs[nm][i], f32).reshape(4, 128).T
        shared["hyb_dww"] = dww
        shared["hvec"] = hvec
        NO = depth // 2
        gcw = np.zeros((128, max(NO, 1) * 96), f32)
        gng = np.zeros((128, max(NO, 1)), f32)
        grow = np.zeros((1, max(NO, 1) * 16), f32)
        gcolm = np.zeros((8, max(NO, 1) * 2), f32)
        for i in range(NO):
            shared[f"gdn_w_in_{i}"] = np.ascontiguousarray(inputs["gdn_w_in"][i], dtype=f32)
            shared[f"gdn_w_out_{i}"] = np.ascontiguousarray(inputs["gdn_w_out"][i], dtype=f32)
            cw = np.asarray(inputs["gdn_conv_w"][i], f32)
            for ch in range(24):
                gcw[:, i * 96 + ch * 4:i * 96 + ch * 4 + 4] = cw[:, ch * 128:(ch + 1) * 128].T
            gng[:, i] = np.asarray(inputs["gdn_norm_g"][i], f32)
            grow[0, i * 16:i * 16 + 8] = np.asarray(inputs["gdn_A_log"][i], f32)
            grow[0, i * 16 + 8:i * 16 + 16] = np.asarray(inputs["gdn_dt_bias"][i], f32)
            gcolm[:, i * 2] = np.asarray(inputs["gdn_A_log"][i], f32)
            gcolm[:, i * 2 + 1] = np.asarray(inputs["gdn_dt_bias"][i], f32)
        shared["gcw"] = gcw
        shared["gng"] = gng
        shared["gsm_row"] = grow
        shared["gsm_col"] = gcolm
    maps = []
    for b in range(B):
        m = dict(shared)
        m["xT"] = np.ascontiguousarray(x[b, :S].T)
        m["pos"] = np.ascontiguousarray(np.asarray(inputs["positions"])[b:b + 1, :S].astype(np.int32))
        maps.append(m)
    return maps


_CACHE = {}


def kernel(**inputs):
    S, depth = 4096, 4
    maps = host_inputs(inputs, S, depth)
    nc, P = build(S, depth)
    res = run_bass_kernel_spmd(nc, maps, core_ids=list(range(NCORES)))
    outs = [np.asarray(r["outT"]) for r in res.results]
    out = np.stack([o.T for o in outs], axis=0)
    return np.ascontiguousarray(out.astype(np.float32))
```

```python
from contextlib import ExitStack
import concourse.bass as bass
import concourse.mybir as mybir

F32 = mybir.dt.float32
BF16 = mybir.dt.bfloat16
I32 = mybir.dt.int32
AF = mybir.ActivationFunctionType
ALU = mybir.AluOpType
AX = mybir.AxisListType

ENGS = ("pe", "act", "dve", "pool", "sp")
EP = 30000
NSLOT = 8
DMA_EP = 1800


class Op:
    __slots__ = ("eng", "idx", "fn", "deps", "signal", "sig", "dma", "slot", "epoch", "val")

    def __init__(self, eng, fn, dma):
        self.eng = eng
        self.fn = fn
        self.dma = dma
        self.deps = []
        self.signal = False
        self.sig = 0


class Buf:
    def __init__(self, name, t):
        self.name = name
        self.t = t
        self.e = {}

    def __getitem__(self, k):
        return self.t[k]

    def _ents(self, key):
        if key is None:
            return list(self.e.values())
        return [self.e[k] for k in (key, None) if k in self.e]

    def read(self, op, key):
        deps = [en[0] for en in self._ents(key) if en[0] is not None]
        en = self.e.setdefault(key, [None, {}, []])
        if op.dma:
            en[2].append(op)
        else:
            en[1][op.eng] = op
        return deps

    def write(self, op, key):
        deps = []
        for en in self._ents(key):
            if en[0] is not None:
                deps.append(en[0])
            deps.extend(en[1].values())
            deps.extend(en[2])
        if key is None:
            self.e = {None: [op, {}, []]}
        else:
            self.e[key] = [op, {}, []]
        return deps


class Prog:
    def __init__(self, nc):
        self.nc = nc
        self.ops = {e: [] for e in ENGS}
        self.stack = ExitStack()
        self.nsem = 0
        self.dq = {e: {"n": 0, "last": [None] * NSLOT, "uses": [0] * NSLOT, "epoch": [0] * NSLOT} for e in ENGS}
        self._esem = {}
        self._dsem = {}
        self.ntile = 0

    SB_LO = 16512
    SB_HI = 229376

    def sb(self, name, shape, dtype):
        self.ntile += 1
        isz = 2 if dtype == BF16 else 4
        n = 1
        for d in shape[1:]:
            n *= d
        nbytes = (n * isz + 63) // 64 * 64
        off = getattr(self, "sb_ptr", self.SB_LO)
        assert off + nbytes <= self.SB_HI, f"SBUF overflow allocating {name}: {off}+{nbytes}"
        self.sb_ptr = off + nbytes
        self.sb_peak = max(getattr(self, "sb_peak", 0), self.sb_ptr)
        t = self.nc.alloc_sbuf_tensor_at(f"{name}_{self.ntile}", list(shape), dtype, offset=off)
        return Buf(name, t)

    def sb_mark(self):
        return getattr(self, "sb_ptr", self.SB_LO)

    def sb_release(self, mark):
        self.sb_ptr = mark

    def barrier(self):
        lasts = []
        for e in ENGS:
            for op in reversed(self.ops[e]):
                if not op.dma and op.fn is not None:
                    lasts.append(op)
                    break
            q = self.dq[e]
            for s_ in range(NSLOT):
                if q["last"][s_] is not None:
                    lasts.append(q["last"][s_])
        for e in ENGS:
            op = Op(e, None, False)
            op.idx = len(self.ops[e])
            for d in lasts:
                if (not d.dma) and d.eng == e:
                    continue
                op.deps.append(d)
                if not d.dma:
                    d.signal = True
            self.ops[e].append(op)

    def ps(self, name, shape, dtype=F32):
        self.ntile += 1
        t = self.stack.enter_context(self.nc.psum_tensor(f"{name}_{self.ntile}", list(shape), dtype))
        return Buf(name, t)

    def dram(self, name, shape, dtype, kind="Internal"):
        t = self.nc.dram_tensor(name, list(shape), dtype, kind=kind)
        return Buf(name, t)

    def _sem(self, name):
        self.nsem += 1
        return self.stack.enter_context(self.nc.semaphore(f"{name}_{self.nsem}"))

    def add(self, eng, fn, reads=(), writes=(), dma=False):
        op = Op(eng, fn, dma)
        op.idx = len(self.ops[eng])
        deps = []
        for r in reads:
            b, k = r if isinstance(r, tuple) else (r, None)
            if b is not None:
                deps.extend(b.read(op, k))
        for w in writes:
            b, k = w if isinstance(w, tuple) else (w, None)
            if b is not None:
                deps.extend(b.write(op, k))
        if dma:
            q = self.dq[eng]
            s = q["n"] % NSLOT
            q["n"] += 1
            if q["last"][s] is not None:
                deps.append(q["last"][s])
            if q["uses"][s] >= DMA_EP:
                q["uses"][s] = 0
                q["epoch"][s] += 1
            q["uses"][s] += 1
            op.slot = s
            op.epoch = q["epoch"][s]
            op.val = 16 * q["uses"][s]
            q["last"][s] = op
        seen = set()
        for d in deps:
            if d is op or id(d) in seen:
                continue
            seen.add(id(d))
            if (not d.dma) and d.eng == "pe" and eng == "pe" and not dma:
                continue
            op.deps.append(d)
            if not d.dma:
                d.signal = True
        self.ops[eng].append(op)
        return op

    def dma(self, out, in_, reads=(), writes=(), eng="sp", **kw):
        return self.add(eng, lambda h: h.dma_start(out=out, in_=in_, **kw), reads, writes, dma=True)

    def mm(self, out, lhsT, rhs, start=True, stop=True, reads=(), writes=(), **kw):
        return self.add("pe", lambda h: h.matmul(out, lhsT, rhs, start=start, stop=stop, **kw), reads, writes)

    def tr(self, out, in_, ident, reads=(), writes=()):
        return self.add("pe", lambda h: h.transpose(out, in_, ident), reads, writes)

    def act(self, out, in_, func, reads=(), writes=(), eng="act", **kw):
        return self.add(eng, lambda h: h.activation(out=out, in_=in_, func=func, **kw), reads, writes)

    def tt(self, eng, out, in0, in1, op, reads=(), writes=()):
        return self.add(eng, lambda h: h.tensor_tensor(out=out, in0=in0, in1=in1, op=op), reads, writes)

    def ts(self, eng, out, in0, s1, s2, op0, op1=None, reads=(), writes=()):
        if op1 is None:
            return self.add(eng, lambda h: h.tensor_scalar(out=out, in0=in0, scalar1=s1, scalar2=None, op0=op0), reads, writes)
        return self.add(eng, lambda h: h.tensor_scalar(out=out, in0=in0, scalar1=s1, scalar2=s2, op0=op0, op1=op1), reads, writes)

    def stt(self, eng, out, in0, scalar, in1, op0, op1, reads=(), writes=()):
        return self.add(eng, lambda h: h.scalar_tensor_tensor(out=out, in0=in0, scalar=scalar, in1=in1, op0=op0, op1=op1), reads, writes)

    def copy(self, eng, out, in_, reads=(), writes=()):
        if eng == "act":
            return self.add(eng, lambda h: h.copy(out=out, in_=in_), reads, writes)
        return self.add(eng, lambda h: h.tensor_copy(out=out, in_=in_), reads, writes)

    def memset(self, eng, ap, val, writes=()):
        return self.add(eng, lambda h: h.memset(ap, val), (), writes)

    def esem(self, e, ep):
        k = (e, ep)
        if k not in self._esem:
            self._esem[k] = self._sem(f"e_{e}{ep}")
        return self._esem[k]

    def dsem(self, e, slot, ep):
        k = (e, slot, ep)
        if k not in self._dsem:
            self._dsem[k] = self._sem(f"d_{e}{slot}_{ep}")
        return self._dsem[k]

    def emit(self):
        nc = self.nc
        for e in ENGS:
            n = 0
            for op in self.ops[e]:
                if (not op.dma) and op.signal:
                    n += 1
                    op.sig = n
        for e in ENGS:
            for op in self.ops[e]:
                if op.dma:
                    self.dsem(e, op.slot, op.epoch)
                elif op.signal:
                    self.esem(e, (op.sig - 1) // EP)
        final = []
        for e in ENGS:
            q = self.dq[e]
            for s in range(NSLOT):
                if q["last"][s] is not None:
                    o = q["last"][s]
                    final.append((self.dsem(e, o.slot, o.epoch), o.val))
        P = self

        def run(e, h):
            weng = {}
            wdma = {}
            for op in P.ops[e]:
                for d in op.deps:
                    if d.dma:
                        k = (d.eng, d.slot, d.epoch)
                        if wdma.get(k, 0) < d.val:
                            h.wait_ge(P.dsem(*k), d.val)
                            wdma[k] = d.val
                    else:
                        if weng.get(d.eng, 0) < d.sig:
                            ep, v = divmod(d.sig - 1, EP)
                            h.wait_ge(P.esem(d.eng, ep), v + 1)
                            weng[d.eng] = d.sig
                if op.fn is None:
                    continue
                ins = op.fn(h)
                if op.dma:
                    ins.then_inc(P.dsem(e, op.slot, op.epoch), 16)
                elif op.signal:
                    ins.then_inc(P.esem(e, (op.sig - 1) // EP), 1)
            if e == "sp":
                for sem, v in final:
                    h.wait_ge(sem, v)

        with nc.Block() as block:
            @block.tensor
            def _(h):
                run("pe", h)

            @block.scalar
            def _(h):
                run("act", h)

            @block.vector
            def _(h):
                run("dve", h)

            @block.gpsimd
            def _(h):
                run("pool", h)

            @block.sync
            def _(h):
                run("sp", h)

    def close(self):
        self.stack.close()

import numpy as np
from concourse.bass_utils import run_bass_kernel_spmd

D = 1024
FF = 2816
NFC = FF // 128
EPS = 1e-6
NCORES = 8


class Ctx:
    pass


def cast_jobs(src_ap, dst, R, Cc):
    CB = 1024
    jobs = []
    for r0 in range(0, R, 128):
        rr = min(128, R - r0)
        for c0 in range(0, Cc, CB):
            cc = min(CB, Cc - c0)
            jobs.append(("plain", src_ap, dst, r0, rr, c0, cc))
    return jobs


def perm_jobs(src_ap, dst):
    return [("perm", src_ap, dst, r0, 128, 0, 1024) for r0 in range(0, D, 128)]


def run_cast_jobs(P, C, jobs, eng="pool"):
    LA = len(C.cv32) - 1
    n = len(jobs)
    for i in range(n + LA):
        if i < n:
            kind, src_ap, dst, r0, rr, c0, cc = jobs[i]
            a = C.cv32[C.cvi % len(C.cv32)]
            jobs[i] = jobs[i] + (a,)
            C.cvi += 1
            P.dma(a.t[0:rr, 0:cc], src_ap[r0:r0 + rr, c0:c0 + cc], writes=[a], eng=eng)
        j = i - LA
        if j >= 0:
            kind, src_ap, dst, r0, rr, c0, cc, a = jobs[j]
            b = C.cv16[C.cvj % len(C.cv16)]
            C.cvj += 1
            if kind == "plain":
                P.copy("pool", b.t[0:rr, 0:cc], a.t[0:rr, 0:cc], reads=[a], writes=[b])
            else:
                av = a.t[:, 0:1024].rearrange("p (h e) -> p h e", e=64)
                bv = b.t[:, 0:1024].rearrange("p (h e) -> p h e", e=64)
                P.memset("pool", b.t[:, 0:1024], 0.0, writes=[b])
                P.ts("pool", bv[:, :, 0:8], av[:, :, 8:16], -1.0, None, ALU.mult, reads=[a], writes=[b])
                P.copy("pool", bv[:, :, 8:16], av[:, :, 0:8], reads=[a], writes=[b])
            P.dma(dst.t[r0:r0 + rr, c0:c0 + cc], b.t[0:rr, 0:cc], reads=[b], writes=[(dst, ("r", r0, c0))], eng=eng)


def rmsnorm_tile(P, C, src, gcol, dst, ntok, srckey=None, doff=0):
    for s0 in range(0, ntok, 512):
        n = min(512, ntok - s0)
        sl = slice(s0, s0 + n)
        ps = C.PS[7]
        for c in range(8):
            P.act(C.sq.t[:, c, 0:n], src.t[:, c, sl], AF.Square, reads=[(src, srckey)], writes=[(C.sq, c)])
        for c in range(8):
            P.mm(ps.t[:, 0:n], C.ones_bf.t[:, :], C.sq.t[:, c, 0:n], start=(c == 0), stop=(c == 7),
                 reads=[(C.sq, c), C.ones_bf], writes=[ps])
        P.act(C.rstd.t[:, 0:n], ps.t[:, 0:n], AF.Ln, reads=[ps, C.cst], writes=[C.rstd], scale=1.0 / D, bias=C.cst.t[:, 0:1])
        P.act(C.rstd.t[:, 0:n], C.rstd.t[:, 0:n], AF.Exp, reads=[C.rstd], writes=[C.rstd], scale=-0.5)
        for c in range(8):
            P.stt("dve", dst.t[:, c, doff + s0:doff + s0 + n], src.t[:, c, sl], gcol[:, c:c + 1], C.rstd.t[:, 0:n], ALU.mult, ALU.mult,
                  reads=[(src, srckey), C.rstd, C.gains], writes=[(dst, (c, doff + s0))])


def ffn_phase(P, C, S, T, gcol, w_in, w_out, jobs=None):
    hview = C.hT.t.rearrange("(c p) t -> p c t", p=128)
    WB = 512
    nwb = FF // WB + (1 if FF % WB else 0)
    wi = 0
    wo_i = 0
    if jobs:
        run_cast_jobs(P, C, jobs)
    for tt in range(S // T):
        tsl = slice(tt * T, (tt + 1) * T)
        P.dma(C.htile.t[:, :, :], hview[:, :, tsl], reads=[(C.hT, tt)], writes=[C.htile])
        rmsnorm_tile(P, C, C.htile, gcol, C.xn, T)
        for wb in range(nwb):
            c0 = wb * WB
            cw = min(WB, FF - c0)
            wg = C.wg[wi % 2]
            wu = C.wu[wi % 2]
            wi += 1
            P.dma(wg.t[:, :, 0:cw], w_in.t[:, c0:c0 + cw].rearrange("(c p) f -> p c f", p=128), reads=[w_in], writes=[wg])
            P.dma(wu.t[:, :, 0:cw], w_in.t[:, FF + c0:FF + c0 + cw].rearrange("(c p) f -> p c f", p=128), reads=[w_in], writes=[wu])
            for fi in range(cw // 128):
                f = c0 // 128 + fi
                fs = slice(fi * 128, (fi + 1) * 128)
                for s0 in range(0, T, 512):
                    sl = slice(s0, s0 + 512)
                    k = (f * (T // 512) + s0 // 512) % 2
                    pg = C.PS[k]
                    pu = C.PS[2 + k]
                    sg = C.sg[k]
                    for c in range(8):
                        P.mm(pg.t[:, :], wg.t[:, c, fs], C.xn.t[:, c, sl], start=(c == 0), stop=(c == 7),
                             reads=[wg, C.xn], writes=[pg])
                    for c in range(8):
                        P.mm(pu.t[:, :], wu.t[:, c, fs], C.xn.t[:, c, sl], start=(c == 0), stop=(c == 7),
                             reads=[wu, C.xn], writes=[pu])
                    P.act(sg.t[:, :], pg.t[:, :], AF.Silu, reads=[pg], writes=[sg])
                    P.tt("dve", C.gT.t[:, f, sl], sg.t[:, :], pu.t[:, :], ALU.mult, reads=[sg, pu], writes=[(C.gT, (f, s0))])
        OB = 256
        for ob in range(D // OB):
            wo = C.wo[wo_i % 2]
            wo_i += 1
            P.dma(wo.t[:, :, :], w_out.t[:, ob * OB:(ob + 1) * OB].rearrange("(f p) d -> p f d", p=128), reads=[w_out], writes=[wo])
            for di in range(OB // 128):
                dc = ob * (OB // 128) + di
                for s0 in range(0, T, 512):
                    sl = slice(s0, s0 + 512)
                    k = (dc * (T // 512) + s0 // 512) % 2
                    py = C.PS[4 + k]
                    for f in range(NFC):
                        P.mm(py.t[:, :], wo.t[:, f, di * 128:(di + 1) * 128], C.gT.t[:, f, sl], start=(f == 0), stop=(f == NFC - 1),
                             reads=[wo, C.gT], writes=[py])
                    P.stt("dve", C.htile.t[:, dc, sl], py.t[:, :], 0.5, C.htile.t[:, dc, sl], ALU.mult, ALU.add,
                          reads=[py, (C.htile, (dc, s0))], writes=[(C.htile, (dc, s0))])
        P.dma(hview[:, :, tsl], C.htile.t[:, :, :], reads=[C.htile], writes=[(C.hT, tt)])


def final_phase(P, C, S, T, gcol, outT):
    hview = C.hT.t.rearrange("(c p) t -> p c t", p=128)
    oview = outT.t.rearrange("(c p) t -> p c t", p=128)
    for tt in range(S // T):
        tsl = slice(tt * T, (tt + 1) * T)
        P.dma(C.htile.t[:, :, :], hview[:, :, tsl], reads=[(C.hT, tt)], writes=[C.htile])
        rmsnorm_tile(P, C, C.htile, gcol, C.fo, T)
        P.dma(oview[:, :, tsl], C.fo.t[:, :, :], reads=[C.fo], writes=[(outT, tt)])


AW = 512
HYB_IN = 2560
CONVW = 31
PADL = 32
DIL = ((128, 1), (512, 4), (2048, 16))


def rope_tables(P, C, S, pos_ap):
    m = P.sb_mark()
    posi = P.sb("posi", [128, S], I32)
    ang = P.sb("ang", [128, S], F32)
    tmp = P.sb("tmp", [128, S], F32)
    res = P.sb("res", [128, S], F32)
    P.dma(posi.t[:, :], pos_ap[0:1, :].to_broadcast([128, S]), writes=[posi])
    P.copy("dve", ang.t[:, :], posi.t[:, :], reads=[posi], writes=[ang])
    P.ts("dve", ang.t[:, :], ang.t[:, :], C.consts.t[:, 640:641], None, ALU.mult, reads=[ang, C.consts], writes=[ang])
    PI = float(np.pi)
    ki = posi
    for shift, dst in ((0.5 * PI, C.cosT), (0.0, C.sinT)):
        P.ts("dve", tmp.t[:, :], ang.t[:, :], shift, None, ALU.add, reads=[ang], writes=[tmp])
        P.ts("dve", ki.t[:, :], tmp.t[:, :], 1.0 / (2 * PI), 0.5, ALU.mult, ALU.add, reads=[tmp], writes=[ki])
        P.copy("dve", res.t[:, :], ki.t[:, :], reads=[ki], writes=[res])
        P.stt("dve", tmp.t[:, :], res.t[:, :], -2 * PI, tmp.t[:, :], ALU.mult, ALU.add, reads=[res, tmp], writes=[tmp])
        P.ts("dve", res.t[:, :], tmp.t[:, :], -PI, 2 * PI, ALU.is_lt, ALU.mult, reads=[tmp], writes=[res])
        P.tt("dve", tmp.t[:, :], tmp.t[:, :], res.t[:, :], ALU.add, reads=[tmp, res], writes=[tmp])
        P.ts("dve", res.t[:, :], tmp.t[:, :], PI, -2 * PI, ALU.is_gt, ALU.mult, reads=[tmp], writes=[res])
        P.tt("dve", tmp.t[:, :], tmp.t[:, :], res.t[:, :], ALU.add, reads=[tmp, res], writes=[tmp])
        P.ts("dve", tmp.t[:, :], tmp.t[:, :], PI, -PI, ALU.min, ALU.max, reads=[tmp], writes=[tmp])
        P.act(res.t[:, :], tmp.t[:, :], AF.Sin, reads=[tmp], writes=[res])
        P.dma(dst.t[:, :], res.t[:, :], reads=[res], writes=[dst])
    P.barrier()
    P.sb_release(m)


def hyb_proj_phase(P, C, S, T, gcol, Wh, Whp, hv, l):
    m = P.sb_mark()
    C.htile = P.sb("htile", [128, 8, T], F32)
    C.xn = P.sb("xn", [128, 8, T], BF16)
    C.sq = P.sb("sq", [128, 8, 512], BF16)
    C.rstd = P.sb("rstd", [128, 512], F32)
    cos = P.sb("cos", [128, T], F32)
    sin = P.sb("sin", [128, T], F32)
    wa = [P.sb(f"wa{i}", [128, 8, 512], BF16) for i in range(2)]
    wb = [P.sb(f"wb{i}", [128, 8, 512], BF16) for i in range(2)]
    wv = P.sb("wv", [128, 8, 512], BF16)
    wn = 0
    t1 = [P.sb(f"t1{i}", [128, 512], F32) for i in range(2)]
    t2 = [P.sb(f"t2{i}", [128, 512], F32) for i in range(2)]
    ob = [P.sb(f"ob{i}", [128, T], BF16) for i in range(2)]
    vt = P.sb("vt", [128, T // 128, 512], BF16)
    hview = C.hT.t.rearrange("(c p) t -> p c t", p=128)
    n = 0
    oi = 0
    for tt in range(S // T):
        tsl = slice(tt * T, (tt + 1) * T)
        P.dma(C.htile.t[:, :, :], hview[:, :, tsl], reads=[(C.hT, tt)], writes=[C.htile])
        rmsnorm_tile(P, C, C.htile, gcol, C.xn, T)
        P.dma(cos.t[:, :], C.cosT.t[:, tsl], reads=[C.cosT], writes=[cos])
        P.dma(sin.t[:, :], C.sinT.t[:, tsl], reads=[C.sinT], writes=[sin])
        for qk in range(2):
            w1 = wa[wn % 2]
            w2 = wb[wn % 2]
            wn += 1
            P.dma(w1.t[:, :, :], Wh.t[:, qk * 512:(qk + 1) * 512].rearrange("(c p) f -> p c f", p=128), reads=[Wh], writes=[w1])
            P.dma(w2.t[:, :, :], Whp.t[:, qk * 512:(qk + 1) * 512].rearrange("(c p) f -> p c f", p=128), reads=[Whp], writes=[w2])
            for ch in range(4):
                cs_ = slice(ch * 128, (ch + 1) * 128)
                o = ob[oi % 2]
                oi += 1
                for s0 in range(0, T, 512):
                    sl = slice(s0, s0 + 512)
                    k = n % 2
                    n += 1
                    pa, pb = C.PS[k], C.PS[2 + k]
                    for c in range(8):
                        P.mm(pa.t[:, :], w1.t[:, c, cs_], C.xn.t[:, c, sl], start=(c == 0), stop=(c == 7), reads=[w1, C.xn], writes=[pa])
                    for c in range(8):
                        P.mm(pb.t[:, :], w2.t[:, c, cs_], C.xn.t[:, c, sl], start=(c == 0), stop=(c == 7), reads=[w2, C.xn], writes=[pb])
                    P.tt("dve", t1[k].t[:, :], pa.t[:, :], cos.t[:, sl], ALU.mult, reads=[pa, cos], writes=[t1[k]])
                    P.tt("dve", t2[k].t[:, :], pb.t[:, :], sin.t[:, sl], ALU.mult, reads=[pb, sin], writes=[t2[k]])
                    P.tt("pool", o.t[:, sl], t1[k].t[:, :], t2[k].t[:, :], ALU.add, reads=[t1[k], t2[k]], writes=[(o, s0)])
                dst = C.qT_d if qk == 0 else C.kT_d
                P.dma(dst.t[ch * 128:(ch + 1) * 128, tsl], o.t[:, :], reads=[o], writes=[(dst, (ch, tt))])
        P.dma(wv.t[:, :, :], Wh.t[:, 1024:1536].rearrange("(c p) f -> p c f", p=128), reads=[Wh], writes=[wv])
        for blk in range(T // 128):
            pv = C.PS[4 + blk % 2]
            bs = slice(blk * 128, (blk + 1) * 128)
            for c in range(8):
                P.mm(pv.t[:, :], C.xn.t[:, c, bs], wv.t[:, c, :], start=(c == 0), stop=(c == 7), reads=[wv, C.xn], writes=[pv])
            P.copy("act", vt.t[:, blk, :], pv.t[:, :], reads=[pv], writes=[(vt, blk)])
        P.dma(C.v_d.t[tsl, :].rearrange("(b p) f -> p b f", p=128), vt.t[:, :, :], reads=[vt], writes=[(C.v_d, tt)])
        w1 = wa[wn % 2]
        w2 = wb[wn % 2]
        wn += 1
        P.dma(w1.t[:, :, :], Wh.t[:, 1536:2048].rearrange("(c p) f -> p c f", p=128), reads=[Wh], writes=[w1])
        P.dma(w2.t[:, :, :], Wh.t[:, 2048:2560].rearrange("(c p) f -> p c f", p=128), reads=[Wh], writes=[w2])
        for ch in range(4):
            cs_ = slice(ch * 128, (ch + 1) * 128)
            o = ob[oi % 2]
            oi += 1
            for s0 in range(0, T, 512):
                sl = slice(s0, s0 + 512)
                k = n % 2
                n += 1
                pa, pb = C.PS[k], C.PS[2 + k]
                for c in range(8):
                    P.mm(pa.t[:, :], w1.t[:, c, cs_], C.xn.t[:, c, sl], start=(c == 0), stop=(c == 7), reads=[w1, C.xn], writes=[pa])
                for c in range(8):
                    P.mm(pb.t[:, :], w2.t[:, c, cs_], C.xn.t[:, c, sl], start=(c == 0), stop=(c == 7), reads=[w2, C.xn], writes=[pb])
                P.act(t1[k].t[:, :], pb.t[:, :], AF.Sigmoid, reads=[pb], writes=[t1[k]])
                P.tt("dve", o.t[:, sl], pa.t[:, :], t1[k].t[:, :], ALU.mult, reads=[pa, t1[k]], writes=[(o, s0)])
            P.dma(C.gluT_d.t[ch * 128:(ch + 1) * 128, PADL + tt * T:PADL + (tt + 1) * T], o.t[:, :], reads=[o], writes=[(C.gluT_d, (ch, tt))])
    P.barrier()
    P.sb_release(m)


def hyb_conv_phase(P, C, S, hvcol, dww):
    m = P.sb_mark()
    diag = P.sb("diag", [128, 4, CONVW, 128], BF16)
    glu = [P.sb(f"glu{i}", [128, PADL + 512], BF16) for i in range(2)]
    csb = P.sb("csb", [128, 4, 512], F32)
    cbf = P.sb("cbf", [128, 4, 512], BF16)
    csq = P.sb("csq", [128, 4, 512], BF16)
    mean = P.sb("mean", [128, 512], F32)
    msq = P.sb("msq", [128, 512], F32)
    rstd = P.sb("rstdc", [128, 512], F32)
    tmp = [P.sb(f"ctmp{i}", [128, 512], F32) for i in range(2)]
    co = [P.sb(f"co{i}", [128, 512], BF16) for i in range(2)]
    for cc in range(4):
        for k in range(CONVW):
            P.ts("pool", diag.t[:, cc, k, :], C.ident_bf.t[:, :], dww[:, cc * CONVW + k:cc * CONVW + k + 1], None, ALU.mult,
                 reads=[C.ident_bf, C.hyb_dww], writes=[(diag, (cc, k))])
    gi = 0
    oi = 0
    off = PADL - (CONVW - 1)
    for t0 in range(0, S, 512):
        for cc in range(4):
            g = glu[gi % 2]
            gi += 1
            P.dma(g.t[:, :], C.gluT_d.t[cc * 128:(cc + 1) * 128, t0:t0 + PADL + 512], reads=[C.gluT_d], writes=[g])
            pc = C.PS[cc % 2]
            for k in range(CONVW):
                P.mm(pc.t[:, :], diag.t[:, cc, k, :], g.t[:, off + k:off + k + 512], start=(k == 0), stop=(k == CONVW - 1),
                     reads=[diag, g], writes=[pc])
            P.act(csb.t[:, cc, :], pc.t[:, :], AF.Identity, reads=[pc, C.hvec], writes=[(csb, cc)], bias=hvcol[:, cc:cc + 1])
            P.act(csq.t[:, cc, :], pc.t[:, :], AF.Square, reads=[pc, C.hvec], writes=[(csq, cc)], bias=hvcol[:, cc:cc + 1])
            P.copy("pool", cbf.t[:, cc, :], csb.t[:, cc, :], reads=[(csb, cc)], writes=[(cbf, cc)])
        p1, p2 = C.PS[4], C.PS[5]
        for cc in range(4):
            P.mm(p1.t[:, :], C.ones_bf.t[:, :], cbf.t[:, cc, :], start=(cc == 0), stop=(cc == 3), reads=[(cbf, cc), C.ones_bf], writes=[p1])
        for cc in range(4):
            P.mm(p2.t[:, :], C.ones_bf.t[:, :], csq.t[:, cc, :], start=(cc == 0), stop=(cc == 3), reads=[(csq, cc), C.ones_bf], writes=[p2])
        P.ts("dve", mean.t[:, :], p1.t[:, :], 1.0 / 512, None, ALU.mult, reads=[p1], writes=[mean])
        P.tt("dve", msq.t[:, :], mean.t[:, :], mean.t[:, :], ALU.mult, reads=[mean], writes=[msq])
        P.stt("dve", msq.t[:, :], p2.t[:, :], 1.0 / 512, msq.t[:, :], ALU.mult, ALU.subtract, reads=[p2, msq], writes=[msq])
        P.act(rstd.t[:, :], msq.t[:, :], AF.Ln, reads=[msq, C.cst], writes=[rstd], bias=C.cst.t[:, 0:1])
        P.act(rstd.t[:, :], rstd.t[:, :], AF.Exp, reads=[rstd], writes=[rstd], scale=-0.5)
        for cc in range(4):
            tm = tmp[cc % 2]
            o = co[oi % 2]
            oi += 1
            P.tt("dve", tm.t[:, :], csb.t[:, cc, :], mean.t[:, :], ALU.subtract, reads=[(csb, cc), mean], writes=[tm])
            P.tt("dve", tm.t[:, :], tm.t[:, :], rstd.t[:, :], ALU.mult, reads=[tm, rstd], writes=[tm])
            P.act(o.t[:, :], tm.t[:, :], AF.Silu, reads=[tm, C.hvec], writes=[o], scale=hvcol[:, 4 + cc:5 + cc], bias=hvcol[:, 8 + cc:9 + cc])
            P.dma(C.catT_d.t[512 + cc * 128:512 + (cc + 1) * 128, t0:t0 + 512], o.t[:, :], reads=[o], writes=[(C.catT_d, (4 + cc, t0))])
    P.barrier()
    P.sb_release(m)


def hyb_attn_phase(P, C, S):
    m = P.sb_mark()
    qT = P.sb("qTa", [128, S], BF16)
    kT = P.sb("kTa", [128, S], BF16)
    acc = [P.sb(f"acc{i}", [65, S], F32) for i in range(2)]
    bc = P.sb("bc", [64, S], F32)
    rden = P.sb("rden", [65, S], F32)
    ao = P.sb("ao", [64, S], BF16)
    NV = 3
    vts = [P.sb(f"vts{i}", [128, 32, 2, 65], BF16) for i in range(NV)]
    NPT = 6
    pts = [P.sb(f"pt{i}", [128, 512], BF16) for i in range(NPT)]
    for v_ in vts:
        P.memset("pool", v_.t[:, :, :, 64:65], 1.0, writes=[(v_, "ones")])
    cnt = {"pt": 0, "ps": 0}
    groups = [(r, c) for (_, r) in DIL for c in range(r)]

    def load_v(hp, g):
        r, c = groups[g]
        nb = (S // r) // 128
        vt = vts[g % NV]
        for hh in range(2):
            src = C.v_d.t[:, hp * 128 + hh * 64:hp * 128 + (hh + 1) * 64]
            src = src[c::r, :].rearrange("(j p) e -> p j e", p=128)
            for jb in range(0, nb, 8):
                je = min(nb, jb + 8)
                P.dma(vt.t[:, jb:je, hh, 0:64], src[:, jb:je, :], reads=[C.v_d], writes=[(vt, (hh, jb))])

    for hp in range(4):
        P.dma(qT.t[:, :], C.qT_d.t[hp * 128:(hp + 1) * 128, :], reads=[C.qT_d], writes=[qT])
        P.dma(kT.t[:, :], C.kT_d.t[hp * 128:(hp + 1) * 128, :], reads=[C.kT_d], writes=[kT])
        first = [True, True]
        items = []
        load_v(hp, 0)
        for g, (r, c) in enumerate(groups):
            L = S // r
            nb = L // 128
            vt = vts[g % NV]
            for hh in range(2):
                hb = hh * 64
                A = acc[hh]
                ptl = {}
                isfirst = (g == 0)

                def tok(j0, nblk, r=r, c=c):
                    return slice(c + r * 128 * j0, c + r * (128 * (j0 + nblk) - 1) + 1, r)

                for j0 in range(0, nb, 2):
                    def stageA(j0=j0, nb=nb, hb=hb, tok=tok, ptl=ptl, g=g, hh=hh, hp=hp):
                        if j0 == 0 and hh == 0 and g + 1 < len(groups):
                            load_v(hp, g + 1)
                        nkb = min(2, nb - j0)
                        pt = pts[cnt["pt"] % NPT]
                        cnt["pt"] += 1
                        ps = C.PS[cnt["ps"] % 2]
                        cnt["ps"] += 1
                        widths = []
                        for jj in range(nkb):
                            j = j0 + jj
                            nq = 2 if j + 1 < nb else 1
                            P.mm(ps.t[:, jj * 256:jj * 256 + nq * 128], kT.t[hb:hb + 64, tok(j, 1)], qT.t[hb:hb + 64, tok(j, nq)],
                                 reads=[kT, qT], writes=[ps])
                            widths.append(nq)
                        wtot = 256 * (nkb - 1) + 128 * widths[-1]
                        P.act(pt.t[:, 0:wtot], ps.t[:, 0:wtot], AF.Exp, reads=[ps], writes=[pt], scale=0.125)
                        P.tt("pool", pt.t[:, 0:wtot], pt.t[:, 0:wtot], C.mask4.t[:, 0:wtot], ALU.mult, reads=[pt, C.mask4], writes=[pt])
                        for jj in range(nkb):
                            ptl[j0 + jj] = (pt, jj * 256)

                    def stageB(j0=j0, nb=nb, tok=tok, ptl=ptl, vt=vt, hh=hh, A=A, isfirst=isfirst):
                        nkb = min(2, nb - j0)
                        for jj in range(nkb):
                            n_ = j0 + jj
                            q4 = n_ % 4
                            po = C.PS[4 + (n_ // 4) % 2]
                            ptc, oc = ptl[n_]
                            has_prev = n_ > 0
                            P.mm(po.t[0:65, q4 * 128:(q4 + 1) * 128], vt.t[:, n_, hh, :], ptc.t[:, oc:oc + 128], start=True, stop=not has_prev,
                                 reads=[vt, ptc], writes=[(po, q4)])
                            if has_prev:
                                ptp, op_ = ptl[n_ - 1]
                                P.mm(po.t[0:65, q4 * 128:(q4 + 1) * 128], vt.t[:, n_ - 1, hh, :], ptp.t[:, op_ + 128:op_ + 256], start=False, stop=True,
                                     reads=[vt, ptp], writes=[(po, q4)])
                            if q4 == 3 or n_ == nb - 1:
                                nq4 = q4 + 1
                                b0 = n_ - q4
                                dsts = A.t[:, tok(b0, nq4)]
                                if isfirst:
                                    P.copy("act", dsts, po.t[0:65, 0:nq4 * 128], reads=[po], writes=[A])
                                else:
                                    P.tt("dve", dsts, dsts, po.t[0:65, 0:nq4 * 128], ALU.add, reads=[po, A], writes=[A])

                    items.append((stageA, stageB))
        for k_ in range(len(items) + 1):
            if k_ < len(items):
                items[k_][0]()
            if k_ >= 1:
                items[k_ - 1][1]()
        for hh in range(2):
            A = acc[hh]
            h = hp * 2 + hh
            P.add("dve", lambda e, A=A: e.reciprocal(out=rden.t[64:65, :], in_=A.t[64:65, :]), reads=[A], writes=[rden])
            P.dma(C.den_d.t[0:1, :], rden.t[64:65, :], reads=[rden], writes=[C.den_d])
            P.dma(bc.t[:, :], C.den_d.t[0:1, :].to_broadcast([64, S]), reads=[C.den_d], writes=[bc])
            P.tt("dve", ao.t[:, :], A.t[0:64, :], bc.t[:, :], ALU.mult, reads=[A, bc], writes=[ao])
            P.dma(C.catT_d.t[h * 64:(h + 1) * 64, :], ao.t[:, :], reads=[ao], writes=[(C.catT_d, ("a", h))])
    P.barrier()
    P.sb_release(m)


def mix_out_phase(P, C, S, T, Wo):
    m = P.sb_mark()
    htile = P.sb("htile", [128, 8, T], F32)
    cat = P.sb("cat", [128, 8, T], BF16)
    wo = P.sb("wo_m", [128, 8, D], BF16)
    hview = C.hT.t.rearrange("(c p) t -> p c t", p=128)
    cview = C.catT_d.t.rearrange("(c p) t -> p c t", p=128)
    P.dma(wo.t[:, :, :], Wo.t.rearrange("(c p) d -> p c d", p=128), reads=[Wo], writes=[wo])
    n = 0
    for tt in range(S // T):
        tsl = slice(tt * T, (tt + 1) * T)
        P.dma(htile.t[:, :, :], hview[:, :, tsl], reads=[(C.hT, tt)], writes=[htile])
        P.dma(cat.t[:, :, :], cview[:, :, tsl], reads=[C.catT_d], writes=[cat])
        for dc in range(8):
            for s0 in range(0, T, 512):
                sl = slice(s0, s0 + 512)
                py = C.PS[n % 2]
                n += 1
                for f in range(8):
                    P.mm(py.t[:, :], wo.t[:, f, dc * 128:(dc + 1) * 128], cat.t[:, f, sl], start=(f == 0), stop=(f == 7), reads=[wo, cat], writes=[py])
                P.tt("dve", htile.t[:, dc, sl], py.t[:, :], htile.t[:, dc, sl], ALU.add, reads=[py, (htile, (dc, s0))], writes=[(htile, (dc, s0))])
        P.dma(hview[:, :, tsl], htile.t[:, :, :], reads=[htile], writes=[(C.hT, tt)])
    P.barrier()
    P.sb_release(m)


GDN_IN = 4112
import os as _os
GDN_STOP = int(_os.environ.get('GDN_STOP', '0'))
QSCALE = 128.0 ** -0.5


def gdn_alloc_scalars(P, C, S):
    NB = S // 128
    sc = Ctx()
    sc.beta = P.sb("g_beta", [128, NB, 8], F32)
    sc.gc = P.sb("g_gc", [128, NB, 8], F32)
    sc.kd = P.sb("g_kd", [128, NB, 8], F32)
    sc.kbg = P.sb("g_kbg", [128, NB, 8], F32)
    sc.egl = P.sb("g_egl", [128, 2, NB, 8], F32)
    return sc


def gdn_proj_phase(P, C, S, gcol, i, sc, Wg):
    m = P.sb_mark()
    NB = S // 128
    xn = P.sb("xnf", [128, 8, S], BF16)
    mA = P.sb_mark()
    ht = [P.sb(f"ht{j}", [128, 8, 512], F32) for j in range(1)]
    C.sq = P.sb("sq", [128, 8, 512], BF16)
    C.rstd = P.sb("rstd", [128, 512], F32)
    hview = C.hT.t.rearrange("(c p) t -> p c t", p=128)
    for t0 in range(0, S, 512):
        h = ht[0]
        P.dma(h.t[:, :, :], hview[:, :, t0:t0 + 512], reads=[C.hT], writes=[h])
        rmsnorm_tile(P, C, h, gcol, xn, 512, doff=t0)
    dg = P.sb("dg", [128, 24, 4, 128], BF16)
    cw = C.gcw.t[:, i * 96:(i + 1) * 96]
    for ch in range(24):
        for k in range(4):
            P.ts("pool", dg.t[:, ch, k, :], C.ident_bf.t[:, :], cw[:, ch * 4 + k:ch * 4 + k + 1], None, ALU.mult,
                 reads=[C.ident_bf, C.gcw], writes=[(dg, (ch, k))])
    PADG = 4
    pre = [P.sb(f"pre{j}", [128, PADG + S], BF16) for j in range(2)]
    for pr in pre:
        P.memset("pool", pr.t[:, 0:PADG], 0.0, writes=[(pr, "pad")])
    wch = [P.sb(f"wch{j}", [128, 8, 512], BF16) for j in range(2)]
    HS = min(2048, S)
    yh = P.sb("yh", [128, HS], BF16)
    ssh = P.sb("ssh", [128, HS], F32)
    sqb = [P.sb(f"sqb{j}", [128, 512], BF16) for j in range(2)]
    ob = [P.sb(f"gob{j}", [128, HS], BF16) for j in range(2)]
    n = 0
    oi = 0
    for cg in range(8):
        w = wch[cg % 2]
        P.dma(w.t[:, :, :], Wg.t[:, cg * 512:(cg + 1) * 512].rearrange("(c p) f -> p c f", p=128), reads=[Wg], writes=[w])
        for ci in range(4):
            ch = cg * 4 + ci
            cs_ = slice(ci * 128, (ci + 1) * 128)
            r0 = (ch % 8) * 128
            if ch < 24:
                pr = pre[ch % 2]
                for s0 in range(0, S, 512):
                    k = n % 2
                    n += 1
                    ps = C.PS[k]
                    for c in range(8):
                        P.mm(ps.t[:, :], w.t[:, c, cs_], xn.t[:, c, s0:s0 + 512], start=(c == 0), stop=(c == 7), reads=[w, xn], writes=[ps])
                    P.copy("act", pr.t[:, PADG + s0:PADG + s0 + 512], ps.t[:, :], reads=[ps], writes=[(pr, s0)])
                dst = (C.qT_g, C.kT_g, C.vT_g)[ch // 8]
                for h0 in range(0, S, HS):
                    o = ob[oi % 2]
                    oi += 1
                    for s0 in range(h0, h0 + HS, 512):
                        k = n % 2
                        n += 1
                        pc = C.PS[2 + k]
                        ls = slice(s0 - h0, s0 - h0 + 512)
                        for kk in range(4):
                            P.mm(pc.t[:, :], dg.t[:, ch, kk, :], pr.t[:, s0 + 1 + kk:s0 + 1 + kk + 512], start=(kk == 0), stop=(kk == 3),
                                 reads=[dg, pr], writes=[pc])
                        if ch >= 16:
                            P.act(o.t[:, ls], pc.t[:, :], AF.Silu, reads=[pc], writes=[(o, s0)])
                        else:
                            sq_ = sqb[k]
                            pn = C.PS[4 + k]
                            P.act(yh.t[:, ls], pc.t[:, :], AF.Silu, reads=[pc], writes=[(yh, s0)])
                            P.tt("dve", sq_.t[:, :], yh.t[:, ls], yh.t[:, ls], ALU.mult, reads=[(yh, s0)], writes=[sq_])
                            P.mm(pn.t[:, :], C.ones_bf.t[:, :], sq_.t[:, :], reads=[sq_, C.ones_bf], writes=[pn])
                            P.copy("dve", ssh.t[:, ls], pn.t[:, :], reads=[pn], writes=[(ssh, s0)])
                    if ch < 16:
                        P.act(ssh.t[:, :], ssh.t[:, :], AF.Ln, reads=[ssh, C.cst], writes=[ssh], bias=C.cst.t[:, 0:1])
                        P.act(ssh.t[:, :], ssh.t[:, :], AF.Exp, reads=[ssh], writes=[ssh], scale=-0.5)
                        if ch < 8:
                            P.stt("dve", o.t[:, :], yh.t[:, :], QSCALE, ssh.t[:, :], ALU.mult, ALU.mult, reads=[yh, ssh], writes=[o])
                        else:
                            P.tt("dve", o.t[:, :], yh.t[:, :], ssh.t[:, :], ALU.mult, reads=[yh, ssh], writes=[o])
                    P.dma(dst.t[r0:r0 + 128, h0:h0 + HS], o.t[:, :], reads=[o], writes=[(dst, (ch % 8, h0))])
            else:
                for h0 in range(0, S, HS):
                    o = ob[oi % 2]
                    oi += 1
                    for s0 in range(h0, h0 + HS, 512):
                        k = n % 2
                        n += 1
                        ps = C.PS[k]
                        for c in range(8):
                            P.mm(ps.t[:, :], w.t[:, c, cs_], xn.t[:, c, s0:s0 + 512], start=(c == 0), stop=(c == 7), reads=[w, xn], writes=[ps])
                        P.act(o.t[:, s0 - h0:s0 - h0 + 512], ps.t[:, :], AF.Silu, reads=[ps], writes=[(o, s0)])
                    P.dma(C.szT_g.t[r0:r0 + 128, h0:h0 + HS], o.t[:, :], reads=[o], writes=[(C.szT_g, (ch % 8, h0))])

    P.barrier()
    P.sb_release(mA)
    if GDN_STOP == 1:
        P.sb_release(m)
        return
    wba = P.sb("wba", [128, 8, 16], BF16)
    P.dma(wba.t[:, :, :], Wg.t[:, 4096:4112].rearrange("(c p) f -> p c f", p=128), reads=[Wg], writes=[wba])
    ba = P.sb("ba", [128, NB, 16], F32)
    x_ = P.sb("gx", [128, NB, 8], F32)
    e_ = P.sb("ge", [128, NB, 8], F32)
    g_ = P.sb("gg", [128, NB, 8], F32)
    gls = P.sb("gls", [128, NB, 8], F32)
    gsm = P.sb("gsm", [128, 16], F32)
    nA = P.sb("nA", [128, 8], F32)
    P.dma(gsm.t[:, :], C.gsm_row_in[0:1, i * 16:(i + 1) * 16].to_broadcast([128, 16]), writes=[gsm])
    P.act(nA.t[:, :], gsm.t[:, 0:8], AF.Exp, reads=[gsm], writes=[nA])
    P.ts("dve", nA.t[:, :], nA.t[:, :], -1.0, None, ALU.mult, reads=[nA], writes=[nA])
    if GDN_STOP == 21:
        P.barrier()
        P.sb_release(m)
        return
    pba = C.PS[6]
    for blk in range(NB):
        for c in range(8):
            P.mm(pba.t[:, blk * 16:(blk + 1) * 16], xn.t[:, c, blk * 128:(blk + 1) * 128], wba.t[:, c, :], start=(c == 0), stop=(c == 7),
                 reads=[wba, xn], writes=[(pba, blk)])
    P.copy("dve", ba.t[:, :, :], pba.t[:, 0:NB * 16].rearrange("p (b f) -> p b f", f=16), reads=[pba], writes=[ba])

    if GDN_STOP == 22:
        P.barrier()
        P.sb_release(m)
        return
    def bc8(ap):
        return ap.unsqueeze(1).to_broadcast([128, NB, 8])

    P.act(sc.beta.t[:, :, :], ba.t[:, :, 0:8], AF.Sigmoid, reads=[ba], writes=[sc.beta])
    P.tt("dve", x_.t[:, :, :], ba.t[:, :, 8:16], bc8(gsm.t[:, 8:16]), ALU.add, reads=[ba, gsm], writes=[x_])
    P.ts("dve", e_.t[:, :, :], x_.t[:, :, :], 30.0, None, ALU.min, reads=[x_], writes=[e_])
    P.act(e_.t[:, :, :], e_.t[:, :, :], AF.Exp, reads=[e_], writes=[e_])
    P.act(e_.t[:, :, :], e_.t[:, :, :], AF.Ln, reads=[e_, C.cst], writes=[e_], bias=C.cst.t[:, 2:3])
    P.tt("dve", e_.t[:, :, :], e_.t[:, :, :], x_.t[:, :, :], ALU.max, reads=[e_, x_], writes=[e_])
    P.tt("dve", g_.t[:, :, :], e_.t[:, :, :], bc8(nA.t[:, :]), ALU.mult, reads=[e_, nA], writes=[g_])
    if GDN_STOP == 23:
        P.barrier()
        P.sb_release(m)
        return
    gflat = g_.t[:, :, :].rearrange("p b h -> p (b h)")
    W8 = NB * 8
    pgc, pl0, pl1 = C.PS[5], C.PS[4], C.PS[3]
    P.mm(pgc.t[:, 0:W8], C.consts.t[:, 1024:1152], gflat, reads=[g_, C.consts], writes=[pgc])
    P.mm(pl0.t[:, 0:W8], C.consts.t[:, 1152:1280], gflat, reads=[g_, C.consts], writes=[pl0])
    P.mm(pl1.t[:, 0:W8], C.consts.t[:, 1280:1408], gflat, reads=[g_, C.consts], writes=[pl1])

    def v3(ap):
        return ap.rearrange("p (b h) -> p b h", h=8)
    if GDN_STOP == 24:
        P.barrier()
        P.sb_release(m)
        return

    P.copy("dve", sc.gc.t[:, :, :], v3(pgc.t[:, 0:W8]), reads=[pgc], writes=[sc.gc])
    P.copy("dve", gls.t[0:64, :, :], v3(pl0.t[0:64, 0:W8]), reads=[pl0], writes=[(gls, 0)])
    P.copy("dve", gls.t[64:128, :, :], v3(pl1.t[64:128, 0:W8]), reads=[pl1], writes=[(gls, 1)])
    if GDN_STOP == 25:
        P.barrier()
        P.sb_release(m)
        return
    P.copy("dve", sc.egl.t[:, 0, :, :], v3(pl0.t[:, 0:W8]), reads=[pl0], writes=[(sc.egl, 0)])
    P.copy("dve", sc.egl.t[:, 1, :, :], v3(pl1.t[:, 0:W8]), reads=[pl1], writes=[(sc.egl, 1)])
    P.act(sc.egl.t[:, :, :, :], sc.egl.t[:, :, :, :], AF.Exp, reads=[sc.egl], writes=[sc.egl])
    if GDN_STOP == 26:
        P.barrier()
        P.sb_release(m)
        return
    P.tt("dve", gls.t[:, :, :], gls.t[:, :, :], sc.gc.t[:, :, :], ALU.subtract, reads=[gls, sc.gc], writes=[gls])
    P.act(sc.kd.t[:, :, :], gls.t[:, :, :], AF.Exp, reads=[gls], writes=[sc.kd])
    P.act(sc.kbg.t[:, :, :], sc.gc.t[:, :, :], AF.Exp, reads=[sc.gc], writes=[sc.kbg])
    P.tt("dve", sc.kbg.t[:, :, :], sc.kbg.t[:, :, :], sc.beta.t[:, :, :], ALU.mult, reads=[sc.kbg, sc.beta], writes=[sc.kbg])

    if GDN_STOP == 2:
        P.barrier()
        P.sb_release(m)
        return
    aT = P.sb("aT", [8, S], F32)
    eT = P.sb("eT", [8, S], F32)
    mk = P.sb("mk", [8, S], F32)
    gcol8 = P.sb("gcol8", [8, 4], F32)
    P.dma(gcol8.t[:, 0:2], C.gsm_col_in[:, i * 2:(i + 1) * 2], writes=[gcol8])
    P.act(gcol8.t[:, 2:3], gcol8.t[:, 0:1], AF.Exp, reads=[gcol8], writes=[gcol8])
    P.ts("dve", gcol8.t[:, 2:3], gcol8.t[:, 2:3], -1.0, None, ALU.mult, reads=[gcol8], writes=[gcol8])
    for s0 in range(0, S, 512):
        k = n % 2
        n += 1
        ps = C.PS[k]
        for c in range(8):
            P.mm(ps.t[0:8, :], wba.t[:, c, 8:16], xn.t[:, c, s0:s0 + 512], start=(c == 0), stop=(c == 7), reads=[wba, xn], writes=[ps])
        P.ts("dve", aT.t[:, s0:s0 + 512], ps.t[0:8, :], gcol8.t[:, 1:2], None, ALU.add, reads=[ps, gcol8], writes=[(aT, s0)])
    P.ts("dve", eT.t[:, :], aT.t[:, :], 30.0, None, ALU.min, reads=[aT], writes=[eT])
    P.act(eT.t[:, :], eT.t[:, :], AF.Exp, reads=[eT], writes=[eT])
    P.act(eT.t[:, :], eT.t[:, :], AF.Ln, reads=[eT, C.cst], writes=[eT], bias=C.cst.t[0:8, 2:3])
    P.tt("dve", eT.t[:, :], eT.t[:, :], aT.t[:, :], ALU.max, reads=[eT, aT], writes=[eT])
    P.ts("dve", eT.t[:, :], eT.t[:, :], gcol8.t[:, 2:3], None, ALU.mult, reads=[eT, gcol8], writes=[eT])
    P.memset("dve", mk.t[:, :], 1.0, writes=[mk])
    P.memset("dve", mk.t[:, 0:S:64], 0.0, writes=[mk])
    P.add("dve", lambda e: e.tensor_tensor_scan(out=aT.t[:, :], data0=mk.t[:, :], data1=eT.t[:, :], initial=0.0, op0=ALU.mult, op1=ALU.add),
          reads=[mk, eT], writes=[aT])
    P.dma(C.gc_d.t[:, :], aT.t[:, :], reads=[aT], writes=[C.gc_d])
    P.barrier()
    P.sb_release(m)


def gdn_core_phase(P, C, S, i, sc):
    m = P.sb_mark()
    NB = S // 128
    LT = 256
    BPL = LT // 128

    def ld(name, dt):
        return [P.sb(f"{name}{j}", [128, 8, LT], dt) for j in range(2)]

    qT, kT, vT, szT, gcB = ld("gq", BF16), ld("gk", BF16), ld("gv", BF16), ld("gsz", BF16), ld("gcB", F32)
    Sst = P.sb("Sst", [128, 8, 128], F32)
    Sbf = P.sb("Sbf", [128, 8, 128], BF16)
    P.memset("pool", Sst.t[:, :, :], 0.0, writes=[Sst])
    P.memset("pool", Sbf.t[:, :, :], 0.0, writes=[Sbf])

    class Wset:
        pass

    W = Wset()
    for nm, dt in (("egc", F32), ("qd", BF16), ("kbg", BF16), ("kd", BF16), ("vb", BF16), ("E", F32), ("En", F32),
                   ("attnT", BF16), ("X", BF16), ("XT", BF16), ("Pa", BF16), ("Pb", BF16), ("PTa", BF16), ("PTb", BF16),
                   ("TTa", BF16), ("TTb", BF16), ("u", F32), ("wT", BF16), ("vn", BF16), ("oT", F32), ("osq", BF16),
                   ("rn", F32), ("ob", BF16)):
        setattr(W, nm, P.sb("w_" + nm, [128, 8, 128], dt))

    maskU = C.consts.t[:, 1024:1152].unsqueeze(1).to_broadcast([128, 8, 128])
    maskLn = C.consts.t[:, 1408:1536].unsqueeze(1).to_broadcast([128, 8, 128])
    identb = C.ident_bf.t[:, :].unsqueeze(1).to_broadcast([128, 8, 128])
    ngcol = C.gng.t[:, i:i + 1]

    def bcf(ap):
        return ap.unsqueeze(2).to_broadcast([128, 8, 128])

    def h4(ps, lo=0, hi=128):
        return ps.t[lo:hi, :].rearrange("p (h d) -> p h d", h=4)

    for b in range(NB):
        lt = b // BPL
        li = lt % 2
        off = (b % BPL) * 128
        bsl = slice(off, off + 128)
        if b % BPL == 0:
            tsl = slice(lt * LT, (lt + 1) * LT)
            for dstl, srcd in ((qT, C.qT_g), (kT, C.kT_g), (vT, C.vT_g), (szT, C.szT_g)):
                P.dma(dstl[li].t[:, :, :], srcd.t[:, tsl].rearrange("(h p) t -> p h t", p=128), reads=[srcd], writes=[dstl[li]])
            P.dma(gcB[li].t[:, :, :], C.gc_d.t[:, tsl].unsqueeze(0).to_broadcast([128, 8, LT]), reads=[C.gc_d], writes=[gcB[li]])
        q_, k_, v_, sz_, gb_ = qT[li], kT[li], vT[li], szT[li], gcB[li]
        P.act(W.egc.t[:, :, :], gb_.t[:, :, bsl], AF.Exp, reads=[gb_], writes=[W.egc])
        P.tt("dve", W.qd.t[:, :, :], q_.t[:, :, bsl], W.egc.t[:, :, :], ALU.mult, reads=[q_, W.egc], writes=[W.qd])
        pk = C.PS[0].t[:, :].bitcast(BF16)
        for h in range(8):
            P.tr(pk[:, h * 128:(h + 1) * 128], k_.t[:, h, bsl], C.ident_bf.t[:, :], reads=[k_, C.ident_bf], writes=[(C.PS[0], h)])
        pk3 = pk.rearrange("p (h d) -> p h d", h=8)
        P.tt("dve", W.kbg.t[:, :, :], pk3, bcf(sc.kbg.t[:, b, :]), ALU.mult, reads=[C.PS[0], sc.kbg], writes=[W.kbg])
        P.tt("dve", W.kd.t[:, :, :], pk3, bcf(sc.kd.t[:, b, :]), ALU.mult, reads=[C.PS[0], sc.kd], writes=[W.kd])
        pv = C.PS[1].t[:, :].bitcast(BF16)
        for h in range(8):
            P.tr(pv[:, h * 128:(h + 1) * 128], v_.t[:, h, bsl], C.ident_bf.t[:, :], reads=[v_, C.ident_bf], writes=[(C.PS[1], h)])
        P.tt("dve", W.vb.t[:, :, :], pv.rearrange("p (h d) -> p h d", h=8), bcf(sc.beta.t[:, b, :]), ALU.mult,
             reads=[C.PS[1], sc.beta], writes=[W.vb])
        for half in range(2):
            pg, pq = C.PS[2 + half], C.PS[4 + half]
            for hh in range(4):
                h = half * 4 + hh
                P.mm(pg.t[:, hh * 128:(hh + 1) * 128], k_.t[:, h, bsl], k_.t[:, h, bsl], reads=[k_], writes=[(pg, hh)])
            for hh in range(4):
                h = half * 4 + hh
                P.mm(pq.t[:, hh * 128:(hh + 1) * 128], k_.t[:, h, bsl], q_.t[:, h, bsl], reads=[k_, q_], writes=[(pq, hh)])
        P.tt("dve", W.E.t[:, :, :], gb_.t[:, :, bsl], bcf(sc.gc.t[:, b, :]), ALU.subtract, reads=[gb_, sc.gc], writes=[W.E])
        P.ts("dve", W.En.t[:, :, :], W.E.t[:, :, :], -1.0, 0.0, ALU.mult, ALU.min, reads=[W.E], writes=[W.En])
        P.ts("dve", W.E.t[:, :, :], W.E.t[:, :, :], 0.0, None, ALU.min, reads=[W.E], writes=[W.E])
        P.act(W.E.t[:, :, :], W.E.t[:, :, :], AF.Exp, reads=[W.E], writes=[W.E])
        P.act(W.En.t[:, :, :], W.En.t[:, :, :], AF.Exp, reads=[W.En], writes=[W.En])
        P.tt("pool", W.E.t[:, :, :], W.E.t[:, :, :], maskU, ALU.mult, reads=[W.E, C.consts], writes=[W.E])
        P.tt("pool", W.En.t[:, :, :], W.En.t[:, :, :], maskLn, ALU.mult, reads=[W.En, C.consts], writes=[W.En])
        P.tt("pool", W.En.t[:, :, :], W.En.t[:, :, :], bcf(sc.beta.t[:, b, :]), ALU.mult, reads=[W.En, sc.beta], writes=[W.En])
        for half in range(2):
            hs = slice(half * 4, half * 4 + 4)
            P.tt("dve", W.attnT.t[:, hs, :], h4(C.PS[4 + half]), W.E.t[:, hs, :], ALU.mult, reads=[C.PS[4 + half], W.E], writes=[(W.attnT, half)])
            P.tt("dve", W.X.t[:, hs, :], h4(C.PS[2 + half]), W.En.t[:, hs, :], ALU.mult, reads=[C.PS[2 + half], W.En], writes=[(W.X, half)])
        px = C.PS[0].t[:, :].bitcast(BF16)
        for h in range(8):
            P.tr(px[:, h * 128:(h + 1) * 128], W.X.t[:, h, :], C.ident_bf.t[:, :], reads=[W.X, C.ident_bf], writes=[(C.PS[0], h)])
        P.copy("act", W.XT.t[:, :, :], px.rearrange("p (h d) -> p h d", h=8), reads=[C.PS[0]], writes=[W.XT])
        P.tt("pool", W.TTa.t[:, :, :], W.XT.t[:, :, :], identb, ALU.add, reads=[W.XT, C.ident_bf], writes=[W.TTa])
        Pc, PTc, TTc = W.X, W.XT, W.TTa
        pbufs = [(W.Pa, W.PTa), (W.Pb, W.PTb)]
        ttbufs = [W.TTb, W.TTa]
        for kk in range(1, 6):
            Pn, PTn = pbufs[kk % 2]
            TTn = ttbufs[(kk - 1) % 2]
            for half in range(2):
                hs = slice(half * 4, half * 4 + 4)
                pp = C.PS[2 + half]
                for hh in range(4):
                    h = half * 4 + hh
                    P.mm(pp.t[:, hh * 128:(hh + 1) * 128], PTc.t[:, h, :], Pc.t[:, h, :], reads=[PTc, Pc], writes=[(pp, hh)])
                P.copy("act", Pn.t[:, hs, :], h4(pp), reads=[pp], writes=[(Pn, half)])
                if kk < 5:
                    ppt = C.PS[4 + half]
                    for hh in range(4):
                        h = half * 4 + hh
                        P.mm(ppt.t[:, hh * 128:(hh + 1) * 128], Pc.t[:, h, :], PTc.t[:, h, :], reads=[PTc, Pc], writes=[(ppt, hh)])
                    P.copy("dve", PTn.t[:, hs, :], h4(ppt), reads=[ppt], writes=[(PTn, half)])
            for half in range(2):
                hs = slice(half * 4, half * 4 + 4)
                pt = C.PS[6 + half]
                for hh in range(4):
                    h = half * 4 + hh
                    P.mm(pt.t[:, hh * 128:(hh + 1) * 128], Pn.t[:, h, :], TTc.t[:, h, :], start=True, stop=False, reads=[Pn, TTc], writes=[(pt, hh)])
                    P.mm(pt.t[:, hh * 128:(hh + 1) * 128], C.ident_bf.t[:, :], TTc.t[:, h, :], start=False, stop=True, reads=[C.ident_bf, TTc], writes=[(pt, hh)])
                P.copy("act" if half == 0 else "dve", TTn.t[:, hs, :], h4(pt), reads=[pt], writes=[(TTn, half)])
            Pc, PTc, TTc = Pn, PTn, TTn
        for half in range(2):
            hs = slice(half * 4, half * 4 + 4)
            pu = C.PS[6 + half]
            for hh in range(4):
                h = half * 4 + hh
                P.mm(pu.t[:, hh * 128:(hh + 1) * 128], TTc.t[:, h, :], W.vb.t[:, h, :], reads=[TTc, W.vb], writes=[(pu, hh)])
            P.copy("act", W.u.t[:, hs, :], h4(pu), reads=[pu], writes=[(W.u, half)])
        for half in range(2):
            hs = slice(half * 4, half * 4 + 4)
            pw = C.PS[6 + half]
            for hh in range(4):
                h = half * 4 + hh
                P.mm(pw.t[:, hh * 128:(hh + 1) * 128], W.kbg.t[:, h, :], TTc.t[:, h, :], reads=[TTc, W.kbg], writes=[(pw, hh)])
            P.copy("dve", W.wT.t[:, hs, :], h4(pw), reads=[pw], writes=[(W.wT, half)])
        for cix in range(2):
            lo, hi = cix * 64, cix * 64 + 64
            cs = slice(lo, hi)
            for half in range(2):
                hs = slice(half * 4, half * 4 + 4)
                pW = C.PS[2 + half]
                for hh in range(4):
                    h = half * 4 + hh
                    P.mm(pW.t[cs, hh * 128:(hh + 1) * 128], W.wT.t[:, h, cs], Sbf.t[:, h, :], reads=[W.wT, Sbf], writes=[(pW, hh)])
                P.tt("dve", W.vn.t[cs, hs, :], W.u.t[cs, hs, :], h4(pW, lo, hi), ALU.subtract, reads=[pW, W.u], writes=[(W.vn, (cix, half))])
            pO = C.PS[1]
            for h in range(8):
                P.mm(pO.t[:, h * 64:(h + 1) * 64], Sbf.t[:, h, :], W.qd.t[:, h, cs], start=True, stop=False, reads=[Sbf, W.qd], writes=[(pO, h)])
                P.mm(pO.t[:, h * 64:(h + 1) * 64], W.vn.t[cs, h, :], W.attnT.t[cs, h, cs], start=False, stop=True, reads=[W.vn, W.attnT], writes=[(pO, h)])
            P.copy("act", W.oT.t[:, :, cs], pO.t[:, :].rearrange("p (h t) -> p h t", h=8), reads=[pO], writes=[(W.oT, cix)])
            for half in range(2):
                pS = C.PS[4 + half]
                for hh in range(4):
                    h = half * 4 + hh
                    P.mm(pS.t[:, hh * 128:(hh + 1) * 128], W.kd.t[cs, h, :], W.vn.t[cs, h, :], reads=[W.kd, W.vn], writes=[(pS, hh)])
            P.tt("dve", Sst.t[:, :, :], Sst.t[:, :, :], bcf(sc.egl.t[:, cix, b, :]), ALU.mult, reads=[Sst, sc.egl], writes=[Sst])
            for half in range(2):
                hs = slice(half * 4, half * 4 + 4)
                P.tt("dve", Sst.t[:, hs, :], Sst.t[:, hs, :], h4(C.PS[4 + half]), ALU.add, reads=[C.PS[4 + half], Sst], writes=[Sst])
            P.copy("act", Sbf.t[:, :, :], Sst.t[:, :, :], reads=[Sst], writes=[Sbf])
        P.tt("dve", W.osq.t[:, :, :], W.oT.t[:, :, :], W.oT.t[:, :, :], ALU.mult, reads=[W.oT], writes=[W.osq])
        for half in range(2):
            hs = slice(half * 4, half * 4 + 4)
            pn = C.PS[6 + half]
            P.mm(pn.t[:, :], C.ones_bf.t[:, :], W.osq.t[:, hs, :], reads=[W.osq, C.ones_bf], writes=[pn])
            P.act(W.rn.t[:, hs, :], h4(pn), AF.Ln, reads=[pn, C.cst], writes=[(W.rn, half)], scale=1.0 / 128, bias=C.cst.t[:, 0:1])
        P.act(W.rn.t[:, :, :], W.rn.t[:, :, :], AF.Exp, reads=[W.rn], writes=[W.rn], scale=-0.5)
        P.stt("dve", W.rn.t[:, :, :], W.oT.t[:, :, :], ngcol, W.rn.t[:, :, :], ALU.mult, ALU.mult, reads=[W.oT, W.rn, C.gng], writes=[W.rn])
        P.tt("dve", W.ob.t[:, :, :], W.rn.t[:, :, :], sz_.t[:, :, bsl], ALU.mult, reads=[W.rn, sz_], writes=[W.ob])
        P.dma(C.catT_d.t[:, b * 128:(b + 1) * 128].rearrange("(h p) t -> p h t", p=128), W.ob.t[:, :, :], reads=[W.ob], writes=[(C.catT_d, ("g", b))])
    P.barrier()
    P.sb_release(m)


def gdn_layer(P, C, S, T, gmix, i):
    m = P.sb_mark()
    sc = gdn_alloc_scalars(P, C, S)
    gdn_proj_phase(P, C, S, gmix, i, sc, C.Wg[i])
    if GDN_STOP == 0 or GDN_STOP > 3:
        gdn_core_phase(P, C, S, i, sc)
    P.sb_release(m)
    mix_out_phase(P, C, S, T, C.Wgo[i])

def build(S=4096, depth=4, T=1024, mixers=True, dbg=None):
    nc = bass.Bass("TRN2", target_bir_lowering=False)
    P = Prog(nc)
    C = Ctx()
    C.S = S
    NE = (depth + 1) // 2
    NO = depth // 2

    def din(name, shape, dt=F32):
        return nc.dram_tensor(name, list(shape), dt, kind="ExternalInput")

    xT = din("xT", [D, S])
    pos = din("pos", [1, S], I32)
    gains_in = din("gains", [128, depth * 24 + 8])
    consts_in = din("consts", [128, 2048])
    ffn_w_in = [[din(f"ffn{j + 1}_w_in_{l}", [D, 2 * FF]) for j in range(2)] for l in range(depth)]
    ffn_w_out = [[din(f"ffn{j + 1}_w_out_{l}", [FF, D]) for j in range(2)] for l in range(depth)]
    if mixers:
        hyb_w_in = [din(f"hyb_w_in_{i}", [D, HYB_IN]) for i in range(NE)]
        hyb_w_out = [din(f"hyb_w_out_{i}", [D, D]) for i in range(NE)]
        hyb_dww_in = din("hyb_dww", [128, NE * 4 * CONVW])
        hvec_in = din("hvec", [128, NE * 12])
        gdn_w_in = [din(f"gdn_w_in_{i}", [D, GDN_IN]) for i in range(NO)]
        gdn_w_out = [din(f"gdn_w_out_{i}", [D, D]) for i in range(NO)]
        gcw_in = din("gcw", [128, max(NO, 1) * 96])
        gng_in = din("gng", [128, max(NO, 1)])
        C.gsm_row_in = din("gsm_row", [1, max(NO, 1) * 16])
        C.gsm_col_in = din("gsm_col", [8, max(NO, 1) * 2])
    C.outT = P.dram("outT", [D, S], F32, kind="ExternalOutput")
    C.hT = P.dram("hT", [D, S], F32)
    C.w_in_bf = [[P.dram(f"w_in_bf_{l}_{j}", [D, 2 * FF], BF16) for j in range(2)] for l in range(depth)]
    C.w_out_bf = [[P.dram(f"w_out_bf_{l}_{j}", [FF, D], BF16) for j in range(2)] for l in range(depth)]
    if mixers:
        C.Wh = [P.dram(f"Wh_{i}", [D, HYB_IN], BF16) for i in range(NE)]
        C.Whp = [P.dram(f"Whp_{i}", [D, 1024], BF16) for i in range(NE)]
        C.Who = [P.dram(f"Who_{i}", [D, D], BF16) for i in range(NE)]
        C.qT_d = P.dram("qT_d", [AW, S], BF16)
        C.kT_d = P.dram("kT_d", [AW, S], BF16)
        C.v_d = P.dram("v_d", [S, AW], BF16)
        C.gluT_d = P.dram("gluT_d", [512, PADL + S], BF16)
        C.catT_d = P.dram("catT_d", [D, S], BF16)
        C.cosT = P.dram("cosT", [128, S], F32)
        C.sinT = P.dram("sinT", [128, S], F32)
        C.den_d = P.dram("den_d", [1, S], F32)
        C.Wg = [P.dram(f"Wg_{i}", [D, GDN_IN], BF16) for i in range(NO)]
        C.Wgo = [P.dram(f"Wgo_{i}", [D, D], BF16) for i in range(NO)]
        C.qT_g = P.dram("qT_g", [D, S], BF16)
        C.kT_g = P.dram("kT_g", [D, S], BF16)
        C.vT_g = P.dram("vT_g", [D, S], BF16)
        C.szT_g = P.dram("szT_g", [D, S], BF16)
        C.gc_d = P.dram("gc_d", [8, S], F32)
    if dbg:
        C.dbg = {k: P.dram("dbg_" + k, shp, dt, kind="ExternalOutput") for k, (shp, dt) in dbg.items()}

    C.PS = [P.ps(f"ps{i}", [128, 512], F32) for i in range(8)]
    C.gains = P.sb("gains", [128, depth * 24 + 8], F32)
    C.consts = P.sb("consts", [128, 2048], F32)
    C.ones_bf = P.sb("ones_bf", [128, 128], BF16)
    C.ident_bf = P.sb("ident_bf", [128, 128], BF16)
    C.mask4 = P.sb("mask4", [128, 512], BF16)
    C.cst = P.sb("cst", [128, 8], F32)
    C.cv32 = [P.sb(f"cv32_{i}", [128, 1024], F32) for i in range(3)]
    C.cv16 = [P.sb(f"cv16_{i}", [128, 1024], BF16) for i in range(2)]
    P.dma(C.gains.t[:, :], gains_in[:, :], writes=[C.gains])
    P.dma(C.consts.t[:, :], consts_in[:, :], writes=[C.consts])
    P.memset("pool", C.cst.t[:, 0:1], EPS, writes=[C.cst])
    P.memset("pool", C.cst.t[:, 1:2], -float(np.pi), writes=[C.cst])
    P.memset("pool", C.cst.t[:, 2:3], 1.0, writes=[C.cst])
    P.memset("pool", C.ones_bf.t[:, :], 1.0, writes=[C.ones_bf])
    P.copy("pool", C.ident_bf.t[:, :], C.consts.t[:, 0:128], reads=[C.consts], writes=[C.ident_bf])
    P.copy("pool", C.mask4.t[:, :], C.consts.t[:, 128:640], reads=[C.consts], writes=[C.mask4])
    if mixers:
        C.hyb_dww = P.sb("hyb_dww", [128, NE * 4 * CONVW], F32)
        C.hvec = P.sb("hvec", [128, NE * 12], F32)
        P.dma(C.hyb_dww.t[:, :], hyb_dww_in[:, :], writes=[C.hyb_dww])
        P.dma(C.hvec.t[:, :], hvec_in[:, :], writes=[C.hvec])
        C.gcw = P.sb("gcw", [128, max(NO, 1) * 96], F32)
        C.gng = P.sb("gng", [128, max(NO, 1)], F32)
        P.dma(C.gcw.t[:, :], gcw_in[:, :], writes=[C.gcw])
        P.dma(C.gng.t[:, :], gng_in[:, :], writes=[C.gng])
    persist = P.sb_mark()

    P.dma(C.hT.t[:, :], xT[:, :], writes=[C.hT])

    C.cvi = 0
    C.cvj = 0

    def jobs_ffn(l, j):
        return cast_jobs(ffn_w_in[l][j], C.w_in_bf[l][j], D, 2 * FF) + cast_jobs(ffn_w_out[l][j], C.w_out_bf[l][j], FF, D)

    def jobs_mixer(l):
        if not mixers:
            return []
        i = l // 2
        if l % 2 == 0:
            return cast_jobs(hyb_w_in[i], C.Wh[i], D, HYB_IN) + perm_jobs(hyb_w_in[i], C.Whp[i]) + cast_jobs(hyb_w_out[i], C.Who[i], D, D)
        return cast_jobs(gdn_w_in[i], C.Wg[i], D, GDN_IN) + cast_jobs(gdn_w_out[i], C.Wgo[i], D, D)

    run_cast_jobs(P, C, jobs_ffn(0, 0))

    if mixers and NE > 0:
        zt = C.cv16[0]
        P.memset("pool", zt.t[:, 0:PADL], 0.0, writes=[zt])
        for cc in range(4):
            P.dma(C.gluT_d.t[cc * 128:(cc + 1) * 128, 0:PADL], zt.t[:, 0:PADL], reads=[zt], writes=[(C.gluT_d, ("pad", cc))])
        P.barrier()
        rope_tables(P, C, S, pos)

    def alloc_ffn():
        C.htile = P.sb("htile", [128, 8, T], F32)
        C.xn = P.sb("xn", [128, 8, T], BF16)
        C.gT = P.sb("gT", [128, NFC, T], BF16)
        C.sq = P.sb("sq", [128, 8, 512], BF16)
        C.rstd = P.sb("rstd", [128, 512], F32)
        C.sg = [P.sb(f"sg{i}", [128, 512], F32) for i in range(2)]
        C.wg = [P.sb(f"wg{i}", [128, 8, 512], BF16) for i in range(2)]
        C.wu = [P.sb(f"wu{i}", [128, 8, 512], BF16) for i in range(2)]
        C.wo = [P.sb(f"wo{i}", [128, NFC, 256], BF16) for i in range(2)]

    for l in range(depth):
        P.barrier()
        P.sb_release(persist)
        alloc_ffn()
        ffn_phase(P, C, S, T, C.gains.t[:, l * 24:l * 24 + 8], C.w_in_bf[l][0], C.w_out_bf[l][0],
                  jobs=jobs_mixer(l) + jobs_ffn(l, 1))
        if mixers:
            P.barrier()
            P.sb_release(persist)
            gmix = C.gains.t[:, l * 24 + 8:l * 24 + 16]
            if l % 2 == 0:
                i = l // 2
                hyb_proj_phase(P, C, S, T, gmix, C.Wh[i], C.Whp[i], None, l)
                hyb_conv_phase(P, C, S, C.hvec.t[:, i * 12:(i + 1) * 12], C.hyb_dww.t[:, i * 4 * CONVW:(i + 1) * 4 * CONVW])
                hyb_attn_phase(P, C, S)
                mix_out_phase(P, C, S, T, C.Who[i])
            else:
                gdn_layer(P, C, S, T, gmix, l // 2)
            P.sb_release(persist)
            alloc_ffn()
        ffn_phase(P, C, S, T, C.gains.t[:, l * 24 + 16:l * 24 + 24], C.w_in_bf[l][1], C.w_out_bf[l][1],
                  jobs=(jobs_ffn(l + 1, 0) if l + 1 < depth else None))

    P.barrier()
    P.sb_release(persist)
    C.htile = P.sb("htile", [128, 8, T], F32)
    C.fo = P.sb("fo", [128, 8, T], F32)
    C.sq = P.sb("sq", [128, 8, 512], BF16)
    C.rstd = P.sb("rstd", [128, 512], F32)
    final_phase(P, C, S, T, C.gains.t[:, depth * 24:depth * 24 + 8], C.outT)
    P.emit()
    return nc, P


def make_consts():
    f32 = np.float32
    c = np.zeros((128, 2048), f32)
    c[:, 0:128] = np.eye(128, dtype=f32)
    k = np.arange(128)[:, None]
    q = np.arange(128)[None, :]
    cur = (q >= k).astype(f32)
    prev = (q <= k).astype(f32)
    c[:, 128:640] = np.concatenate([cur, prev, cur, prev], axis=1)
    inv = np.power(f32(500000.0), -np.arange(0, 16, 2, dtype=f32) / f32(16)).astype(f32)
    for p in range(128):
        i = p % 64
        c[p, 640] = inv[i % 8] if i < 16 else 0.0
    same = (k // 64) == (q // 64)
    c[:, 1024:1152] = ((q >= k) & same).astype(f32)
    c[:, 1152:1280] = (k < 64).astype(f32) * np.ones((1, 128), f32)
    c[:, 1280:1408] = (k >= 64).astype(f32) * np.ones((1, 128), f32)
    c[:, 1408:1536] = -((k > q) & same).astype(f32)
    return c


def host_inputs(inputs, S=4096, depth=4, mixers=True):
    f32 = np.float32
    NE = (depth + 1) // 2
    x = np.asarray(inputs["x"], dtype=f32)
    B = x.shape[0]
    gains = np.zeros((128, depth * 24 + 8), f32)
    for l in range(depth):
        for j, nm in enumerate(("ffn1_norm", "mix_norm", "ffn2_norm")):
            gains[:, l * 24 + j * 8:l * 24 + j * 8 + 8] = np.asarray(inputs[nm][l], f32).reshape(8, 128).T
    gains[:, depth * 24:] = np.asarray(inputs["final_norm"], f32).reshape(8, 128).T
    shared = {"gains": gains, "consts": make_consts()}
    for l in range(depth):
        for j in range(2):
            shared[f"ffn{j + 1}_w_in_{l}"] = np.ascontiguousarray(inputs[f"ffn{j + 1}_w_in"][l], dtype=f32)
            shared[f"ffn{j + 1}_w_out_{l}"] = np.ascontiguousarray(inputs[f"ffn{j + 1}_w_out"][l], dtype=f32)
    if mixers:
        dww = np.zeros((128, NE * 4 * CONVW), f32)
        hvec = np.zeros((128, NE * 12), f32)
        for i in range(NE):
            shared[f"hyb_w_in_{i}"] = np.ascontiguousarray(inputs["hyb_w_in"][i], dtype=f32)
            shared[f"hyb_w_out_{i}"] = np.ascontiguousarray(inputs["hyb_w_out"][i], dtype=f32)
            w = np.asarray(inputs["hyb_dw_w"][i], f32)
            for cc in range(4):
                dww[:, (i * 4 + cc) * CONVW:(i * 4 + cc + 1) * CONVW] = w[:, cc * 128:(cc + 1) * 128].T
            for j, nm in enumerate(("hyb_dw_b", "hyb_ln_g", "hyb_ln_b")):
                hvec[:, i * 12 + j * 4:i * 12 + j * 4 + 4] = np.asarray(inputs[nm][i], f32).reshape(4, 128).T
        shared["hyb_dww"] = dww
        shared["hvec"] = hvec
        NO = depth // 2
        gcw = np.zeros((128, max(NO, 1) * 96), f32)
        gng = np.zeros((128, max(NO, 1)), f32)
        grow = np.zeros((1, max(NO, 1) * 16), f32)
        gcolm = np.zeros((8, max(NO, 1) * 2), f32)
        for i in range(NO):
            shared[f"gdn_w_in_{i}"] = np.ascontiguousarray(inputs["gdn_w_in"][i], dtype=f32)
            shared[f"gdn_w_out_{i}"] = np.ascontiguousarray(inputs["gdn_w_out"][i], dtype=f32)
            cw = np.asarray(inputs["gdn_conv_w"][i], f32)
            for ch in range(24):
                gcw[:, i * 96 + ch * 4:i * 96 + ch * 4 + 4] = cw[:, ch * 128:(ch + 1) * 128].T
            gng[:, i] = np.asarray(inputs["gdn_norm_g"][i], f32)
            grow[0, i * 16:i * 16 + 8] = np.asarray(inputs["gdn_A_log"][i], f32)
            grow[0, i * 16 + 8:i * 16 + 16] = np.asarray(inputs["gdn_dt_bias"][i], f32)
            gcolm[:, i * 2] = np.asarray(inputs["gdn_A_log"][i], f32)
            gcolm[:, i * 2 + 1] = np.asarray(inputs["gdn_dt_bias"][i], f32)
        shared["gcw"] = gcw
        shared["gng"] = gng
        shared["gsm_row"] = grow
        shared["gsm_col"] = gcolm
    maps = []
    for b in range(B):
        m = dict(shared)
        m["xT"] = np.ascontiguousarray(x[b, :S].T)
        m["pos"] = np.ascontiguousarray(np.asarray(inputs["positions"])[b:b + 1, :S].astype(np.int32))
        maps.append(m)
    return maps


_CACHE = {}


def kernel(**inputs):
    S, depth = 4096, 4
    maps = host_inputs(inputs, S, depth)
    nc, P = build(S, depth)
    res = run_bass_kernel_spmd(nc, maps, core_ids=list(range(NCORES)))
    outs = [np.asarray(r["outT"]) for r in res.results]
    out = np.stack([o.T for o in outs], axis=0)
    return np.ascontiguousarray(out.astype(np.float32))
```

```python
from contextlib import ExitStack
import concourse.bass as bass
import concourse.mybir as mybir

F32 = mybir.dt.float32
BF16 = mybir.dt.bfloat16
I32 = mybir.dt.int32
AF = mybir.ActivationFunctionType
ALU = mybir.AluOpType
AX = mybir.AxisListType

import os as _os0
ANNOTATE = bool(int(_os0.environ.get('KANNOT', '0')))
ENGS = ("pe", "act", "dve", "pool", "sp")
EP = 30000
NSLOT = 8
DMA_EP = 1800


class Op:
    __slots__ = ("eng", "idx", "fn", "deps", "signal", "sig", "dma", "slot", "epoch", "val", "phase")

    def __init__(self, eng, fn, dma):
        self.eng = eng
        self.fn = fn
        self.dma = dma
        self.deps = []
        self.signal = False
        self.sig = 0


class Buf:
    def __init__(self, name, t):
        self.name = name
        self.t = t
        self.e = {}

    def __getitem__(self, k):
        return self.t[k]

    def _ents(self, key):
        if key is None:
            return list(self.e.values())
        return [self.e[k] for k in (key, None) if k in self.e]

    def read(self, op, key):
        deps = [en[0] for en in self._ents(key) if en[0] is not None]
        en = self.e.setdefault(key, [None, {}, []])
        if op.dma:
            en[2].append(op)
        else:
            en[1][op.eng] = op
        return deps

    def write(self, op, key):
        deps = []
        for en in self._ents(key):
            if en[0] is not None:
                deps.append(en[0])
            deps.extend(en[1].values())
            deps.extend(en[2])
        if key is None:
            self.e = {None: [op, {}, []]}
        else:
            self.e[key] = [op, {}, []]
        return deps


class Prog:
    def __init__(self, nc):
        self.nc = nc
        self.ops = {e: [] for e in ENGS}
        self.stack = ExitStack()
        self.nsem = 0
        self.dq = {e: {"n": 0, "last": [None] * NSLOT, "uses": [0] * NSLOT, "epoch": [0] * NSLOT} for e in ENGS}
        self._esem = {}
        self._dsem = {}
        self.ntile = 0

    SB_LO = 16512
    SB_HI = 229376

    def sb(self, name, shape, dtype):
        self.ntile += 1
        isz = 2 if dtype == BF16 else 4
        n = 1
        for d in shape[1:]:
            n *= d
        nbytes = (n * isz + 63) // 64 * 64
        off = getattr(self, "sb_ptr", self.SB_LO)
        assert off + nbytes <= self.SB_HI, f"SBUF overflow allocating {name}: {off}+{nbytes}"
        self.sb_ptr = off + nbytes
        self.sb_peak = max(getattr(self, "sb_peak", 0), self.sb_ptr)
        t = self.nc.alloc_sbuf_tensor_at(f"{name}_{self.ntile}", list(shape), dtype, offset=off)
        return Buf(name, t)

    def sb_mark(self):
        return getattr(self, "sb_ptr", self.SB_LO)

    def sb_release(self, mark):
        self.sb_ptr = mark

    def barrier(self):
        lasts = []
        for e in ENGS:
            for op in reversed(self.ops[e]):
                if not op.dma and op.fn is not None:
                    lasts.append(op)
                    break
            q = self.dq[e]
            for s_ in range(NSLOT):
                if q["last"][s_] is not None:
                    lasts.append(q["last"][s_])
        for e in ENGS:
            op = Op(e, None, False)
            op.phase = None
            op.idx = len(self.ops[e])
            for d in lasts:
                if (not d.dma) and d.eng == e:
                    continue
                op.deps.append(d)
                if not d.dma:
                    d.signal = True
            self.ops[e].append(op)

    def ps(self, name, shape, dtype=F32):
        self.ntile += 1
        t = self.stack.enter_context(self.nc.psum_tensor(f"{name}_{self.ntile}", list(shape), dtype))
        return Buf(name, t)

    def dram(self, name, shape, dtype, kind="Internal"):
        t = self.nc.dram_tensor(name, list(shape), dtype, kind=kind)
        return Buf(name, t)

    def _sem(self, name):
        self.nsem += 1
        return self.stack.enter_context(self.nc.semaphore(f"{name}_{self.nsem}"))

    def add(self, eng, fn, reads=(), writes=(), dma=False):
        op = Op(eng, fn, dma)
        op.idx = len(self.ops[eng])
        op.phase = getattr(self, "phase", None)
        deps = []
        for r in reads:
            b, k = r if isinstance(r, tuple) else (r, None)
            if b is not None:
                deps.extend(b.read(op, k))
        for w in writes:
            b, k = w if isinstance(w, tuple) else (w, None)
            if b is not None:
                deps.extend(b.write(op, k))
        if dma:
            q = self.dq[eng]
            s = q["n"] % NSLOT
            q["n"] += 1
            if q["last"][s] is not None:
                deps.append(q["last"][s])
            if q["uses"][s] >= DMA_EP:
                q["uses"][s] = 0
                q["epoch"][s] += 1
            q["uses"][s] += 1
            op.slot = s
            op.epoch = q["epoch"][s]
            op.val = 16 * q["uses"][s]
            q["last"][s] = op
        seen = set()
        for d in deps:
            if d is op or id(d) in seen:
                continue
            seen.add(id(d))
            if (not d.dma) and d.eng == "pe" and eng == "pe" and not dma:
                continue
            op.deps.append(d)
            if not d.dma:
                d.signal = True
        self.ops[eng].append(op)
        return op

    def dma(self, out, in_, reads=(), writes=(), eng="sp", **kw):
        return self.add(eng, lambda h: h.dma_start(out=out, in_=in_, **kw), reads, writes, dma=True)

    def mm(self, out, lhsT, rhs, start=True, stop=True, reads=(), writes=(), **kw):
        return self.add("pe", lambda h: h.matmul(out, lhsT, rhs, start=start, stop=stop, **kw), reads, writes)

    def tr(self, out, in_, ident, reads=(), writes=()):
        return self.add("pe", lambda h: h.transpose(out, in_, ident), reads, writes)

    def act(self, out, in_, func, reads=(), writes=(), eng="act", **kw):
        return self.add(eng, lambda h: h.activation(out=out, in_=in_, func=func, **kw), reads, writes)

    def tt(self, eng, out, in0, in1, op, reads=(), writes=()):
        return self.add(eng, lambda h: h.tensor_tensor(out=out, in0=in0, in1=in1, op=op), reads, writes)

    def ts(self, eng, out, in0, s1, s2, op0, op1=None, reads=(), writes=()):
        if op1 is None:
            return self.add(eng, lambda h: h.tensor_scalar(out=out, in0=in0, scalar1=s1, scalar2=None, op0=op0), reads, writes)
        return self.add(eng, lambda h: h.tensor_scalar(out=out, in0=in0, scalar1=s1, scalar2=s2, op0=op0, op1=op1), reads, writes)

    def stt(self, eng, out, in0, scalar, in1, op0, op1, reads=(), writes=()):
        return self.add(eng, lambda h: h.scalar_tensor_tensor(out=out, in0=in0, scalar=scalar, in1=in1, op0=op0, op1=op1), reads, writes)

    def copy(self, eng, out, in_, reads=(), writes=()):
        if eng == "act":
            return self.add(eng, lambda h: h.copy(out=out, in_=in_), reads, writes)
        return self.add(eng, lambda h: h.tensor_copy(out=out, in_=in_), reads, writes)

    def memset(self, eng, ap, val, writes=()):
        return self.add(eng, lambda h: h.memset(ap, val), (), writes)

    def esem(self, e, ep):
        k = (e, ep)
        if k not in self._esem:
            self._esem[k] = self._sem(f"e_{e}{ep}")
        return self._esem[k]

    def dsem(self, e, slot, ep):
        k = (e, slot, ep)
        if k not in self._dsem:
            self._dsem[k] = self._sem(f"d_{e}{slot}_{ep}")
        return self._dsem[k]

    def emit(self):
        nc = self.nc
        for e in ENGS:
            n = 0
            for op in self.ops[e]:
                if (not op.dma) and op.signal:
                    n += 1
                    op.sig = n
        for e in ENGS:
            for op in self.ops[e]:
                if op.dma:
                    self.dsem(e, op.slot, op.epoch)
                elif op.signal:
                    self.esem(e, (op.sig - 1) // EP)
        final = []
        for e in ENGS:
            q = self.dq[e]
            for s in range(NSLOT):
                if q["last"][s] is not None:
                    o = q["last"][s]
                    final.append((self.dsem(e, o.slot, o.epoch), o.val))
        P = self

        def run(e, h):
            weng = {}
            wdma = {}
            for op in P.ops[e]:
                for d in op.deps:
                    if d.dma:
                        k = (d.eng, d.slot, d.epoch)
                        if wdma.get(k, 0) < d.val:
                            h.wait_ge(P.dsem(*k), d.val)
                            wdma[k] = d.val
                    else:
                        if weng.get(d.eng, 0) < d.sig:
                            ep, v = divmod(d.sig - 1, EP)
                            h.wait_ge(P.esem(d.eng, ep), v + 1)
                            weng[d.eng] = d.sig
                if op.fn is None:
                    continue
                ins = op.fn(h)
                if ANNOTATE and op.phase:
                    ins.annotate(op.phase)
                if op.dma:
                    ins.then_inc(P.dsem(e, op.slot, op.epoch), 16)
                elif op.signal:
                    ins.then_inc(P.esem(e, (op.sig - 1) // EP), 1)
            if e == "sp":
                for sem, v in final:
                    h.wait_ge(sem, v)

        with nc.Block() as block:
            @block.tensor
            def _(h):
                run("pe", h)

            @block.scalar
            def _(h):
                run("act", h)

            @block.vector
            def _(h):
                run("dve", h)

            @block.gpsimd
            def _(h):
                run("pool", h)

            @block.sync
            def _(h):
                run("sp", h)

    def close(self):
        self.stack.close()

import numpy as np
from concourse.bass_utils import run_bass_kernel_spmd

D = 1024
FF = 2816
NFC = FF // 128
EPS = 1e-6
NCORES = 8


class Ctx:
    pass


def cast_jobs(src_ap, dst, R, Cc):
    CB = 1024
    jobs = []
    for r0 in range(0, R, 128):
        rr = min(128, R - r0)
        for c0 in range(0, Cc, CB):
            cc = min(CB, Cc - c0)
            jobs.append(("plain", src_ap, dst, r0, rr, c0, cc))
    return jobs


def perm_jobs(src_ap, dst):
    return [("perm", src_ap, dst, r0, 128, 0, 1024) for r0 in range(0, D, 128)]


def run_cast_jobs(P, C, jobs, eng="pool"):
    LA = len(C.cv32) - 1
    n = len(jobs)
    for i in range(n + LA):
        if i < n:
            kind, src_ap, dst, r0, rr, c0, cc = jobs[i]
            a = C.cv32[C.cvi % len(C.cv32)]
            jobs[i] = jobs[i] + (a,)
            C.cvi += 1
            P.dma(a.t[0:rr, 0:cc], src_ap[r0:r0 + rr, c0:c0 + cc], writes=[a], eng=eng)
        j = i - LA
        if j >= 0:
            kind, src_ap, dst, r0, rr, c0, cc, a = jobs[j]
            b = C.cv16[C.cvj % len(C.cv16)]
            C.cvj += 1
            if kind == "plain":
                P.copy("pool", b.t[0:rr, 0:cc], a.t[0:rr, 0:cc], reads=[a], writes=[b])
            else:
                av = a.t[:, 0:1024].rearrange("p (h e) -> p h e", e=64)
                bv = b.t[:, 0:1024].rearrange("p (h e) -> p h e", e=64)
                P.memset("pool", b.t[:, 0:1024], 0.0, writes=[b])
                P.ts("pool", bv[:, :, 0:8], av[:, :, 8:16], -1.0, None, ALU.mult, reads=[a], writes=[b])
                P.copy("pool", bv[:, :, 8:16], av[:, :, 0:8], reads=[a], writes=[b])
            P.dma(dst.t[r0:r0 + rr, c0:c0 + cc], b.t[0:rr, 0:cc], reads=[b], writes=[(dst, ("r", r0, c0))], eng=eng)


def rmsnorm_tile(P, C, src, gcol, dst, ntok, srckey=None, doff=0):
    for s0 in range(0, ntok, 512):
        n = min(512, ntok - s0)
        sl = slice(s0, s0 + n)
        ps = C.PS[7]
        for c in range(8):
            P.act(C.sq.t[:, c, 0:n], src.t[:, c, sl], AF.Square, reads=[(src, srckey)], writes=[(C.sq, c)])
        for c in range(8):
            P.mm(ps.t[:, 0:n], C.ones_bf.t[:, :], C.sq.t[:, c, 0:n], start=(c == 0), stop=(c == 7),
                 reads=[(C.sq, c), C.ones_bf], writes=[ps])
        P.act(C.rstd.t[:, 0:n], ps.t[:, 0:n], AF.Ln, reads=[ps, C.cst], writes=[C.rstd], scale=1.0 / D, bias=C.cst.t[:, 0:1])
        P.act(C.rstd.t[:, 0:n], C.rstd.t[:, 0:n], AF.Exp, reads=[C.rstd], writes=[C.rstd], scale=-0.5)
        for c in range(8):
            P.stt("dve", dst.t[:, c, doff + s0:doff + s0 + n], src.t[:, c, sl], gcol[:, c:c + 1], C.rstd.t[:, 0:n], ALU.mult, ALU.mult,
                  reads=[(src, srckey), C.rstd, C.gains], writes=[(dst, (c, doff + s0))])


def ffn_phase(P, C, S, T, gcol, w_in, w_out, jobs=None):
    P.phase = "ffn_phase"
    hview = C.hT.t.rearrange("(c p) t -> p c t", p=128)
    WB = 256
    nwb = FF // WB
    wi = 0
    wo_i = 0
    if jobs:
        run_cast_jobs(P, C, jobs)
    for tt in range(S // T):
        tsl = slice(tt * T, (tt + 1) * T)
        htile = C.htiles[tt % 2]
        P.dma(htile.t[:, :, :], hview[:, :, tsl], reads=[(C.hT, tt)], writes=[htile])
        rmsnorm_tile(P, C, htile, gcol, C.xn, T)
        for wb in range(nwb):
            c0 = wb * WB
            cw = min(WB, FF - c0)
            wg = C.wg[wi % 2]
            wu = C.wu[wi % 2]
            wi += 1
            P.dma(wg.t[:, :, 0:cw], w_in.t[:, c0:c0 + cw].rearrange("(c p) f -> p c f", p=128), reads=[w_in], writes=[wg])
            P.dma(wu.t[:, :, 0:cw], w_in.t[:, FF + c0:FF + c0 + cw].rearrange("(c p) f -> p c f", p=128), reads=[w_in], writes=[wu])
            for fi in range(cw // 128):
                f = c0 // 128 + fi
                fs = slice(fi * 128, (fi + 1) * 128)
                for s0 in range(0, T, 512):
                    sl = slice(s0, s0 + 512)
                    k = (f * (T // 512) + s0 // 512) % 2
                    pg = C.PS[k]
                    pu = C.PS[2 + k]
                    sg = C.sg[k]
                    for c in range(8):
                        P.mm(pg.t[:, :], wg.t[:, c, fs], C.xn.t[:, c, sl], start=(c == 0), stop=(c == 7),
                             reads=[wg, C.xn], writes=[pg])
                    for c in range(8):
                        P.mm(pu.t[:, :], wu.t[:, c, fs], C.xn.t[:, c, sl], start=(c == 0), stop=(c == 7),
                             reads=[wu, C.xn], writes=[pu])
                    P.act(sg.t[:, :], pg.t[:, :], AF.Silu, reads=[pg], writes=[sg])
                    P.tt("dve", C.gT.t[:, f, sl], sg.t[:, :], pu.t[:, :], ALU.mult, reads=[sg, pu], writes=[(C.gT, (f, s0))])
        OB = 256
        for ob in range(D // OB):
            wo = C.wo[wo_i % 2]
            wo_i += 1
            P.dma(wo.t[:, :, :], w_out.t[:, ob * OB:(ob + 1) * OB].rearrange("(f p) d -> p f d", p=128), reads=[w_out], writes=[wo])
            for di in range(OB // 128):
                dc = ob * (OB // 128) + di
                for s0 in range(0, T, 512):
                    sl = slice(s0, s0 + 512)
                    k = (dc * (T // 512) + s0 // 512) % 2
                    py = C.PS[4 + k]
                    for f in range(NFC):
                        P.mm(py.t[:, :], wo.t[:, f, di * 128:(di + 1) * 128], C.gT.t[:, f, sl], start=(f == 0), stop=(f == NFC - 1),
                             reads=[wo, C.gT], writes=[py])
                    P.stt("dve", htile.t[:, dc, sl], py.t[:, :], 0.5, htile.t[:, dc, sl], ALU.mult, ALU.add,
                          reads=[py, (htile, (dc, s0))], writes=[(htile, (dc, s0))])
        P.dma(hview[:, :, tsl], htile.t[:, :, :], reads=[htile], writes=[(C.hT, tt)])


def final_phase(P, C, S, T, gcol, outT):
    P.phase = "final_phase"
    hview = C.hT.t.rearrange("(c p) t -> p c t", p=128)
    oview = outT.t.rearrange("(c p) t -> p c t", p=128)
    for tt in range(S // T):
        tsl = slice(tt * T, (tt + 1) * T)
        P.dma(C.htile.t[:, :, :], hview[:, :, tsl], reads=[(C.hT, tt)], writes=[C.htile])
        rmsnorm_tile(P, C, C.htile, gcol, C.fo, T)
        P.dma(oview[:, :, tsl], C.fo.t[:, :, :], reads=[C.fo], writes=[(outT, tt)])


AW = 512
HYB_IN = 2560
CONVW = 31
PADL = 32
DIL = ((128, 1), (512, 4), (2048, 16))


def rope_tables(P, C, S, pos_ap):
    P.phase = "rope_tables"
    m = P.sb_mark()
    posi = P.sb("posi", [128, S], I32)
    ang = P.sb("ang", [128, S], F32)
    tmp = P.sb("tmp", [128, S], F32)
    res = P.sb("res", [128, S], F32)
    P.dma(posi.t[:, :], pos_ap[0:1, :].to_broadcast([128, S]), writes=[posi])
    P.copy("dve", ang.t[:, :], posi.t[:, :], reads=[posi], writes=[ang])
    P.ts("dve", ang.t[:, :], ang.t[:, :], C.consts.t[:, 640:641], None, ALU.mult, reads=[ang, C.consts], writes=[ang])
    PI = float(np.pi)
    ki = posi
    for shift, dst in ((0.5 * PI, C.cosT), (0.0, C.sinT)):
        P.ts("dve", tmp.t[:, :], ang.t[:, :], shift, None, ALU.add, reads=[ang], writes=[tmp])
        P.ts("dve", ki.t[:, :], tmp.t[:, :], 1.0 / (2 * PI), 0.5, ALU.mult, ALU.add, reads=[tmp], writes=[ki])
        P.copy("dve", res.t[:, :], ki.t[:, :], reads=[ki], writes=[res])
        P.stt("dve", tmp.t[:, :], res.t[:, :], -2 * PI, tmp.t[:, :], ALU.mult, ALU.add, reads=[res, tmp], writes=[tmp])
        P.ts("dve", res.t[:, :], tmp.t[:, :], -PI, 2 * PI, ALU.is_lt, ALU.mult, reads=[tmp], writes=[res])
        P.tt("dve", tmp.t[:, :], tmp.t[:, :], res.t[:, :], ALU.add, reads=[tmp, res], writes=[tmp])
        P.ts("dve", res.t[:, :], tmp.t[:, :], PI, -2 * PI, ALU.is_gt, ALU.mult, reads=[tmp], writes=[res])
        P.tt("dve", tmp.t[:, :], tmp.t[:, :], res.t[:, :], ALU.add, reads=[tmp, res], writes=[tmp])
        P.ts("dve", tmp.t[:, :], tmp.t[:, :], PI, -PI, ALU.min, ALU.max, reads=[tmp], writes=[tmp])
        P.act(res.t[:, :], tmp.t[:, :], AF.Sin, reads=[tmp], writes=[res])
        P.dma(dst.t[:, :], res.t[:, :], reads=[res], writes=[dst])
    P.barrier()
    P.sb_release(m)


def hyb_proj_phase(P, C, S, T, gcol, Wh, Whp, hv, l):
    P.phase = "hyb_proj_phase"
    m = P.sb_mark()
    C.htile = P.sb("htile", [128, 8, T], F32)
    C.xn = P.sb("xn", [128, 8, T], BF16)
    C.sq = P.sb("sq", [128, 8, 512], BF16)
    C.rstd = P.sb("rstd", [128, 512], F32)
    cos = P.sb("cos", [128, T], F32)
    sin = P.sb("sin", [128, T], F32)
    wa = [P.sb(f"wa{i}", [128, 8, 512], BF16) for i in range(2)]
    wb = [P.sb(f"wb{i}", [128, 8, 512], BF16) for i in range(2)]
    wv = P.sb("wv", [128, 8, 512], BF16)
    wn = 0
    t1 = [P.sb(f"t1{i}", [128, 512], F32) for i in range(2)]
    t2 = [P.sb(f"t2{i}", [128, 512], F32) for i in range(2)]
    ob = [P.sb(f"ob{i}", [128, T], BF16) for i in range(2)]
    vt = P.sb("vt", [128, T // 128, 512], BF16)
    hview = C.hT.t.rearrange("(c p) t -> p c t", p=128)
    n = 0
    oi = 0
    for tt in range(S // T):
        tsl = slice(tt * T, (tt + 1) * T)
        P.dma(C.htile.t[:, :, :], hview[:, :, tsl], reads=[(C.hT, tt)], writes=[C.htile])
        rmsnorm_tile(P, C, C.htile, gcol, C.xn, T)
        P.dma(cos.t[:, :], C.cosT.t[:, tsl], reads=[C.cosT], writes=[cos])
        P.dma(sin.t[:, :], C.sinT.t[:, tsl], reads=[C.sinT], writes=[sin])
        for qk in range(2):
            w1 = wa[wn % 2]
            w2 = wb[wn % 2]
            wn += 1
            P.dma(w1.t[:, :, :], Wh.t[:, qk * 512:(qk + 1) * 512].rearrange("(c p) f -> p c f", p=128), reads=[Wh], writes=[w1])
            P.dma(w2.t[:, :, :], Whp.t[:, qk * 512:(qk + 1) * 512].rearrange("(c p) f -> p c f", p=128), reads=[Whp], writes=[w2])
            for ch in range(4):
                cs_ = slice(ch * 128, (ch + 1) * 128)
                o = ob[oi % 2]
                oi += 1
                for s0 in range(0, T, 512):
                    sl = slice(s0, s0 + 512)
                    k = n % 2
                    n += 1
                    pa, pb = C.PS[k], C.PS[2 + k]
                    for c in range(8):
                        P.mm(pa.t[:, :], w1.t[:, c, cs_], C.xn.t[:, c, sl], start=(c == 0), stop=(c == 7), reads=[w1, C.xn], writes=[pa])
                    for c in range(8):
                        P.mm(pb.t[:, :], w2.t[:, c, cs_], C.xn.t[:, c, sl], start=(c == 0), stop=(c == 7), reads=[w2, C.xn], writes=[pb])
                    P.tt("dve", t1[k].t[:, :], pa.t[:, :], cos.t[:, sl], ALU.mult, reads=[pa, cos], writes=[t1[k]])
                    P.tt("dve", t2[k].t[:, :], pb.t[:, :], sin.t[:, sl], ALU.mult, reads=[pb, sin], writes=[t2[k]])
                    P.tt("pool", o.t[:, sl], t1[k].t[:, :], t2[k].t[:, :], ALU.add, reads=[t1[k], t2[k]], writes=[(o, s0)])
                dst = C.qT_d if qk == 0 else C.kT_d
                P.dma(dst.t[ch * 128:(ch + 1) * 128, tsl], o.t[:, :], reads=[o], writes=[(dst, (ch, tt))])
        P.dma(wv.t[:, :, :], Wh.t[:, 1024:1536].rearrange("(c p) f -> p c f", p=128), reads=[Wh], writes=[wv])
        for blk in range(T // 128):
            pv = C.PS[4 + blk % 2]
            bs = slice(blk * 128, (blk + 1) * 128)
            for c in range(8):
                P.mm(pv.t[:, :], C.xn.t[:, c, bs], wv.t[:, c, :], start=(c == 0), stop=(c == 7), reads=[wv, C.xn], writes=[pv])
            P.copy("act", vt.t[:, blk, :], pv.t[:, :], reads=[pv], writes=[(vt, blk)])
        P.dma(C.v_d.t[tsl, :].rearrange("(b p) f -> p b f", p=128), vt.t[:, :, :], reads=[vt], writes=[(C.v_d, tt)])
        w1 = wa[wn % 2]
        w2 = wb[wn % 2]
        wn += 1
        P.dma(w1.t[:, :, :], Wh.t[:, 1536:2048].rearrange("(c p) f -> p c f", p=128), reads=[Wh], writes=[w1])
        P.dma(w2.t[:, :, :], Wh.t[:, 2048:2560].rearrange("(c p) f -> p c f", p=128), reads=[Wh], writes=[w2])
        for ch in range(4):
            cs_ = slice(ch * 128, (ch + 1) * 128)
            o = ob[oi % 2]
            oi += 1
            for s0 in range(0, T, 512):
                sl = slice(s0, s0 + 512)
                k = n % 2
                n += 1
                pa, pb = C.PS[k], C.PS[2 + k]
                for c in range(8):
                    P.mm(pa.t[:, :], w1.t[:, c, cs_], C.xn.t[:, c, sl], start=(c == 0), stop=(c == 7), reads=[w1, C.xn], writes=[pa])
                for c in range(8):
                    P.mm(pb.t[:, :], w2.t[:, c, cs_], C.xn.t[:, c, sl], start=(c == 0), stop=(c == 7), reads=[w2, C.xn], writes=[pb])
                P.act(t1[k].t[:, :], pb.t[:, :], AF.Sigmoid, reads=[pb], writes=[t1[k]])
                P.tt("dve", o.t[:, sl], pa.t[:, :], t1[k].t[:, :], ALU.mult, reads=[pa, t1[k]], writes=[(o, s0)])
            P.dma(C.gluT_d.t[ch * 128:(ch + 1) * 128, PADL + tt * T:PADL + (tt + 1) * T], o.t[:, :], reads=[o], writes=[(C.gluT_d, (ch, tt))])
    P.barrier()
    P.sb_release(m)


def hyb_conv_phase(P, C, S, hvcol, dww):
    P.phase = "hyb_conv_phase"
    m = P.sb_mark()
    diag = P.sb("diag", [128, 4, CONVW, 128], BF16)
    glu = [P.sb(f"glu{i}", [128, PADL + 512], BF16) for i in range(2)]
    csb = P.sb("csb", [128, 4, 512], F32)
    cbf = P.sb("cbf", [128, 4, 512], BF16)
    csq = P.sb("csq", [128, 4, 512], BF16)
    mean = P.sb("mean", [128, 512], F32)
    msq = P.sb("msq", [128, 512], F32)
    rstd = P.sb("rstdc", [128, 512], F32)
    tmp = [P.sb(f"ctmp{i}", [128, 512], F32) for i in range(2)]
    co = [P.sb(f"co{i}", [128, 512], BF16) for i in range(2)]
    for cc in range(4):
        for k in range(CONVW):
            P.ts("dve", diag.t[:, cc, k, :], C.ident_bf.t[:, :], dww[:, cc * CONVW + k:cc * CONVW + k + 1], None, ALU.mult,
                 reads=[C.ident_bf, C.hyb_dww], writes=[(diag, (cc, k))])
    gi = 0
    oi = 0
    off = PADL - (CONVW - 1)
    for t0 in range(0, S, 512):
        for cc in range(4):
            g = glu[gi % 2]
            gi += 1
            P.dma(g.t[:, :], C.gluT_d.t[cc * 128:(cc + 1) * 128, t0:t0 + PADL + 512], reads=[C.gluT_d], writes=[g])
            pc = C.PS[cc % 2]
            for k in range(CONVW):
                P.mm(pc.t[:, :], diag.t[:, cc, k, :], g.t[:, off + k:off + k + 512], start=(k == 0), stop=(k == CONVW - 1),
                     reads=[diag, g], writes=[pc])
            P.act(csb.t[:, cc, :], pc.t[:, :], AF.Identity, reads=[pc, C.hvec], writes=[(csb, cc)], bias=hvcol[:, cc:cc + 1])
            P.act(csq.t[:, cc, :], pc.t[:, :], AF.Square, reads=[pc, C.hvec], writes=[(csq, cc)], bias=hvcol[:, cc:cc + 1])
            P.copy("pool", cbf.t[:, cc, :], csb.t[:, cc, :], reads=[(csb, cc)], writes=[(cbf, cc)])
        p1, p2 = C.PS[4], C.PS[5]
        for cc in range(4):
            P.mm(p1.t[:, :], C.ones_bf.t[:, :], cbf.t[:, cc, :], start=(cc == 0), stop=(cc == 3), reads=[(cbf, cc), C.ones_bf], writes=[p1])
        for cc in range(4):
            P.mm(p2.t[:, :], C.ones_bf.t[:, :], csq.t[:, cc, :], start=(cc == 0), stop=(cc == 3), reads=[(csq, cc), C.ones_bf], writes=[p2])
        P.ts("dve", mean.t[:, :], p1.t[:, :], 1.0 / 512, None, ALU.mult, reads=[p1], writes=[mean])
        P.tt("dve", msq.t[:, :], mean.t[:, :], mean.t[:, :], ALU.mult, reads=[mean], writes=[msq])
        P.stt("dve", msq.t[:, :], p2.t[:, :], 1.0 / 512, msq.t[:, :], ALU.mult, ALU.subtract, reads=[p2, msq], writes=[msq])
        P.act(rstd.t[:, :], msq.t[:, :], AF.Ln, reads=[msq, C.cst], writes=[rstd], bias=C.cst.t[:, 0:1])
        P.act(rstd.t[:, :], rstd.t[:, :], AF.Exp, reads=[rstd], writes=[rstd], scale=-0.5)
        for cc in range(4):
            tm = tmp[cc % 2]
            o = co[oi % 2]
            oi += 1
            P.tt("dve", tm.t[:, :], csb.t[:, cc, :], mean.t[:, :], ALU.subtract, reads=[(csb, cc), mean], writes=[tm])
            P.tt("dve", tm.t[:, :], tm.t[:, :], rstd.t[:, :], ALU.mult, reads=[tm, rstd], writes=[tm])
            P.act(o.t[:, :], tm.t[:, :], AF.Silu, reads=[tm, C.hvec], writes=[o], scale=hvcol[:, 4 + cc:5 + cc], bias=hvcol[:, 8 + cc:9 + cc])
            P.dma(C.catT_d.t[512 + cc * 128:512 + (cc + 1) * 128, t0:t0 + 512], o.t[:, :], reads=[o], writes=[(C.catT_d, (4 + cc, t0))])
    P.barrier()
    P.sb_release(m)


def hyb_attn_phase(P, C, S):
    P.phase = "hyb_attn_phase"
    m = P.sb_mark()
    qT = P.sb("qTa", [128, S], BF16)
    kT = P.sb("kTa", [128, S], BF16)
    acc = [P.sb(f"acc{i}", [65, S], F32) for i in range(2)]
    bc = P.sb("bc", [64, S], F32)
    rden = P.sb("rden", [65, S], F32)
    ao = P.sb("ao", [64, S], BF16)
    NV = 4
    vts = [P.sb(f"vts{i}", [128, 32, 2, 65], BF16) for i in range(NV)]
    NPT = 6
    pts = [P.sb(f"pt{i}", [128, 512], BF16) for i in range(NPT)]
    for v_ in vts:
        P.memset("pool", v_.t[:, :, :, 64:65], 1.0, writes=[(v_, "ones")])
    cnt = {"pt": 0, "ps": 0}
    groups = [(r, c) for (_, r) in DIL for c in range(r)]

    def load_v(hp, g):
        r, c = groups[g]
        nb = (S // r) // 128
        vt = vts[g % NV]
        for hh in range(2):
            src = C.v_d.t[:, hp * 128 + hh * 64:hp * 128 + (hh + 1) * 64]
            src = src[c::r, :].rearrange("(j p) e -> p j e", p=128)
            for jb in range(0, nb, 8):
                je = min(nb, jb + 8)
                P.dma(vt.t[:, jb:je, hh, 0:64], src[:, jb:je, :], reads=[C.v_d], writes=[(vt, (hh, jb))])

    for hp in range(4):
        P.dma(qT.t[:, :], C.qT_d.t[hp * 128:(hp + 1) * 128, :], reads=[C.qT_d], writes=[qT])
        P.dma(kT.t[:, :], C.kT_d.t[hp * 128:(hp + 1) * 128, :], reads=[C.kT_d], writes=[kT])
        first = [True, True]
        items = []
        load_v(hp, 0)
        for g, (r, c) in enumerate(groups):
            L = S // r
            nb = L // 128
            vt = vts[g % NV]
            for hh in range(2):
                hb = hh * 64
                A = acc[hh]
                ptl = {}
                isfirst = (g == 0)

                def tok(j0, nblk, r=r, c=c):
                    return slice(c + r * 128 * j0, c + r * (128 * (j0 + nblk) - 1) + 1, r)

                for j0 in range(0, nb, 2):
                    def stageA(j0=j0, nb=nb, hb=hb, tok=tok, ptl=ptl, g=g, hh=hh, hp=hp):
                        if j0 == 0 and hh == 0 and g + 1 < len(groups):
                            load_v(hp, g + 1)
                        nkb = min(2, nb - j0)
                        pt = pts[cnt["pt"] % NPT]
                        cnt["pt"] += 1
                        ps = C.PS[cnt["ps"] % 4]
                        cnt["ps"] += 1
                        widths = []
                        for jj in range(nkb):
                            j = j0 + jj
                            nq = 2 if j + 1 < nb else 1
                            P.mm(ps.t[:, jj * 256:jj * 256 + nq * 128], kT.t[hb:hb + 64, tok(j, 1)], qT.t[hb:hb + 64, tok(j, nq)],
                                 reads=[kT, qT], writes=[ps])
                            widths.append(nq)
                        wtot = 256 * (nkb - 1) + 128 * widths[-1]
                        P.act(pt.t[:, 0:wtot], ps.t[:, 0:wtot], AF.Exp, reads=[ps], writes=[pt], scale=0.125)
                        P.tt("pool", pt.t[:, 0:wtot], pt.t[:, 0:wtot], C.mask4.t[:, 0:wtot], ALU.mult, reads=[pt, C.mask4], writes=[pt])
                        for jj in range(nkb):
                            ptl[j0 + jj] = (pt, jj * 256)

                    def stageB(j0=j0, nb=nb, tok=tok, ptl=ptl, vt=vt, hh=hh, A=A, isfirst=isfirst):
                        nkb = min(2, nb - j0)
                        for jj in range(nkb):
                            n_ = j0 + jj
                            q4 = n_ % 4
                            po = C.PS[4 + (n_ // 4) % 2]
                            ptc, oc = ptl[n_]
                            has_prev = n_ > 0
                            P.mm(po.t[0:65, q4 * 128:(q4 + 1) * 128], vt.t[:, n_, hh, :], ptc.t[:, oc:oc + 128], start=True, stop=not has_prev,
                                 reads=[vt, ptc], writes=[(po, q4)])
                            if has_prev:
                                ptp, op_ = ptl[n_ - 1]
                                P.mm(po.t[0:65, q4 * 128:(q4 + 1) * 128], vt.t[:, n_ - 1, hh, :], ptp.t[:, op_ + 128:op_ + 256], start=False, stop=True,
                                     reads=[vt, ptp], writes=[(po, q4)])
                            if q4 == 3 or n_ == nb - 1:
                                nq4 = q4 + 1
                                b0 = n_ - q4
                                dsts = A.t[:, tok(b0, nq4)]
                                if isfirst:
                                    P.copy("act", dsts, po.t[0:65, 0:nq4 * 128], reads=[po], writes=[A])
                                else:
                                    P.tt("dve", dsts, dsts, po.t[0:65, 0:nq4 * 128], ALU.add, reads=[po, A], writes=[A])

                    items.append((stageA, stageB))
        LOOK = 3
        for k_ in range(len(items) + LOOK):
            if k_ < len(items):
                items[k_][0]()
            if k_ >= LOOK:
                items[k_ - LOOK][1]()
        for hh in range(2):
            A = acc[hh]
            h = hp * 2 + hh
            P.add("dve", lambda e, A=A: e.reciprocal(out=rden.t[64:65, :], in_=A.t[64:65, :]), reads=[A], writes=[rden])
            P.dma(C.den_d.t[0:1, :], rden.t[64:65, :], reads=[rden], writes=[C.den_d])
            P.dma(bc.t[:, :], C.den_d.t[0:1, :].to_broadcast([64, S]), reads=[C.den_d], writes=[bc])
            P.tt("dve", ao.t[:, :], A.t[0:64, :], bc.t[:, :], ALU.mult, reads=[A, bc], writes=[ao])
            P.dma(C.catT_d.t[h * 64:(h + 1) * 64, :], ao.t[:, :], reads=[ao], writes=[(C.catT_d, ("a", h))])
    P.barrier()
    P.sb_release(m)


def mix_out_phase(P, C, S, T, Wo):
    P.phase = "mix_out_phase"
    m = P.sb_mark()
    htile = P.sb("htile", [128, 8, T], F32)
    cat = P.sb("cat", [128, 8, T], BF16)
    wo = P.sb("wo_m", [128, 8, D], BF16)
    hview = C.hT.t.rearrange("(c p) t -> p c t", p=128)
    cview = C.catT_d.t.rearrange("(c p) t -> p c t", p=128)
    P.dma(wo.t[:, :, :], Wo.t.rearrange("(c p) d -> p c d", p=128), reads=[Wo], writes=[wo])
    n = 0
    for tt in range(S // T):
        tsl = slice(tt * T, (tt + 1) * T)
        P.dma(htile.t[:, :, :], hview[:, :, tsl], reads=[(C.hT, tt)], writes=[htile])
        P.dma(cat.t[:, :, :], cview[:, :, tsl], reads=[C.catT_d], writes=[cat])
        for dc in range(8):
            for s0 in range(0, T, 512):
                sl = slice(s0, s0 + 512)
                py = C.PS[n % 2]
                n += 1
                for f in range(8):
                    P.mm(py.t[:, :], wo.t[:, f, dc * 128:(dc + 1) * 128], cat.t[:, f, sl], start=(f == 0), stop=(f == 7), reads=[wo, cat], writes=[py])
                P.tt("dve", htile.t[:, dc, sl], py.t[:, :], htile.t[:, dc, sl], ALU.add, reads=[py, (htile, (dc, s0))], writes=[(htile, (dc, s0))])
        P.dma(hview[:, :, tsl], htile.t[:, :, :], reads=[htile], writes=[(C.hT, tt)])
    P.barrier()
    P.sb_release(m)


GDN_IN = 4112
import os as _os
GDN_STOP = int(_os.environ.get('GDN_STOP', '0'))
QSCALE = 128.0 ** -0.5


def gdn_alloc_scalars(P, C, S):
    NB = S // 128
    sc = Ctx()
    sc.beta = P.sb("g_beta", [128, NB, 8], F32)
    sc.gc = P.sb("g_gc", [128, NB, 8], F32)
    sc.kd = P.sb("g_kd", [128, NB, 8], F32)
    sc.kbg = P.sb("g_kbg", [128, NB, 8], F32)
    sc.egl = P.sb("g_egl", [128, 2, NB, 8], F32)
    return sc


def gdn_proj_phase(P, C, S, gcol, i, sc, Wg):
    P.phase = "gdn_proj_phase"
    m = P.sb_mark()
    NB = S // 128
    xn = P.sb("xnf", [128, 8, S], BF16)
    mA = P.sb_mark()
    ht = [P.sb(f"ht{j}", [128, 8, 256], F32) for j in range(2)]
    C.sq = P.sb("sq", [128, 8, 512], BF16)
    C.rstd = P.sb("rstd", [128, 512], F32)
    hview = C.hT.t.rearrange("(c p) t -> p c t", p=128)
    for t0 in range(0, S, 256):
        h = ht[(t0 // 256) % 2]
        P.dma(h.t[:, :, :], hview[:, :, t0:t0 + 256], reads=[C.hT], writes=[h])
        rmsnorm_tile(P, C, h, gcol, xn, 256, doff=t0)
    dg = P.sb("dg", [128, 24, 4, 128], BF16)
    cw = C.gcw.t[:, i * 96:(i + 1) * 96]
    for ch in range(24):
        for k in range(4):
            P.ts("dve", dg.t[:, ch, k, :], C.ident_bf.t[:, :], cw[:, ch * 4 + k:ch * 4 + k + 1], None, ALU.mult,
                 reads=[C.ident_bf, C.gcw], writes=[(dg, (ch, k))])
    PADG = 4
    pre = [P.sb(f"pre{j}", [128, PADG + S], BF16) for j in range(2)]
    for pr in pre:
        P.memset("pool", pr.t[:, 0:PADG], 0.0, writes=[(pr, "pad")])
    wch = [P.sb(f"wch{j}", [128, 8, 512], BF16) for j in range(2)]
    HS = min(2048, S)
    yhs = [P.sb(f"yh{j}", [128, HS], BF16) for j in range(2)]
    sshs = [P.sb(f"ssh{j}", [128, HS], F32) for j in range(2)]
    sqb = [P.sb(f"sqb{j}", [128, 512], BF16) for j in range(2)]
    ob = [P.sb(f"gob{j}", [128, HS], BF16) for j in range(2)]
    n = 0
    oi = 0
    for cg in range(8):
        w = wch[cg % 2]
        P.dma(w.t[:, :, :], Wg.t[:, cg * 512:(cg + 1) * 512].rearrange("(c p) f -> p c f", p=128), reads=[Wg], writes=[w])
        for ci in range(4):
            ch = cg * 4 + ci
            cs_ = slice(ci * 128, (ci + 1) * 128)
            r0 = (ch % 8) * 128
            if ch < 24:
                pr = pre[ch % 2]
                for s0 in range(0, S, 512):
                    k = n % 2
                    n += 1
                    ps = C.PS[k]
                    for c in range(8):
                        P.mm(ps.t[:, :], w.t[:, c, cs_], xn.t[:, c, s0:s0 + 512], start=(c == 0), stop=(c == 7), reads=[w, xn], writes=[ps])
                    P.copy("act", pr.t[:, PADG + s0:PADG + s0 + 512], ps.t[:, :], reads=[ps], writes=[(pr, s0)])
                dst = (C.qT_g, C.kT_g, C.vT_g)[ch // 8]
                for h0 in range(0, S, HS):
                    o = ob[oi % 2]
                    yh = yhs[oi % 2]
                    ssh = sshs[oi % 2]
                    oi += 1
                    for s0 in range(h0, h0 + HS, 512):
                        k = n % 2
                        n += 1
                        pc = C.PS[2 + k]
                        ls = slice(s0 - h0, s0 - h0 + 512)
                        for kk in range(4):
                            P.mm(pc.t[:, :], dg.t[:, ch, kk, :], pr.t[:, s0 + 1 + kk:s0 + 1 + kk + 512], start=(kk == 0), stop=(kk == 3),
                                 reads=[dg, pr], writes=[pc])
                        if ch >= 16:
                            P.act(o.t[:, ls], pc.t[:, :], AF.Silu, reads=[pc], writes=[(o, s0)])
                        else:
                            sq_ = sqb[k]
                            pn = C.PS[4 + k]
                            P.act(yh.t[:, ls], pc.t[:, :], AF.Silu, reads=[pc], writes=[(yh, s0)])
                            P.tt("dve", sq_.t[:, :], yh.t[:, ls], yh.t[:, ls], ALU.mult, reads=[(yh, s0)], writes=[sq_])
                            P.mm(pn.t[:, :], C.ones_bf.t[:, :], sq_.t[:, :], reads=[sq_, C.ones_bf], writes=[pn])
                            P.copy("dve", ssh.t[:, ls], pn.t[:, :], reads=[pn], writes=[(ssh, s0)])
                    if ch < 16:
                        P.act(ssh.t[:, :], ssh.t[:, :], AF.Ln, reads=[ssh, C.cst], writes=[ssh], bias=C.cst.t[:, 0:1])
                        P.act(ssh.t[:, :], ssh.t[:, :], AF.Exp, reads=[ssh], writes=[ssh], scale=-0.5)
                        if ch < 8:
                            P.stt("dve", o.t[:, :], yh.t[:, :], QSCALE, ssh.t[:, :], ALU.mult, ALU.mult, reads=[yh, ssh], writes=[o])
                        else:
                            P.tt("dve", o.t[:, :], yh.t[:, :], ssh.t[:, :], ALU.mult, reads=[yh, ssh], writes=[o])
                    P.dma(dst.t[r0:r0 + 128, h0:h0 + HS], o.t[:, :], reads=[o], writes=[(dst, (ch % 8, h0))])
            else:
                for h0 in range(0, S, HS):
                    o = ob[oi % 2]
                    oi += 1
                    for s0 in range(h0, h0 + HS, 512):
                        k = n % 2
                        n += 1
                        ps = C.PS[k]
                        for c in range(8):
                            P.mm(ps.t[:, :], w.t[:, c, cs_], xn.t[:, c, s0:s0 + 512], start=(c == 0), stop=(c == 7), reads=[w, xn], writes=[ps])
                        P.act(o.t[:, s0 - h0:s0 - h0 + 512], ps.t[:, :], AF.Silu, reads=[ps], writes=[(o, s0)])
                    P.dma(C.szT_g.t[r0:r0 + 128, h0:h0 + HS], o.t[:, :], reads=[o], writes=[(C.szT_g, (ch % 8, h0))])

    P.barrier()
    P.sb_release(mA)
    if GDN_STOP == 1:
        P.sb_release(m)
        return
    wba = P.sb("wba", [128, 8, 16], BF16)
    P.dma(wba.t[:, :, :], Wg.t[:, 4096:4112].rearrange("(c p) f -> p c f", p=128), reads=[Wg], writes=[wba])
    ba = P.sb("ba", [128, NB, 16], F32)
    x_ = P.sb("gx", [128, NB, 8], F32)
    e_ = P.sb("ge", [128, NB, 8], F32)
    g_ = P.sb("gg", [128, NB, 8], F32)
    gls = P.sb("gls", [128, NB, 8], F32)
    gsm = P.sb("gsm", [128, 16], F32)
    nA = P.sb("nA", [128, 8], F32)
    P.dma(gsm.t[:, :], C.gsm_row_in[0:1, i * 16:(i + 1) * 16].to_broadcast([128, 16]), writes=[gsm])
    P.act(nA.t[:, :], gsm.t[:, 0:8], AF.Exp, reads=[gsm], writes=[nA])
    P.ts("dve", nA.t[:, :], nA.t[:, :], -1.0, None, ALU.mult, reads=[nA], writes=[nA])
    if GDN_STOP == 21:
        P.barrier()
        P.sb_release(m)
        return
    pba = C.PS[6]
    for blk in range(NB):
        for c in range(8):
            P.mm(pba.t[:, blk * 16:(blk + 1) * 16], xn.t[:, c, blk * 128:(blk + 1) * 128], wba.t[:, c, :], start=(c == 0), stop=(c == 7),
                 reads=[wba, xn], writes=[(pba, blk)])
    P.copy("dve", ba.t[:, :, :], pba.t[:, 0:NB * 16].rearrange("p (b f) -> p b f", f=16), reads=[pba], writes=[ba])

    if GDN_STOP == 22:
        P.barrier()
        P.sb_release(m)
        return
    def bc8(ap):
        return ap.unsqueeze(1).to_broadcast([128, NB, 8])

    P.act(sc.beta.t[:, :, :], ba.t[:, :, 0:8], AF.Sigmoid, reads=[ba], writes=[sc.beta])
    P.tt("dve", x_.t[:, :, :], ba.t[:, :, 8:16], bc8(gsm.t[:, 8:16]), ALU.add, reads=[ba, gsm], writes=[x_])
    P.ts("dve", e_.t[:, :, :], x_.t[:, :, :], 30.0, None, ALU.min, reads=[x_], writes=[e_])
    P.act(e_.t[:, :, :], e_.t[:, :, :], AF.Exp, reads=[e_], writes=[e_])
    P.act(e_.t[:, :, :], e_.t[:, :, :], AF.Ln, reads=[e_, C.cst], writes=[e_], bias=C.cst.t[:, 2:3])
    P.tt("dve", e_.t[:, :, :], e_.t[:, :, :], x_.t[:, :, :], ALU.max, reads=[e_, x_], writes=[e_])
    P.tt("dve", g_.t[:, :, :], e_.t[:, :, :], bc8(nA.t[:, :]), ALU.mult, reads=[e_, nA], writes=[g_])
    if GDN_STOP == 23:
        P.barrier()
        P.sb_release(m)
        return
    gflat = g_.t[:, :, :].rearrange("p b h -> p (b h)")
    W8 = NB * 8
    pgc, pl0, pl1 = C.PS[5], C.PS[4], C.PS[3]
    P.mm(pgc.t[:, 0:W8], C.consts.t[:, 1024:1152], gflat, reads=[g_, C.consts], writes=[pgc])
    P.mm(pl0.t[:, 0:W8], C.consts.t[:, 1152:1280], gflat, reads=[g_, C.consts], writes=[pl0])
    P.mm(pl1.t[:, 0:W8], C.consts.t[:, 1280:1408], gflat, reads=[g_, C.consts], writes=[pl1])

    def v3(ap):
        return ap.rearrange("p (b h) -> p b h", h=8)
    if GDN_STOP == 24:
        P.barrier()
        P.sb_release(m)
        return

    P.copy("dve", sc.gc.t[:, :, :], v3(pgc.t[:, 0:W8]), reads=[pgc], writes=[sc.gc])
    P.copy("dve", gls.t[0:64, :, :], v3(pl0.t[0:64, 0:W8]), reads=[pl0], writes=[(gls, 0)])
    P.copy("dve", gls.t[64:128, :, :], v3(pl1.t[64:128, 0:W8]), reads=[pl1], writes=[(gls, 1)])
    if GDN_STOP == 25:
        P.barrier()
        P.sb_release(m)
        return
    P.copy("dve", sc.egl.t[:, 0, :, :], v3(pl0.t[:, 0:W8]), reads=[pl0], writes=[(sc.egl, 0)])
    P.copy("dve", sc.egl.t[:, 1, :, :], v3(pl1.t[:, 0:W8]), reads=[pl1], writes=[(sc.egl, 1)])
    P.act(sc.egl.t[:, :, :, :], sc.egl.t[:, :, :, :], AF.Exp, reads=[sc.egl], writes=[sc.egl])
    if GDN_STOP == 26:
        P.barrier()
        P.sb_release(m)
        return
    P.tt("dve", gls.t[:, :, :], gls.t[:, :, :], sc.gc.t[:, :, :], ALU.subtract, reads=[gls, sc.gc], writes=[gls])
    P.act(sc.kd.t[:, :, :], gls.t[:, :, :], AF.Exp, reads=[gls], writes=[sc.kd])
    P.act(sc.kbg.t[:, :, :], sc.gc.t[:, :, :], AF.Exp, reads=[sc.gc], writes=[sc.kbg])
    P.tt("dve", sc.kbg.t[:, :, :], sc.kbg.t[:, :, :], sc.beta.t[:, :, :], ALU.mult, reads=[sc.kbg, sc.beta], writes=[sc.kbg])

    if GDN_STOP == 2:
        P.barrier()
        P.sb_release(m)
        return
    aT = P.sb("aT", [8, S], F32)
    eT = P.sb("eT", [8, S], F32)
    mk = P.sb("mk", [8, S], F32)
    gcol8 = P.sb("gcol8", [8, 4], F32)
    P.dma(gcol8.t[:, 0:2], C.gsm_col_in[:, i * 2:(i + 1) * 2], writes=[gcol8])
    P.act(gcol8.t[:, 2:3], gcol8.t[:, 0:1], AF.Exp, reads=[gcol8], writes=[gcol8])
    P.ts("dve", gcol8.t[:, 2:3], gcol8.t[:, 2:3], -1.0, None, ALU.mult, reads=[gcol8], writes=[gcol8])
    for s0 in range(0, S, 512):
        k = n % 2
        n += 1
        ps = C.PS[k]
        for c in range(8):
            P.mm(ps.t[0:8, :], wba.t[:, c, 8:16], xn.t[:, c, s0:s0 + 512], start=(c == 0), stop=(c == 7), reads=[wba, xn], writes=[ps])
        P.ts("dve", aT.t[:, s0:s0 + 512], ps.t[0:8, :], gcol8.t[:, 1:2], None, ALU.add, reads=[ps, gcol8], writes=[(aT, s0)])
    P.ts("dve", eT.t[:, :], aT.t[:, :], 30.0, None, ALU.min, reads=[aT], writes=[eT])
    P.act(eT.t[:, :], eT.t[:, :], AF.Exp, reads=[eT], writes=[eT])
    P.act(eT.t[:, :], eT.t[:, :], AF.Ln, reads=[eT, C.cst], writes=[eT], bias=C.cst.t[0:8, 2:3])
    P.tt("dve", eT.t[:, :], eT.t[:, :], aT.t[:, :], ALU.max, reads=[eT, aT], writes=[eT])
    P.ts("dve", eT.t[:, :], eT.t[:, :], gcol8.t[:, 2:3], None, ALU.mult, reads=[eT, gcol8], writes=[eT])
    P.memset("dve", mk.t[:, :], 1.0, writes=[mk])
    P.memset("dve", mk.t[:, 0:S:64], 0.0, writes=[mk])
    P.add("dve", lambda e: e.tensor_tensor_scan(out=aT.t[:, :], data0=mk.t[:, :], data1=eT.t[:, :], initial=0.0, op0=ALU.mult, op1=ALU.add),
          reads=[mk, eT], writes=[aT])
    P.dma(C.gc_d.t[:, :], aT.t[:, :], reads=[aT], writes=[C.gc_d])
    P.barrier()
    P.sb_release(m)


def gdn_core_phase(P, C, S, i, sc):
    P.phase = "gdn_core_phase"
    m = P.sb_mark()
    NB = S // 128
    LT = 256
    BPL = LT // 128

    def ld(name, dt):
        return [P.sb(f"{name}{j}", [128, 8, LT], dt) for j in range(2)]

    qT, kT, vT, szT, gcB = ld("gq", BF16), ld("gk", BF16), ld("gv", BF16), ld("gsz", BF16), ld("gcB", F32)
    Sst = P.sb("Sst", [128, 8, 128], F32)
    Sbf = P.sb("Sbf", [128, 8, 128], BF16)
    P.memset("pool", Sst.t[:, :, :], 0.0, writes=[Sst])
    P.memset("pool", Sbf.t[:, :, :], 0.0, writes=[Sbf])

    class Wset:
        pass

    Ws = []
    for wi_ in range(2):
        W = Wset()
        for nm, dt in (("egc", F32), ("qd", BF16), ("kbg", BF16), ("kd", BF16), ("vb", BF16), ("E", F32), ("En", F32),
                       ("attnT", BF16), ("X", BF16), ("XT", BF16), ("Pa", BF16), ("Pb", BF16), ("PTa", BF16), ("PTb", BF16),
                       ("TTa", BF16), ("TTb", BF16), ("u", F32), ("wT", BF16), ("vn", BF16), ("oT", F32), ("osq", BF16),
                       ("rn", F32), ("ob", BF16)):
            setattr(W, nm, P.sb(f"w{wi_}_" + nm, [128, 8, 128], dt))
        Ws.append(W)

    maskU = C.consts.t[:, 1024:1152].unsqueeze(1).to_broadcast([128, 8, 128])
    maskLn = C.consts.t[:, 1408:1536].unsqueeze(1).to_broadcast([128, 8, 128])
    identb = C.ident_bf.t[:, :].unsqueeze(1).to_broadcast([128, 8, 128])
    ngcol = C.gng.t[:, i:i + 1]

    def bcf(ap):
        return ap.unsqueeze(2).to_broadcast([128, 8, 128])

    def h4(ps, lo=0, hi=128):
        return ps.t[lo:hi, :].rearrange("p (h d) -> p h d", h=4)

    def loads(lt):
        li = lt % 2
        tsl = slice(lt * LT, (lt + 1) * LT)
        for dstl, srcd in ((qT, C.qT_g), (kT, C.kT_g), (vT, C.vT_g), (szT, C.szT_g)):
            P.dma(dstl[li].t[:, :, :], srcd.t[:, tsl].rearrange("(h p) t -> p h t", p=128), reads=[srcd], writes=[dstl[li]])
        P.dma(gcB[li].t[:, :, :], C.gc_d.t[:, tsl].unsqueeze(0).to_broadcast([128, 8, LT]), reads=[C.gc_d], writes=[gcB[li]])

    def block_pre(b, W):
        P.phase = "gdn_pre"
        lt = b // BPL
        li = lt % 2
        off = (b % BPL) * 128
        bsl = slice(off, off + 128)
        q_, k_, v_, gb_ = qT[li], kT[li], vT[li], gcB[li]
        pq_ = b % 2
        P.act(W.egc.t[:, :, :], gb_.t[:, :, bsl], AF.Exp, reads=[gb_], writes=[W.egc])
        P.tt("dve", W.qd.t[:, :, :], q_.t[:, :, bsl], W.egc.t[:, :, :], ALU.mult, reads=[q_, W.egc], writes=[W.qd])
        pk = C.PS[0].t[:, :].bitcast(BF16)
        for h in range(8):
            P.tr(pk[:, h * 128:(h + 1) * 128], k_.t[:, h, bsl], C.ident_bf.t[:, :], reads=[k_, C.ident_bf], writes=[(C.PS[0], h)])
        pk3 = pk.rearrange("p (h d) -> p h d", h=8)
        P.tt("dve", W.kbg.t[:, :, :], pk3, bcf(sc.kbg.t[:, b, :]), ALU.mult, reads=[C.PS[0], sc.kbg], writes=[W.kbg])
        P.tt("dve", W.kd.t[:, :, :], pk3, bcf(sc.kd.t[:, b, :]), ALU.mult, reads=[C.PS[0], sc.kd], writes=[W.kd])
        pv = C.PS[1].t[:, :].bitcast(BF16)
        for h in range(8):
            P.tr(pv[:, h * 128:(h + 1) * 128], v_.t[:, h, bsl], C.ident_bf.t[:, :], reads=[v_, C.ident_bf], writes=[(C.PS[1], h)])
        P.tt("dve", W.vb.t[:, :, :], pv.rearrange("p (h d) -> p h d", h=8), bcf(sc.beta.t[:, b, :]), ALU.mult,
             reads=[C.PS[1], sc.beta], writes=[W.vb])
        yield
        for half in range(2):
            pg, pq = C.PS[2 + half], C.PS[4 + half]
            for hh in range(4):
                h = half * 4 + hh
                P.mm(pg.t[:, hh * 128:(hh + 1) * 128], k_.t[:, h, bsl], k_.t[:, h, bsl], reads=[k_], writes=[(pg, hh)])
            for hh in range(4):
                h = half * 4 + hh
                P.mm(pq.t[:, hh * 128:(hh + 1) * 128], k_.t[:, h, bsl], q_.t[:, h, bsl], reads=[k_, q_], writes=[(pq, hh)])
        P.tt("dve", W.E.t[:, :, :], gb_.t[:, :, bsl], bcf(sc.gc.t[:, b, :]), ALU.subtract, reads=[gb_, sc.gc], writes=[W.E])
        P.ts("dve", W.En.t[:, :, :], W.E.t[:, :, :], -1.0, 0.0, ALU.mult, ALU.min, reads=[W.E], writes=[W.En])
        P.ts("dve", W.E.t[:, :, :], W.E.t[:, :, :], 0.0, None, ALU.min, reads=[W.E], writes=[W.E])
        P.act(W.E.t[:, :, :], W.E.t[:, :, :], AF.Exp, reads=[W.E], writes=[W.E])
        P.act(W.En.t[:, :, :], W.En.t[:, :, :], AF.Exp, reads=[W.En], writes=[W.En])
        P.tt("pool", W.E.t[:, :, :], W.E.t[:, :, :], maskU, ALU.mult, reads=[W.E, C.consts], writes=[W.E])
        P.tt("pool", W.En.t[:, :, :], W.En.t[:, :, :], maskLn, ALU.mult, reads=[W.En, C.consts], writes=[W.En])
        P.tt("pool", W.En.t[:, :, :], W.En.t[:, :, :], bcf(sc.beta.t[:, b, :]), ALU.mult, reads=[W.En, sc.beta], writes=[W.En])
        for half in range(2):
            hs = slice(half * 4, half * 4 + 4)
            P.tt("dve", W.attnT.t[:, hs, :], h4(C.PS[4 + half]), W.E.t[:, hs, :], ALU.mult, reads=[C.PS[4 + half], W.E], writes=[(W.attnT, half)])
            P.tt("dve", W.X.t[:, hs, :], h4(C.PS[2 + half]), W.En.t[:, hs, :], ALU.mult, reads=[C.PS[2 + half], W.En], writes=[(W.X, half)])
        yield
        px = C.PS[pq_].t[:, :].bitcast(BF16)
        for h in range(8):
            P.tr(px[:, h * 128:(h + 1) * 128], W.X.t[:, h, :], C.ident_bf.t[:, :], reads=[W.X, C.ident_bf], writes=[(C.PS[pq_], h)])
        P.copy("act", W.XT.t[:, :, :], px.rearrange("p (h d) -> p h d", h=8), reads=[C.PS[pq_]], writes=[W.XT])
        P.tt("pool", W.TTa.t[:, :, :], W.XT.t[:, :, :], identb, ALU.add, reads=[W.XT, C.ident_bf], writes=[W.TTa])
        yield
        Pc, PTc, TTc = W.X, W.XT, W.TTa
        pbufs = [(W.Pa, W.PTa), (W.Pb, W.PTb)]
        ttbufs = [W.TTb, W.TTa]
        for kk in range(1, 6):
            Pn, PTn = pbufs[kk % 2]
            TTn = ttbufs[(kk - 1) % 2]
            for half in range(2):
                hs = slice(half * 4, half * 4 + 4)
                pp = C.PS[2 + half]
                for hh in range(4):
                    h = half * 4 + hh
                    P.mm(pp.t[:, hh * 128:(hh + 1) * 128], PTc.t[:, h, :], Pc.t[:, h, :], reads=[PTc, Pc], writes=[(pp, hh)])
                P.copy("act", Pn.t[:, hs, :], h4(pp), reads=[pp], writes=[(Pn, half)])
                if kk < 5:
                    ppt = C.PS[4 + half]
                    for hh in range(4):
                        h = half * 4 + hh
                        P.mm(ppt.t[:, hh * 128:(hh + 1) * 128], Pc.t[:, h, :], PTc.t[:, h, :], reads=[PTc, Pc], writes=[(ppt, hh)])
                    P.copy("dve", PTn.t[:, hs, :], h4(ppt), reads=[ppt], writes=[(PTn, half)])
            yield
            for half in range(2):
                hs = slice(half * 4, half * 4 + 4)
                pt = C.PS[6 + half]
                for hh in range(4):
                    h = half * 4 + hh
                    P.mm(pt.t[:, hh * 128:(hh + 1) * 128], Pn.t[:, h, :], TTc.t[:, h, :], start=True, stop=False, reads=[Pn, TTc], writes=[(pt, hh)])
                    P.mm(pt.t[:, hh * 128:(hh + 1) * 128], C.ident_bf.t[:, :], TTc.t[:, h, :], start=False, stop=True, reads=[C.ident_bf, TTc], writes=[(pt, hh)])
                P.copy("act" if half == 0 else "dve", TTn.t[:, hs, :], h4(pt), reads=[pt], writes=[(TTn, half)])
            Pc, PTc, TTc = Pn, PTn, TTn
            yield
        for half in range(2):
            hs = slice(half * 4, half * 4 + 4)
            pu = C.PS[6 + half]
            for hh in range(4):
                h = half * 4 + hh
                P.mm(pu.t[:, hh * 128:(hh + 1) * 128], TTc.t[:, h, :], W.vb.t[:, h, :], reads=[TTc, W.vb], writes=[(pu, hh)])
            P.copy("act", W.u.t[:, hs, :], h4(pu), reads=[pu], writes=[(W.u, half)])
        yield
        for half in range(2):
            hs = slice(half * 4, half * 4 + 4)
            pw = C.PS[6 + half]
            for hh in range(4):
                h = half * 4 + hh
                P.mm(pw.t[:, hh * 128:(hh + 1) * 128], W.kbg.t[:, h, :], TTc.t[:, h, :], reads=[TTc, W.kbg], writes=[(pw, hh)])
            P.copy("dve", W.wT.t[:, hs, :], h4(pw), reads=[pw], writes=[(W.wT, half)])
        yield

    def block_scan(b, W):
        P.phase = "gdn_scan"
        lt = b // BPL
        li = lt % 2
        off = (b % BPL) * 128
        bsl = slice(off, off + 128)
        sz_ = szT[li]
        for cix in range(2):
            lo, hi = cix * 64, cix * 64 + 64
            cs = slice(lo, hi)
            for half in range(2):
                hs = slice(half * 4, half * 4 + 4)
                pW = C.PS[2 + half]
                for hh in range(4):
                    h = half * 4 + hh
                    P.mm(pW.t[cs, hh * 128:(hh + 1) * 128], W.wT.t[:, h, cs], Sbf.t[:, h, :], reads=[W.wT, Sbf], writes=[(pW, hh)])
                P.tt("dve", W.vn.t[cs, hs, :], W.u.t[cs, hs, :], h4(pW, lo, hi), ALU.subtract, reads=[pW, W.u], writes=[(W.vn, (cix, half))])
            pO = C.PS[1]
            for h in range(8):
                P.mm(pO.t[:, h * 64:(h + 1) * 64], Sbf.t[:, h, :], W.qd.t[:, h, cs], start=True, stop=False, reads=[Sbf, W.qd], writes=[(pO, h)])
                P.mm(pO.t[:, h * 64:(h + 1) * 64], W.vn.t[cs, h, :], W.attnT.t[cs, h, cs], start=False, stop=True, reads=[W.vn, W.attnT], writes=[(pO, h)])
            P.copy("act", W.oT.t[:, :, cs], pO.t[:, :].rearrange("p (h t) -> p h t", h=8), reads=[pO], writes=[(W.oT, cix)])
            for half in range(2):
                pS = C.PS[4 + half]
                for hh in range(4):
                    h = half * 4 + hh
                    P.mm(pS.t[:, hh * 128:(hh + 1) * 128], W.kd.t[cs, h, :], W.vn.t[cs, h, :], reads=[W.kd, W.vn], writes=[(pS, hh)])
            P.tt("dve", Sst.t[:, :, :], Sst.t[:, :, :], bcf(sc.egl.t[:, cix, b, :]), ALU.mult, reads=[Sst, sc.egl], writes=[Sst])
            for half in range(2):
                hs = slice(half * 4, half * 4 + 4)
                P.tt("dve", Sst.t[:, hs, :], Sst.t[:, hs, :], h4(C.PS[4 + half]), ALU.add, reads=[C.PS[4 + half], Sst], writes=[Sst])
            P.copy("act", Sbf.t[:, :, :], Sst.t[:, :, :], reads=[Sst], writes=[Sbf])
        P.tt("pool", W.osq.t[:, :, :], W.oT.t[:, :, :], W.oT.t[:, :, :], ALU.mult, reads=[W.oT], writes=[W.osq])
        for half in range(2):
            hs = slice(half * 4, half * 4 + 4)
            pn = C.PS[6 + half]
            P.mm(pn.t[:, :], C.ones_bf.t[:, :], W.osq.t[:, hs, :], reads=[W.osq, C.ones_bf], writes=[pn])
            P.act(W.rn.t[:, hs, :], h4(pn), AF.Ln, reads=[pn, C.cst], writes=[(W.rn, half)], scale=1.0 / 128, bias=C.cst.t[:, 0:1])
        P.act(W.rn.t[:, :, :], W.rn.t[:, :, :], AF.Exp, reads=[W.rn], writes=[W.rn], scale=-0.5)
        P.stt("dve", W.rn.t[:, :, :], W.oT.t[:, :, :], ngcol, W.rn.t[:, :, :], ALU.mult, ALU.mult, reads=[W.oT, W.rn, C.gng], writes=[W.rn])
        P.tt("pool", W.ob.t[:, :, :], W.rn.t[:, :, :], sz_.t[:, :, bsl], ALU.mult, reads=[W.rn, sz_], writes=[W.ob])
        P.dma(C.catT_d.t[:, b * 128:(b + 1) * 128].rearrange("(h p) t -> p h t", p=128), W.ob.t[:, :, :], reads=[W.ob], writes=[(C.catT_d, ("g", b))])

    loads(0)
    for pr_ in range(0, NB, 2):
        blocks = [b for b in (pr_, pr_ + 1) if b < NB]
        if (pr_ // BPL + 1) * LT < S and pr_ % BPL == 0:
            loads(pr_ // BPL + 1)
        gens = [block_pre(b, Ws[b % 2]) for b in blocks]
        alive = list(gens)
        while alive:
            for g_ in list(alive):
                try:
                    P.phase = "gdn_pre"
                    next(g_)
                except StopIteration:
                    alive.remove(g_)
        for b in blocks:
            block_scan(b, Ws[b % 2])
    P.barrier()
    P.sb_release(m)


def gdn_layer(P, C, S, T, gmix, i):
    m = P.sb_mark()
    sc = gdn_alloc_scalars(P, C, S)
    gdn_proj_phase(P, C, S, gmix, i, sc, C.Wg[i])
    if GDN_STOP == 0 or GDN_STOP > 3:
        gdn_core_phase(P, C, S, i, sc)
    P.sb_release(m)
    mix_out_phase(P, C, S, T, C.Wgo[i])

def build(S=4096, depth=4, T=1024, mixers=True, dbg=None):
    nc = bass.Bass("TRN2", target_bir_lowering=False)
    P = Prog(nc)
    C = Ctx()
    C.S = S
    NE = (depth + 1) // 2
    NO = depth // 2

    def din(name, shape, dt=F32):
        return nc.dram_tensor(name, list(shape), dt, kind="ExternalInput")

    xT = din("xT", [D, S])
    pos = din("pos", [1, S], I32)
    gains_in = din("gains", [128, depth * 24 + 8])
    consts_in = din("consts", [128, 2048])
    ffn_w_in = [[din(f"ffn{j + 1}_w_in_{l}", [D, 2 * FF]) for j in range(2)] for l in range(depth)]
    ffn_w_out = [[din(f"ffn{j + 1}_w_out_{l}", [FF, D]) for j in range(2)] for l in range(depth)]
    if mixers:
        hyb_w_in = [din(f"hyb_w_in_{i}", [D, HYB_IN]) for i in range(NE)]
        hyb_w_out = [din(f"hyb_w_out_{i}", [D, D]) for i in range(NE)]
        hyb_dww_in = din("hyb_dww", [128, NE * 4 * CONVW])
        hvec_in = din("hvec", [128, NE * 12])
        gdn_w_in = [din(f"gdn_w_in_{i}", [D, GDN_IN]) for i in range(NO)]
        gdn_w_out = [din(f"gdn_w_out_{i}", [D, D]) for i in range(NO)]
        gcw_in = din("gcw", [128, max(NO, 1) * 96])
        gng_in = din("gng", [128, max(NO, 1)])
        C.gsm_row_in = din("gsm_row", [1, max(NO, 1) * 16])
        C.gsm_col_in = din("gsm_col", [8, max(NO, 1) * 2])
    C.outT = P.dram("outT", [D, S], F32, kind="ExternalOutput")
    C.hT = P.dram("hT", [D, S], F32)
    C.w_in_bf = [[P.dram(f"w_in_bf_{l}_{j}", [D, 2 * FF], BF16) for j in range(2)] for l in range(depth)]
    C.w_out_bf = [[P.dram(f"w_out_bf_{l}_{j}", [FF, D], BF16) for j in range(2)] for l in range(depth)]
    if mixers:
        C.Wh = [P.dram(f"Wh_{i}", [D, HYB_IN], BF16) for i in range(NE)]
        C.Whp = [P.dram(f"Whp_{i}", [D, 1024], BF16) for i in range(NE)]
        C.Who = [P.dram(f"Who_{i}", [D, D], BF16) for i in range(NE)]
        C.qT_d = P.dram("qT_d", [AW, S], BF16)
        C.kT_d = P.dram("kT_d", [AW, S], BF16)
        C.v_d = P.dram("v_d", [S, AW], BF16)
        C.gluT_d = P.dram("gluT_d", [512, PADL + S], BF16)
        C.catT_d = P.dram("catT_d", [D, S], BF16)
        C.cosT = P.dram("cosT", [128, S], F32)
        C.sinT = P.dram("sinT", [128, S], F32)
        C.den_d = P.dram("den_d", [1, S], F32)
        C.Wg = [P.dram(f"Wg_{i}", [D, GDN_IN], BF16) for i in range(NO)]
        C.Wgo = [P.dram(f"Wgo_{i}", [D, D], BF16) for i in range(NO)]
        C.qT_g = P.dram("qT_g", [D, S], BF16)
        C.kT_g = P.dram("kT_g", [D, S], BF16)
        C.vT_g = P.dram("vT_g", [D, S], BF16)
        C.szT_g = P.dram("szT_g", [D, S], BF16)
        C.gc_d = P.dram("gc_d", [8, S], F32)
    if dbg:
        C.dbg = {k: P.dram("dbg_" + k, shp, dt, kind="ExternalOutput") for k, (shp, dt) in dbg.items()}

    C.PS = [P.ps(f"ps{i}", [128, 512], F32) for i in range(8)]
    C.gains = P.sb("gains", [128, depth * 24 + 8], F32)
    C.consts = P.sb("consts", [128, 2048], F32)
    C.ones_bf = P.sb("ones_bf", [128, 128], BF16)
    C.ident_bf = P.sb("ident_bf", [128, 128], BF16)
    C.mask4 = P.sb("mask4", [128, 512], BF16)
    C.cst = P.sb("cst", [128, 8], F32)
    P.dma(C.gains.t[:, :], gains_in[:, :], writes=[C.gains])
    P.dma(C.consts.t[:, :], consts_in[:, :], writes=[C.consts])
    P.memset("pool", C.cst.t[:, 0:1], EPS, writes=[C.cst])
    P.memset("pool", C.cst.t[:, 1:2], -float(np.pi), writes=[C.cst])
    P.memset("pool", C.cst.t[:, 2:3], 1.0, writes=[C.cst])
    P.memset("pool", C.ones_bf.t[:, :], 1.0, writes=[C.ones_bf])
    P.copy("pool", C.ident_bf.t[:, :], C.consts.t[:, 0:128], reads=[C.consts], writes=[C.ident_bf])
    P.copy("pool", C.mask4.t[:, :], C.consts.t[:, 128:640], reads=[C.consts], writes=[C.mask4])
    if mixers:
        C.hyb_dww = P.sb("hyb_dww", [128, NE * 4 * CONVW], F32)
        C.hvec = P.sb("hvec", [128, NE * 12], F32)
        P.dma(C.hyb_dww.t[:, :], hyb_dww_in[:, :], writes=[C.hyb_dww])
        P.dma(C.hvec.t[:, :], hvec_in[:, :], writes=[C.hvec])
        C.gcw = P.sb("gcw", [128, max(NO, 1) * 96], F32)
        C.gng = P.sb("gng", [128, max(NO, 1)], F32)
        P.dma(C.gcw.t[:, :], gcw_in[:, :], writes=[C.gcw])
        P.dma(C.gng.t[:, :], gng_in[:, :], writes=[C.gng])
    persist = P.sb_mark()

    def alloc_cv():
        C.cv32 = [P.sb(f"cv32_{i}", [128, 1024], F32) for i in range(3)]
        C.cv16 = [P.sb(f"cv16_{i}", [128, 1024], BF16) for i in range(2)]

    alloc_cv()

    P.dma(C.hT.t[:, :], xT[:, :], writes=[C.hT])

    C.cvi = 0
    C.cvj = 0

    def jobs_ffn(l, j):
        return cast_jobs(ffn_w_in[l][j], C.w_in_bf[l][j], D, 2 * FF) + cast_jobs(ffn_w_out[l][j], C.w_out_bf[l][j], FF, D)

    def jobs_mixer(l):
        if not mixers:
            return []
        i = l // 2
        if l % 2 == 0:
            return cast_jobs(hyb_w_in[i], C.Wh[i], D, HYB_IN) + perm_jobs(hyb_w_in[i], C.Whp[i]) + cast_jobs(hyb_w_out[i], C.Who[i], D, D)
        return cast_jobs(gdn_w_in[i], C.Wg[i], D, GDN_IN) + cast_jobs(gdn_w_out[i], C.Wgo[i], D, D)

    run_cast_jobs(P, C, jobs_ffn(0, 0))

    if mixers and NE > 0:
        zt = C.cv16[0]
        P.memset("pool", zt.t[:, 0:PADL], 0.0, writes=[zt])
        for cc in range(4):
            P.dma(C.gluT_d.t[cc * 128:(cc + 1) * 128, 0:PADL], zt.t[:, 0:PADL], reads=[zt], writes=[(C.gluT_d, ("pad", cc))])
        P.barrier()
        rope_tables(P, C, S, pos)

    def alloc_ffn():
        alloc_cv()
        C.htiles = [P.sb(f"htile{i}", [128, 8, T], F32) for i in range(2)]
        C.xn = P.sb("xn", [128, 8, T], BF16)
        C.gT = P.sb("gT", [128, NFC, T], BF16)
        C.sq = P.sb("sq", [128, 8, 512], BF16)
        C.rstd = P.sb("rstd", [128, 512], F32)
        C.sg = [P.sb(f"sg{i}", [128, 512], F32) for i in range(2)]
        C.wg = [P.sb(f"wg{i}", [128, 8, 256], BF16) for i in range(2)]
        C.wu = [P.sb(f"wu{i}", [128, 8, 256], BF16) for i in range(2)]
        C.wo = [P.sb(f"wo{i}", [128, NFC, 256], BF16) for i in range(2)]

    for l in range(depth):
        P.barrier()
        P.sb_release(persist)
        alloc_ffn()
        ffn_phase(P, C, S, T, C.gains.t[:, l * 24:l * 24 + 8], C.w_in_bf[l][0], C.w_out_bf[l][0],
                  jobs=jobs_mixer(l) + jobs_ffn(l, 1))
        if mixers:
            P.barrier()
            P.sb_release(persist)
            gmix = C.gains.t[:, l * 24 + 8:l * 24 + 16]
            if l % 2 == 0:
                i = l // 2
                hyb_proj_phase(P, C, S, T, gmix, C.Wh[i], C.Whp[i], None, l)
                hyb_conv_phase(P, C, S, C.hvec.t[:, i * 12:(i + 1) * 12], C.hyb_dww.t[:, i * 4 * CONVW:(i + 1) * 4 * CONVW])
                hyb_attn_phase(P, C, S)
                mix_out_phase(P, C, S, T, C.Who[i])
            else:
                gdn_layer(P, C, S, T, gmix, l // 2)
            P.sb_release(persist)
            alloc_ffn()
        ffn_phase(P, C, S, T, C.gains.t[:, l * 24 + 16:l * 24 + 24], C.w_in_bf[l][1], C.w_out_bf[l][1],
                  jobs=(jobs_ffn(l + 1, 0) if l + 1 < depth else None))

    P.barrier()
    P.sb_release(persist)
    C.htile = P.sb("htile", [128, 8, T], F32)
    C.fo = P.sb("fo", [128, 8, T], F32)
    C.sq = P.sb("sq", [128, 8, 512], BF16)
    C.rstd = P.sb("rstd", [128, 512], F32)
    final_phase(P, C, S, T, C.gains.t[:, depth * 24:depth * 24 + 8], C.outT)
    P.emit()
    return nc, P


def make_consts():
    f32 = np.float32
    c = np.zeros((128, 2048), f32)
    c[:, 0:128] = np.eye(128, dtype=f32)
    k = np.arange(128)[:, None]
    q = np.arange(128)[None, :]
    cur = (q >= k).astype(f32)
    prev = (q <= k).astype(f32)
    c[:, 128:640] = np.concatenate([cur, prev, cur, prev], axis=1)
    inv = np.power(f32(500000.0), -np.arange(0, 16, 2, dtype=f32) / f32(16)).astype(f32)
    for p in range(128):
        i = p % 64
        c[p, 640] = inv[i % 8] if i < 16 else 0.0
    same = (k // 64) == (q // 64)
    c[:, 1024:1152] = ((q >= k) & same).astype(f32)
    c[:, 1152:1280] = (k < 64).astype(f32) * np.ones((1, 128), f32)
    c[:, 1280:1408] = (k >= 64).astype(f32) * np.ones((1, 128), f32)
    c[:, 1408:1536] = -((k > q) & same).astype(f32)
    return c


def host_inputs(inputs, S=4096, depth=4, mixers=True):
    f32 = np.float32
    NE = (depth + 1) // 2
    x = np.asarray(inputs["x"], dtype=f32)
    B = x.shape[0]
    gains = np.zeros((128, depth * 24 + 8), f32)
    for l in range(depth):
        for j, nm in enumerate(("ffn1_norm", "mix_norm", "ffn2_norm")):
            gains[:, l * 24 + j * 8:l * 24 + j * 8 + 8] = np.asarray(inputs[nm][l], f32).reshape(8, 128).T
    gains[:, depth * 24:] = np.asarray(inputs["final_norm"], f32).reshape(8, 128).T
    shared = {"gains": gains, "consts": make_consts()}
    for l in range(depth):
        for j in range(2):
            shared[f"ffn{j + 1}_w_in_{l}"] = np.ascontiguousarray(inputs[f"ffn{j + 1}_w_in"][l], dtype=f32)
            shared[f"ffn{j + 1}_w_out_{l}"] = np.ascontiguousarray(inputs[f"ffn{j + 1}_w_out"][l], dtype=f32)
    if mixers:
        dww = np.zeros((128, NE * 4 * CONVW), f32)
        hvec = np.zeros((128, NE * 12), f32)
        for i in range(NE):
            shared[f"hyb_w_in_{i}"] = np.ascontiguousarray(inputs["hyb_w_in"][i], dtype=f32)
            shared[f"hyb_w_out_{i}"] = np.ascontiguousarray(inputs["hyb_w_out"][i], dtype=f32)
            w = np.asarray(inputs["hyb_dw_w"][i], f32)
            for cc in range(4):
                dww[:, (i * 4 + cc) * CONVW:(i * 4 + cc + 1) * CONVW] = w[:, cc * 128:(cc + 1) * 128].T
            for j, nm in enumerate(("hyb_dw_b", "hyb_ln_g", "hyb_ln_b")):
                hvec[:, i * 12 + j * 4:i * 12 + j * 4 + 4] = np.asarray(inputs[nm][i], f32).reshape(4, 128).T
        shared["hyb_dww"] = dww
        shared["hvec"] = hvec
        NO = depth // 2
        gcw = np.zeros((128, max(NO, 1) * 96), f32)
        gng = np.zeros((128, max(NO, 1)), f32)
        grow = np.zeros((1, max(NO, 1) * 16), f32)
        gcolm = np.zeros((8, max(NO, 1) * 2), f32)
        for i in range(NO):
            shared[f"gdn_w_in_{i}"] = np.ascontiguousarray(inputs["gdn_w_in"][i], dtype=f32)
            shared[f"gdn_w_out_{i}"] = np.ascontiguousarray(inputs["gdn_w_out"][i], dtype=f32)
            cw = np.asarray(inputs["gdn_conv_w"][i], f32)
            for ch in range(24):
                gcw[:, i * 96 + ch * 4:i * 96 + ch * 4 + 4] = cw[:, ch * 128:(ch + 1) * 128].T
            gng[:, i] = np.asarray(inputs["gdn_norm_g"][i], f32)
            grow[0, i * 16:i * 16 + 8] = np.asarray(inputs["gdn_A_log"][i], f32)
            grow[0, i * 16 + 8:i * 16 + 16] = np.asarray(inputs["gdn_dt_bias"][i], f32)
            gcolm[:, i * 2] = np.asarray(inputs["gdn_A_log"][i], f32)
            gcolm[:, i * 2 + 1] = np.asarray(inputs["gdn_dt_bias"][i], f32)
        shared["gcw"] = gcw
        shared["gng"] = gng
        shared["gsm_row"] = grow
        shared["gsm_col"] = gcolm
    maps = []
    for b in range(B):
        m = dict(shared)
        m["xT"] = np.ascontiguousarray(x[b, :S].T)
        m["pos"] = np.ascontiguousarray(np.asarray(inputs["positions"])[b:b + 1, :S].astype(np.int32))
        maps.append(m)
    return maps


_CACHE = {}


def kernel(**inputs):
    S, depth = 4096, 4
    maps = host_inputs(inputs, S, depth)
    nc, P = build(S, depth)
    res = run_bass_kernel_spmd(nc, maps, core_ids=list(range(NCORES)))
    outs = [np.asarray(r["outT"]) for r in res.results]
    out = np.stack([o.T for o in outs], axis=0)
    return np.ascontiguousarray(out.astype(np.float32))
```

```python
from contextlib import ExitStack
import concourse.bass as bass
import concourse.mybir as mybir

F32 = mybir.dt.float32
BF16 = mybir.dt.bfloat16
I32 = mybir.dt.int32
AF = mybir.ActivationFunctionType
ALU = mybir.AluOpType
AX = mybir.AxisListType

import os as _os0
ANNOTATE = bool(int(_os0.environ.get('KANNOT', '0')))
ENGS = ("pe", "act", "dve", "pool", "sp")
EP = 30000
NSLOT = 8
DMA_EP = 1800


class Op:
    __slots__ = ("eng", "idx", "fn", "deps", "signal", "sig", "dma", "slot", "epoch", "val", "phase")

    def __init__(self, eng, fn, dma):
        self.eng = eng
        self.fn = fn
        self.dma = dma
        self.deps = []
        self.signal = False
        self.sig = 0


class Buf:
    def __init__(self, name, t):
        self.name = name
        self.t = t
        self.e = {}

    def __getitem__(self, k):
        return self.t[k]

    def _ents(self, key):
        if key is None:
            return list(self.e.values())
        return [self.e[k] for k in (key, None) if k in self.e]

    def read(self, op, key):
        deps = [en[0] for en in self._ents(key) if en[0] is not None]
        en = self.e.setdefault(key, [None, {}, []])
        if op.dma:
            en[2].append(op)
        else:
            en[1][op.eng] = op
        return deps

    def write(self, op, key):
        deps = []
        for en in self._ents(key):
            if en[0] is not None:
                deps.append(en[0])
            deps.extend(en[1].values())
            deps.extend(en[2])
        if key is None:
            self.e = {None: [op, {}, []]}
        else:
            self.e[key] = [op, {}, []]
        return deps


class Prog:
    def __init__(self, nc):
        self.nc = nc
        self.ops = {e: [] for e in ENGS}
        self.stack = ExitStack()
        self.nsem = 0
        self.dq = {e: {"n": 0, "last": [None] * NSLOT, "uses": [0] * NSLOT, "epoch": [0] * NSLOT} for e in ENGS}
        self._esem = {}
        self._dsem = {}
        self.ntile = 0

    SB_LO = 16512
    SB_HI = 229376

    def sb(self, name, shape, dtype):
        self.ntile += 1
        isz = 2 if dtype == BF16 else 4
        n = 1
        for d in shape[1:]:
            n *= d
        nbytes = (n * isz + 63) // 64 * 64
        off = getattr(self, "sb_ptr", self.SB_LO)
        assert off + nbytes <= self.SB_HI, f"SBUF overflow allocating {name}: {off}+{nbytes}"
        self.sb_ptr = off + nbytes
        self.sb_peak = max(getattr(self, "sb_peak", 0), self.sb_ptr)
        t = self.nc.alloc_sbuf_tensor_at(f"{name}_{self.ntile}", list(shape), dtype, offset=off)
        return Buf(name, t)

    def sb_mark(self):
        return getattr(self, "sb_ptr", self.SB_LO)

    def sb_release(self, mark):
        self.sb_ptr = mark

    def barrier(self):
        lasts = []
        for e in ENGS:
            for op in reversed(self.ops[e]):
                if not op.dma and op.fn is not None:
                    lasts.append(op)
                    break
            q = self.dq[e]
            for s_ in range(NSLOT):
                if q["last"][s_] is not None:
                    lasts.append(q["last"][s_])
        for e in ENGS:
            op = Op(e, None, False)
            op.phase = None
            op.idx = len(self.ops[e])
            for d in lasts:
                if (not d.dma) and d.eng == e:
                    continue
                op.deps.append(d)
                if not d.dma:
                    d.signal = True
            self.ops[e].append(op)

    def ps(self, name, shape, dtype=F32):
        self.ntile += 1
        t = self.stack.enter_context(self.nc.psum_tensor(f"{name}_{self.ntile}", list(shape), dtype))
        return Buf(name, t)

    def dram(self, name, shape, dtype, kind="Internal"):
        t = self.nc.dram_tensor(name, list(shape), dtype, kind=kind)
        return Buf(name, t)

    def _sem(self, name):
        self.nsem += 1
        return self.stack.enter_context(self.nc.semaphore(f"{name}_{self.nsem}"))

    def add(self, eng, fn, reads=(), writes=(), dma=False):
        op = Op(eng, fn, dma)
        op.idx = len(self.ops[eng])
        op.phase = getattr(self, "phase", None)
        deps = []
        for r in reads:
            b, k = r if isinstance(r, tuple) else (r, None)
            if b is not None:
                deps.extend(b.read(op, k))
        for w in writes:
            b, k = w if isinstance(w, tuple) else (w, None)
            if b is not None:
                deps.extend(b.write(op, k))
        if dma:
            q = self.dq[eng]
            s = q["n"] % NSLOT
            q["n"] += 1
            if q["last"][s] is not None:
                deps.append(q["last"][s])
            if q["uses"][s] >= DMA_EP:
                q["uses"][s] = 0
                q["epoch"][s] += 1
            q["uses"][s] += 1
            op.slot = s
            op.epoch = q["epoch"][s]
            op.val = 16 * q["uses"][s]
            q["last"][s] = op
        seen = set()
        for d in deps:
            if d is op or id(d) in seen:
                continue
            seen.add(id(d))
            if (not d.dma) and d.eng == "pe" and eng == "pe" and not dma:
                continue
            op.deps.append(d)
            if not d.dma:
                d.signal = True
        self.ops[eng].append(op)
        return op

    def dma(self, out, in_, reads=(), writes=(), eng="sp", **kw):
        return self.add(eng, lambda h: h.dma_start(out=out, in_=in_, **kw), reads, writes, dma=True)

    def mm(self, out, lhsT, rhs, start=True, stop=True, reads=(), writes=(), **kw):
        return self.add("pe", lambda h: h.matmul(out, lhsT, rhs, start=start, stop=stop, **kw), reads, writes)

    def tr(self, out, in_, ident, reads=(), writes=()):
        return self.add("pe", lambda h: h.transpose(out, in_, ident), reads, writes)

    def act(self, out, in_, func, reads=(), writes=(), eng="act", **kw):
        return self.add(eng, lambda h: h.activation(out=out, in_=in_, func=func, **kw), reads, writes)

    def tt(self, eng, out, in0, in1, op, reads=(), writes=()):
        return self.add(eng, lambda h: h.tensor_tensor(out=out, in0=in0, in1=in1, op=op), reads, writes)

    def ts(self, eng, out, in0, s1, s2, op0, op1=None, reads=(), writes=()):
        if op1 is None:
            return self.add(eng, lambda h: h.tensor_scalar(out=out, in0=in0, scalar1=s1, scalar2=None, op0=op0), reads, writes)
        return self.add(eng, lambda h: h.tensor_scalar(out=out, in0=in0, scalar1=s1, scalar2=s2, op0=op0, op1=op1), reads, writes)

    def stt(self, eng, out, in0, scalar, in1, op0, op1, reads=(), writes=()):
        return self.add(eng, lambda h: h.scalar_tensor_tensor(out=out, in0=in0, scalar=scalar, in1=in1, op0=op0, op1=op1), reads, writes)

    def copy(self, eng, out, in_, reads=(), writes=()):
        if eng == "act":
            return self.add(eng, lambda h: h.copy(out=out, in_=in_), reads, writes)
        return self.add(eng, lambda h: h.tensor_copy(out=out, in_=in_), reads, writes)

    def memset(self, eng, ap, val, writes=()):
        return self.add(eng, lambda h: h.memset(ap, val), (), writes)

    def esem(self, e, ep):
        k = (e, ep)
        if k not in self._esem:
            self._esem[k] = self._sem(f"e_{e}{ep}")
        return self._esem[k]

    def dsem(self, e, slot, ep):
        k = (e, slot, ep)
        if k not in self._dsem:
            self._dsem[k] = self._sem(f"d_{e}{slot}_{ep}")
        return self._dsem[k]

    def emit(self):
        nc = self.nc
        for e in ENGS:
            n = 0
            for op in self.ops[e]:
                if (not op.dma) and op.signal:
                    n += 1
                    op.sig = n
        for e in ENGS:
            for op in self.ops[e]:
                if op.dma:
                    self.dsem(e, op.slot, op.epoch)
                elif op.signal:
                    self.esem(e, (op.sig - 1) // EP)
        final = []
        for e in ENGS:
            q = self.dq[e]
            for s in range(NSLOT):
                if q["last"][s] is not None:
                    o = q["last"][s]
                    final.append((self.dsem(e, o.slot, o.epoch), o.val))
        P = self

        def run(e, h):
            weng = {}
            wdma = {}
            for op in P.ops[e]:
                for d in op.deps:
                    if d.dma:
                        k = (d.eng, d.slot, d.epoch)
                        if wdma.get(k, 0) < d.val:
                            h.wait_ge(P.dsem(*k), d.val)
                            wdma[k] = d.val
                    else:
                        if weng.get(d.eng, 0) < d.sig:
                            ep, v = divmod(d.sig - 1, EP)
                            h.wait_ge(P.esem(d.eng, ep), v + 1)
                            weng[d.eng] = d.sig
                if op.fn is None:
                    continue
                ins = op.fn(h)
                if ANNOTATE and op.phase:
                    ins.annotate(op.phase)
                if op.dma:
                    ins.then_inc(P.dsem(e, op.slot, op.epoch), 16)
                elif op.signal:
                    ins.then_inc(P.esem(e, (op.sig - 1) // EP), 1)
            if e == "sp":
                for sem, v in final:
                    h.wait_ge(sem, v)

        with nc.Block() as block:
            @block.tensor
            def _(h):
                run("pe", h)

            @block.scalar
            def _(h):
                run("act", h)

            @block.vector
            def _(h):
                run("dve", h)

            @block.gpsimd
            def _(h):
                run("pool", h)

            @block.sync
            def _(h):
                run("sp", h)

    def close(self):
        self.stack.close()

import numpy as np
from concourse.bass_utils import run_bass_kernel_spmd

D = 1024
FF = 2816
NFC = FF // 128
EPS = 1e-6
NCORES = 8


class Ctx:
    pass


def cast_jobs(src_ap, dst, R, Cc):
    CB = 1024
    jobs = []
    for r0 in range(0, R, 128):
        rr = min(128, R - r0)
        for c0 in range(0, Cc, CB):
            cc = min(CB, Cc - c0)
            jobs.append(("plain", src_ap, dst, r0, rr, c0, cc))
    return jobs


def perm_jobs(src_ap, dst):
    return [("perm", src_ap, dst, r0, 128, 0, 1024) for r0 in range(0, D, 128)]


def run_cast_jobs(P, C, jobs, eng="pool"):
    LA = len(C.cv32) - 1
    n = len(jobs)
    for i in range(n + LA):
        if i < n:
            kind, src_ap, dst, r0, rr, c0, cc = jobs[i]
            a = C.cv32[C.cvi % len(C.cv32)]
            jobs[i] = jobs[i] + (a,)
            C.cvi += 1
            P.dma(a.t[0:rr, 0:cc], src_ap[r0:r0 + rr, c0:c0 + cc], writes=[a], eng=eng)
        j = i - LA
        if j >= 0:
            kind, src_ap, dst, r0, rr, c0, cc, a = jobs[j]
            b = C.cv16[C.cvj % len(C.cv16)]
            C.cvj += 1
            if kind == "plain":
                P.copy("pool", b.t[0:rr, 0:cc], a.t[0:rr, 0:cc], reads=[a], writes=[b])
            else:
                av = a.t[:, 0:1024].rearrange("p (h e) -> p h e", e=64)
                bv = b.t[:, 0:1024].rearrange("p (h e) -> p h e", e=64)
                P.memset("pool", b.t[:, 0:1024], 0.0, writes=[b])
                P.ts("pool", bv[:, :, 0:8], av[:, :, 8:16], -1.0, None, ALU.mult, reads=[a], writes=[b])
                P.copy("pool", bv[:, :, 8:16], av[:, :, 0:8], reads=[a], writes=[b])
            P.dma(dst.t[r0:r0 + rr, c0:c0 + cc], b.t[0:rr, 0:cc], reads=[b], writes=[(dst, ("r", r0, c0))], eng=eng)


def rmsnorm_tile(P, C, src, gcol, dst, ntok, srckey=None, doff=0):
    for s0 in range(0, ntok, 512):
        n = min(512, ntok - s0)
        sl = slice(s0, s0 + n)
        ps = C.PS[7]
        for c in range(8):
            P.act(C.sq.t[:, c, 0:n], src.t[:, c, sl], AF.Square, reads=[(src, srckey)], writes=[(C.sq, c)])
        for c in range(8):
            P.mm(ps.t[:, 0:n], C.ones_bf.t[:, :], C.sq.t[:, c, 0:n], start=(c == 0), stop=(c == 7),
                 reads=[(C.sq, c), C.ones_bf], writes=[ps])
        P.act(C.rstd.t[:, 0:n], ps.t[:, 0:n], AF.Ln, reads=[ps, C.cst], writes=[C.rstd], scale=1.0 / D, bias=C.cst.t[:, 0:1])
        P.act(C.rstd.t[:, 0:n], C.rstd.t[:, 0:n], AF.Exp, reads=[C.rstd], writes=[C.rstd], scale=-0.5)
        for c in range(8):
            P.stt("dve", dst.t[:, c, doff + s0:doff + s0 + n], src.t[:, c, sl], gcol[:, c:c + 1], C.rstd.t[:, 0:n], ALU.mult, ALU.mult,
                  reads=[(src, srckey), C.rstd, C.gains], writes=[(dst, (c, doff + s0))])


def ffn_phase(P, C, S, T, gcol, w_in, w_out, jobs=None):
    P.phase = "ffn_phase"
    hview = C.hT.t.rearrange("(c p) t -> p c t", p=128)
    WB = 256
    nwb = FF // WB
    wi = 0
    wo_i = 0
    if jobs:
        run_cast_jobs(P, C, jobs)
    for tt in range(S // T):
        tsl = slice(tt * T, (tt + 1) * T)
        htile = C.htiles[tt % 2]
        P.dma(htile.t[:, :, :], hview[:, :, tsl], reads=[(C.hT, tt)], writes=[htile])
        rmsnorm_tile(P, C, htile, gcol, C.xn, T)
        for wb in range(nwb):
            c0 = wb * WB
            cw = min(WB, FF - c0)
            wg = C.wg[wi % 2]
            wu = C.wu[wi % 2]
            wi += 1
            P.dma(wg.t[:, :, 0:cw], w_in.t[:, c0:c0 + cw].rearrange("(c p) f -> p c f", p=128), reads=[w_in], writes=[wg])
            P.dma(wu.t[:, :, 0:cw], w_in.t[:, FF + c0:FF + c0 + cw].rearrange("(c p) f -> p c f", p=128), reads=[w_in], writes=[wu])
            for fi in range(cw // 128):
                f = c0 // 128 + fi
                fs = slice(fi * 128, (fi + 1) * 128)
                for s0 in range(0, T, 512):
                    sl = slice(s0, s0 + 512)
                    k = (f * (T // 512) + s0 // 512) % 2
                    pg = C.PS[k]
                    pu = C.PS[2 + k]
                    sg = C.sg[k]
                    for c in range(8):
                        P.mm(pg.t[:, :], wg.t[:, c, fs], C.xn.t[:, c, sl], start=(c == 0), stop=(c == 7),
                             reads=[wg, C.xn], writes=[pg])
                    for c in range(8):
                        P.mm(pu.t[:, :], wu.t[:, c, fs], C.xn.t[:, c, sl], start=(c == 0), stop=(c == 7),
                             reads=[wu, C.xn], writes=[pu])
                    P.act(sg.t[:, :], pg.t[:, :], AF.Silu, reads=[pg], writes=[sg])
                    P.tt("dve", C.gT.t[:, f, sl], sg.t[:, :], pu.t[:, :], ALU.mult, reads=[sg, pu], writes=[(C.gT, (f, s0))])
        OB = 256
        for ob in range(D // OB):
            wo = C.wo[wo_i % 2]
            wo_i += 1
            P.dma(wo.t[:, :, :], w_out.t[:, ob * OB:(ob + 1) * OB].rearrange("(f p) d -> p f d", p=128), reads=[w_out], writes=[wo])
            for di in range(OB // 128):
                dc = ob * (OB // 128) + di
                for s0 in range(0, T, 512):
                    sl = slice(s0, s0 + 512)
                    k = (dc * (T // 512) + s0 // 512) % 2
                    py = C.PS[4 + k]
                    for f in range(NFC):
                        P.mm(py.t[:, :], wo.t[:, f, di * 128:(di + 1) * 128], C.gT.t[:, f, sl], start=(f == 0), stop=(f == NFC - 1),
                             reads=[wo, C.gT], writes=[py])
                    P.stt("dve", htile.t[:, dc, sl], py.t[:, :], 0.5, htile.t[:, dc, sl], ALU.mult, ALU.add,
                          reads=[py, (htile, (dc, s0))], writes=[(htile, (dc, s0))])
        P.dma(hview[:, :, tsl], htile.t[:, :, :], reads=[htile], writes=[(C.hT, tt)])


def final_phase(P, C, S, T, gcol, outT):
    P.phase = "final_phase"
    hview = C.hT.t.rearrange("(c p) t -> p c t", p=128)
    oview = outT.t.rearrange("(c p) t -> p c t", p=128)
    for tt in range(S // T):
        tsl = slice(tt * T, (tt + 1) * T)
        P.dma(C.htile.t[:, :, :], hview[:, :, tsl], reads=[(C.hT, tt)], writes=[C.htile])
        rmsnorm_tile(P, C, C.htile, gcol, C.fo, T)
        P.dma(oview[:, :, tsl], C.fo.t[:, :, :], reads=[C.fo], writes=[(outT, tt)])


AW = 512
HYB_IN = 2560
CONVW = 31
PADL = 32
DIL = ((128, 1), (512, 4), (2048, 16))


def rope_tables(P, C, S, pos_ap):
    P.phase = "rope_tables"
    m = P.sb_mark()
    posi = P.sb("posi", [128, S], I32)
    ang = P.sb("ang", [128, S], F32)
    tmp = P.sb("tmp", [128, S], F32)
    res = P.sb("res", [128, S], F32)
    P.dma(posi.t[:, :], pos_ap[0:1, :].to_broadcast([128, S]), writes=[posi])
    P.copy("dve", ang.t[:, :], posi.t[:, :], reads=[posi], writes=[ang])
    P.ts("dve", ang.t[:, :], ang.t[:, :], C.consts.t[:, 640:641], None, ALU.mult, reads=[ang, C.consts], writes=[ang])
    PI = float(np.pi)
    ki = posi
    for shift, dst in ((0.5 * PI, C.cosT), (0.0, C.sinT)):
        P.ts("dve", tmp.t[:, :], ang.t[:, :], shift, None, ALU.add, reads=[ang], writes=[tmp])
        P.ts("dve", ki.t[:, :], tmp.t[:, :], 1.0 / (2 * PI), 0.5, ALU.mult, ALU.add, reads=[tmp], writes=[ki])
        P.copy("dve", res.t[:, :], ki.t[:, :], reads=[ki], writes=[res])
        P.stt("dve", tmp.t[:, :], res.t[:, :], -2 * PI, tmp.t[:, :], ALU.mult, ALU.add, reads=[res, tmp], writes=[tmp])
        P.ts("dve", res.t[:, :], tmp.t[:, :], -PI, 2 * PI, ALU.is_lt, ALU.mult, reads=[tmp], writes=[res])
        P.tt("dve", tmp.t[:, :], tmp.t[:, :], res.t[:, :], ALU.add, reads=[tmp, res], writes=[tmp])
        P.ts("dve", res.t[:, :], tmp.t[:, :], PI, -2 * PI, ALU.is_gt, ALU.mult, reads=[tmp], writes=[res])
        P.tt("dve", tmp.t[:, :], tmp.t[:, :], res.t[:, :], ALU.add, reads=[tmp, res], writes=[tmp])
        P.ts("dve", tmp.t[:, :], tmp.t[:, :], PI, -PI, ALU.min, ALU.max, reads=[tmp], writes=[tmp])
        P.act(res.t[:, :], tmp.t[:, :], AF.Sin, reads=[tmp], writes=[res])
        P.dma(dst.t[:, :], res.t[:, :], reads=[res], writes=[dst])
    P.barrier()
    P.sb_release(m)


def hyb_proj_phase(P, C, S, T, gcol, Wh, Whp, hv, l):
    P.phase = "hyb_proj_phase"
    m = P.sb_mark()
    C.htile = P.sb("htile", [128, 8, T], F32)
    C.xn = P.sb("xn", [128, 8, T], BF16)
    C.sq = P.sb("sq", [128, 8, 512], BF16)
    C.rstd = P.sb("rstd", [128, 512], F32)
    cos = P.sb("cos", [128, T], F32)
    sin = P.sb("sin", [128, T], F32)
    wa = [P.sb(f"wa{i}", [128, 8, 512], BF16) for i in range(2)]
    wb = [P.sb(f"wb{i}", [128, 8, 512], BF16) for i in range(2)]
    wv = P.sb("wv", [128, 8, 512], BF16)
    wn = 0
    t1 = [P.sb(f"t1{i}", [128, 512], F32) for i in range(2)]
    t2 = [P.sb(f"t2{i}", [128, 512], F32) for i in range(2)]
    ob = [P.sb(f"ob{i}", [128, T], BF16) for i in range(2)]
    vt = P.sb("vt", [128, T // 128, 512], BF16)
    hview = C.hT.t.rearrange("(c p) t -> p c t", p=128)
    n = 0
    oi = 0
    for tt in range(S // T):
        tsl = slice(tt * T, (tt + 1) * T)
        P.dma(C.htile.t[:, :, :], hview[:, :, tsl], reads=[(C.hT, tt)], writes=[C.htile])
        rmsnorm_tile(P, C, C.htile, gcol, C.xn, T)
        P.dma(cos.t[:, :], C.cosT.t[:, tsl], reads=[C.cosT], writes=[cos])
        P.dma(sin.t[:, :], C.sinT.t[:, tsl], reads=[C.sinT], writes=[sin])
        for qk in range(2):
            w1 = wa[wn % 2]
            w2 = wb[wn % 2]
            wn += 1
            P.dma(w1.t[:, :, :], Wh.t[:, qk * 512:(qk + 1) * 512].rearrange("(c p) f -> p c f", p=128), reads=[Wh], writes=[w1])
            P.dma(w2.t[:, :, :], Whp.t[:, qk * 512:(qk + 1) * 512].rearrange("(c p) f -> p c f", p=128), reads=[Whp], writes=[w2])
            for ch in range(4):
                cs_ = slice(ch * 128, (ch + 1) * 128)
                o = ob[oi % 2]
                oi += 1
                for s0 in range(0, T, 512):
                    sl = slice(s0, s0 + 512)
                    k = n % 2
                    n += 1
                    pa, pb = C.PS[k], C.PS[2 + k]
                    for c in range(8):
                        P.mm(pa.t[:, :], w1.t[:, c, cs_], C.xn.t[:, c, sl], start=(c == 0), stop=(c == 7), reads=[w1, C.xn], writes=[pa])
                    for c in range(8):
                        P.mm(pb.t[:, :], w2.t[:, c, cs_], C.xn.t[:, c, sl], start=(c == 0), stop=(c == 7), reads=[w2, C.xn], writes=[pb])
                    P.tt("dve", t1[k].t[:, :], pa.t[:, :], cos.t[:, sl], ALU.mult, reads=[pa, cos], writes=[t1[k]])
                    P.tt("dve", t2[k].t[:, :], pb.t[:, :], sin.t[:, sl], ALU.mult, reads=[pb, sin], writes=[t2[k]])
                    P.tt("pool", o.t[:, sl], t1[k].t[:, :], t2[k].t[:, :], ALU.add, reads=[t1[k], t2[k]], writes=[(o, s0)])
                dst = C.qT_d if qk == 0 else C.kT_d
                P.dma(dst.t[ch * 128:(ch + 1) * 128, tsl], o.t[:, :], reads=[o], writes=[(dst, (ch, tt))])
        P.dma(wv.t[:, :, :], Wh.t[:, 1024:1536].rearrange("(c p) f -> p c f", p=128), reads=[Wh], writes=[wv])
        for blk in range(T // 128):
            pv = C.PS[4 + blk % 2]
            bs = slice(blk * 128, (blk + 1) * 128)
            for c in range(8):
                P.mm(pv.t[:, :], C.xn.t[:, c, bs], wv.t[:, c, :], start=(c == 0), stop=(c == 7), reads=[wv, C.xn], writes=[pv])
            P.copy("act", vt.t[:, blk, :], pv.t[:, :], reads=[pv], writes=[(vt, blk)])
        P.dma(C.v_d.t[tsl, :].rearrange("(b p) f -> p b f", p=128), vt.t[:, :, :], reads=[vt], writes=[(C.v_d, tt)])
        w1 = wa[wn % 2]
        w2 = wb[wn % 2]
        wn += 1
        P.dma(w1.t[:, :, :], Wh.t[:, 1536:2048].rearrange("(c p) f -> p c f", p=128), reads=[Wh], writes=[w1])
        P.dma(w2.t[:, :, :], Wh.t[:, 2048:2560].rearrange("(c p) f -> p c f", p=128), reads=[Wh], writes=[w2])
        for ch in range(4):
            cs_ = slice(ch * 128, (ch + 1) * 128)
            o = ob[oi % 2]
            oi += 1
            for s0 in range(0, T, 512):
                sl = slice(s0, s0 + 512)
                k = n % 2
                n += 1
                pa, pb = C.PS[k], C.PS[2 + k]
                for c in range(8):
                    P.mm(pa.t[:, :], w1.t[:, c, cs_], C.xn.t[:, c, sl], start=(c == 0), stop=(c == 7), reads=[w1, C.xn], writes=[pa])
                for c in range(8):
                    P.mm(pb.t[:, :], w2.t[:, c, cs_], C.xn.t[:, c, sl], start=(c == 0), stop=(c == 7), reads=[w2, C.xn], writes=[pb])
                P.act(t1[k].t[:, :], pb.t[:, :], AF.Sigmoid, reads=[pb], writes=[t1[k]])
                P.tt("dve", o.t[:, sl], pa.t[:, :], t1[k].t[:, :], ALU.mult, reads=[pa, t1[k]], writes=[(o, s0)])
            P.dma(C.gluT_d.t[ch * 128:(ch + 1) * 128, PADL + tt * T:PADL + (tt + 1) * T], o.t[:, :], reads=[o], writes=[(C.gluT_d, (ch, tt))])
    P.barrier()
    P.sb_release(m)


def hyb_conv_gen(P, C, S, hvcol, dww):
    P.phase = "hyb_conv_phase"
    diag = P.sb("diag", [128, 4, CONVW, 128], BF16)
    glu = [P.sb(f"glu{i}", [128, PADL + 512], BF16) for i in range(2)]
    csb = P.sb("csb", [128, 4, 512], F32)
    cbf = P.sb("cbf", [128, 4, 512], BF16)
    csq = P.sb("csq", [128, 4, 512], BF16)
    mean = P.sb("mean", [128, 512], F32)
    msq = P.sb("msq", [128, 512], F32)
    rstd = P.sb("rstdc", [128, 512], F32)
    tmp = [P.sb(f"ctmp{i}", [128, 512], F32) for i in range(2)]
    co = [P.sb(f"co{i}", [128, 512], BF16) for i in range(2)]
    for cc in range(4):
        for k in range(CONVW):
            P.ts("dve", diag.t[:, cc, k, :], C.ident_bf.t[:, :], dww[:, cc * CONVW + k:cc * CONVW + k + 1], None, ALU.mult,
                 reads=[C.ident_bf, C.hyb_dww], writes=[(diag, (cc, k))])
    gi = 0
    oi = 0
    off = PADL - (CONVW - 1)
    yield
    for t0 in range(0, S, 512):
        for cc in range(4):
            P.phase = "hyb_conv_phase"
            g = glu[gi % 2]
            gi += 1
            P.dma(g.t[:, :], C.gluT_d.t[cc * 128:(cc + 1) * 128, t0:t0 + PADL + 512], reads=[C.gluT_d], writes=[g])
            pc = C.PS[6 + cc % 2]
            for k in range(CONVW):
                P.mm(pc.t[:, :], diag.t[:, cc, k, :], g.t[:, off + k:off + k + 512], start=(k == 0), stop=(k == CONVW - 1),
                     reads=[diag, g], writes=[pc])
            P.act(csb.t[:, cc, :], pc.t[:, :], AF.Identity, reads=[pc, C.hvec], writes=[(csb, cc)], bias=hvcol[:, cc:cc + 1])
            P.act(csq.t[:, cc, :], pc.t[:, :], AF.Square, reads=[pc, C.hvec], writes=[(csq, cc)], bias=hvcol[:, cc:cc + 1])
            P.copy("pool", cbf.t[:, cc, :], csb.t[:, cc, :], reads=[(csb, cc)], writes=[(cbf, cc)])
            yield
        P.phase = "hyb_conv_phase"
        p1, p2 = C.PS[6], C.PS[7]
        for cc in range(4):
            P.mm(p1.t[:, :], C.ones_bf.t[:, :], cbf.t[:, cc, :], start=(cc == 0), stop=(cc == 3), reads=[(cbf, cc), C.ones_bf], writes=[p1])
        for cc in range(4):
            P.mm(p2.t[:, :], C.ones_bf.t[:, :], csq.t[:, cc, :], start=(cc == 0), stop=(cc == 3), reads=[(csq, cc), C.ones_bf], writes=[p2])
        P.ts("dve", mean.t[:, :], p1.t[:, :], 1.0 / 512, None, ALU.mult, reads=[p1], writes=[mean])
        P.tt("dve", msq.t[:, :], mean.t[:, :], mean.t[:, :], ALU.mult, reads=[mean], writes=[msq])
        P.stt("dve", msq.t[:, :], p2.t[:, :], 1.0 / 512, msq.t[:, :], ALU.mult, ALU.subtract, reads=[p2, msq], writes=[msq])
        P.act(rstd.t[:, :], msq.t[:, :], AF.Ln, reads=[msq, C.cst], writes=[rstd], bias=C.cst.t[:, 0:1])
        P.act(rstd.t[:, :], rstd.t[:, :], AF.Exp, reads=[rstd], writes=[rstd], scale=-0.5)
        for cc in range(4):
            tm = tmp[cc % 2]
            o = co[oi % 2]
            oi += 1
            P.tt("dve", tm.t[:, :], csb.t[:, cc, :], mean.t[:, :], ALU.subtract, reads=[(csb, cc), mean], writes=[tm])
            P.tt("dve", tm.t[:, :], tm.t[:, :], rstd.t[:, :], ALU.mult, reads=[tm, rstd], writes=[tm])
            P.act(o.t[:, :], tm.t[:, :], AF.Silu, reads=[tm, C.hvec], writes=[o], scale=hvcol[:, 4 + cc:5 + cc], bias=hvcol[:, 8 + cc:9 + cc])
            P.dma(C.catT_d.t[512 + cc * 128:512 + (cc + 1) * 128, t0:t0 + 512], o.t[:, :], reads=[o], writes=[(C.catT_d, (4 + cc, t0))])
        yield


def hyb_attn_phase(P, C, S, co=None):
    m = P.sb_mark()
    if co is not None:
        next(co)
    P.phase = "hyb_attn_phase"
    qT = P.sb("qTa", [128, S], BF16)
    kT = P.sb("kTa", [128, S], BF16)
    acc = [P.sb(f"acc{i}", [65, S], F32) for i in range(2)]
    bc = P.sb("bc", [64, S], F32)
    rden = P.sb("rden", [65, S], F32)
    ao = P.sb("ao", [64, S], BF16)
    NV = 4
    vts = [P.sb(f"vts{i}", [128, 32, 2, 65], BF16) for i in range(NV)]
    NPT = 6
    pts = [P.sb(f"pt{i}", [128, 512], BF16) for i in range(NPT)]
    for v_ in vts:
        P.memset("pool", v_.t[:, :, :, 64:65], 1.0, writes=[(v_, "ones")])
    cnt = {"pt": 0, "ps": 0}
    groups = [(r, c) for (_, r) in DIL for c in range(r)]

    def load_v(hp, g):
        r, c = groups[g]
        nb = (S // r) // 128
        vt = vts[g % NV]
        for hh in range(2):
            src = C.v_d.t[:, hp * 128 + hh * 64:hp * 128 + (hh + 1) * 64]
            src = src[c::r, :].rearrange("(j p) e -> p j e", p=128)
            for jb in range(0, nb, 8):
                je = min(nb, jb + 8)
                P.dma(vt.t[:, jb:je, hh, 0:64], src[:, jb:je, :], reads=[C.v_d], writes=[(vt, (hh, jb))])

    for hp in range(4):
        P.dma(qT.t[:, :], C.qT_d.t[hp * 128:(hp + 1) * 128, :], reads=[C.qT_d], writes=[qT])
        P.dma(kT.t[:, :], C.kT_d.t[hp * 128:(hp + 1) * 128, :], reads=[C.kT_d], writes=[kT])
        first = [True, True]
        items = []
        load_v(hp, 0)
        for g, (r, c) in enumerate(groups):
            L = S // r
            nb = L // 128
            vt = vts[g % NV]
            for hh in range(2):
                hb = hh * 64
                A = acc[hh]
                ptl = {}
                isfirst = (g == 0)

                def tok(j0, nblk, r=r, c=c):
                    return slice(c + r * 128 * j0, c + r * (128 * (j0 + nblk) - 1) + 1, r)

                for j0 in range(0, nb, 2):
                    def stageA(j0=j0, nb=nb, hb=hb, tok=tok, ptl=ptl, g=g, hh=hh, hp=hp):
                        if j0 == 0 and hh == 0 and g + 1 < len(groups):
                            load_v(hp, g + 1)
                        nkb = min(2, nb - j0)
                        pt = pts[cnt["pt"] % NPT]
                        cnt["pt"] += 1
                        ps = C.PS[cnt["ps"] % 4]
                        cnt["ps"] += 1
                        widths = []
                        for jj in range(nkb):
                            j = j0 + jj
                            nq = 2 if j + 1 < nb else 1
                            P.mm(ps.t[:, jj * 256:jj * 256 + nq * 128], kT.t[hb:hb + 64, tok(j, 1)], qT.t[hb:hb + 64, tok(j, nq)],
                                 reads=[kT, qT], writes=[ps])
                            widths.append(nq)
                        wtot = 256 * (nkb - 1) + 128 * widths[-1]
                        P.act(pt.t[:, 0:wtot], ps.t[:, 0:wtot], AF.Exp, reads=[ps], writes=[pt], scale=0.125)
                        P.tt("pool", pt.t[:, 0:wtot], pt.t[:, 0:wtot], C.mask4.t[:, 0:wtot], ALU.mult, reads=[pt, C.mask4], writes=[pt])
                        for jj in range(nkb):
                            ptl[j0 + jj] = (pt, jj * 256)

                    def stageB(j0=j0, nb=nb, tok=tok, ptl=ptl, vt=vt, hh=hh, A=A, isfirst=isfirst):
                        nkb = min(2, nb - j0)
                        for jj in range(nkb):
                            n_ = j0 + jj
                            q4 = n_ % 4
                            po = C.PS[4 + (n_ // 4) % 2]
                            ptc, oc = ptl[n_]
                            has_prev = n_ > 0
                            P.mm(po.t[0:65, q4 * 128:(q4 + 1) * 128], vt.t[:, n_, hh, :], ptc.t[:, oc:oc + 128], start=True, stop=not has_prev,
                                 reads=[vt, ptc], writes=[(po, q4)])
                            if has_prev:
                                ptp, op_ = ptl[n_ - 1]
                                P.mm(po.t[0:65, q4 * 128:(q4 + 1) * 128], vt.t[:, n_ - 1, hh, :], ptp.t[:, op_ + 128:op_ + 256], start=False, stop=True,
                                     reads=[vt, ptp], writes=[(po, q4)])
                            if q4 == 3 or n_ == nb - 1:
                                nq4 = q4 + 1
                                b0 = n_ - q4
                                dsts = A.t[:, tok(b0, nq4)]
                                if isfirst:
                                    P.copy("act", dsts, po.t[0:65, 0:nq4 * 128], reads=[po], writes=[A])
                                else:
                                    P.tt("dve", dsts, dsts, po.t[0:65, 0:nq4 * 128], ALU.add, reads=[po, A], writes=[A])

                    items.append((stageA, stageB))
        LOOK = 3
        for k_ in range(len(items) + LOOK):
            P.phase = "hyb_attn_phase"
            if k_ < len(items):
                items[k_][0]()
            if k_ >= LOOK:
                items[k_ - LOOK][1]()
            if co is not None and k_ % 8 == 7:
                next(co, None)
                P.phase = "hyb_attn_phase"
        for hh in range(2):
            A = acc[hh]
            h = hp * 2 + hh
            P.add("dve", lambda e, A=A: e.reciprocal(out=rden.t[64:65, :], in_=A.t[64:65, :]), reads=[A], writes=[rden])
            P.dma(C.den_d.t[0:1, :], rden.t[64:65, :], reads=[rden], writes=[C.den_d])
            P.dma(bc.t[:, :], C.den_d.t[0:1, :].to_broadcast([64, S]), reads=[C.den_d], writes=[bc])
            P.tt("dve", ao.t[:, :], A.t[0:64, :], bc.t[:, :], ALU.mult, reads=[A, bc], writes=[ao])
            P.dma(C.catT_d.t[h * 64:(h + 1) * 64, :], ao.t[:, :], reads=[ao], writes=[(C.catT_d, ("a", h))])
    if co is not None:
        for _ in co:
            pass
    P.barrier()
    P.sb_release(m)


def mix_out_phase(P, C, S, T, Wo):
    P.phase = "mix_out_phase"
    m = P.sb_mark()
    htile = P.sb("htile", [128, 8, T], F32)
    cat = P.sb("cat", [128, 8, T], BF16)
    wo = P.sb("wo_m", [128, 8, D], BF16)
    hview = C.hT.t.rearrange("(c p) t -> p c t", p=128)
    cview = C.catT_d.t.rearrange("(c p) t -> p c t", p=128)
    P.dma(wo.t[:, :, :], Wo.t.rearrange("(c p) d -> p c d", p=128), reads=[Wo], writes=[wo])
    n = 0
    for tt in range(S // T):
        tsl = slice(tt * T, (tt + 1) * T)
        P.dma(htile.t[:, :, :], hview[:, :, tsl], reads=[(C.hT, tt)], writes=[htile])
        P.dma(cat.t[:, :, :], cview[:, :, tsl], reads=[C.catT_d], writes=[cat])
        for dc in range(8):
            for s0 in range(0, T, 512):
                sl = slice(s0, s0 + 512)
                py = C.PS[n % 2]
                n += 1
                for f in range(8):
                    P.mm(py.t[:, :], wo.t[:, f, dc * 128:(dc + 1) * 128], cat.t[:, f, sl], start=(f == 0), stop=(f == 7), reads=[wo, cat], writes=[py])
                P.tt("dve", htile.t[:, dc, sl], py.t[:, :], htile.t[:, dc, sl], ALU.add, reads=[py, (htile, (dc, s0))], writes=[(htile, (dc, s0))])
        P.dma(hview[:, :, tsl], htile.t[:, :, :], reads=[htile], writes=[(C.hT, tt)])
    P.barrier()
    P.sb_release(m)


GDN_IN = 4112
import os as _os
GDN_STOP = int(_os.environ.get('GDN_STOP', '0'))
QSCALE = 128.0 ** -0.5


def gdn_alloc_scalars(P, C, S):
    NB = S // 128
    sc = Ctx()
    sc.beta = P.sb("g_beta", [128, NB, 8], F32)
    sc.gc = P.sb("g_gc", [128, NB, 8], F32)
    sc.kd = P.sb("g_kd", [128, NB, 8], F32)
    sc.kbg = P.sb("g_kbg", [128, NB, 8], F32)
    sc.egl = P.sb("g_egl", [128, 2, NB, 8], F32)
    return sc


def gdn_proj_phase(P, C, S, gcol, i, sc, Wg):
    P.phase = "gdn_proj_phase"
    m = P.sb_mark()
    NB = S // 128
    xn = P.sb("xnf", [128, 8, S], BF16)
    mA = P.sb_mark()
    ht = [P.sb(f"ht{j}", [128, 8, 256], F32) for j in range(2)]
    C.sq = P.sb("sq", [128, 8, 512], BF16)
    C.rstd = P.sb("rstd", [128, 512], F32)
    hview = C.hT.t.rearrange("(c p) t -> p c t", p=128)
    for t0 in range(0, S, 256):
        h = ht[(t0 // 256) % 2]
        P.dma(h.t[:, :, :], hview[:, :, t0:t0 + 256], reads=[C.hT], writes=[h])
        rmsnorm_tile(P, C, h, gcol, xn, 256, doff=t0)
    dg = P.sb("dg", [128, 24, 4, 128], BF16)
    cw = C.gcw.t[:, i * 96:(i + 1) * 96]
    for ch in range(24):
        for k in range(4):
            P.ts("dve", dg.t[:, ch, k, :], C.ident_bf.t[:, :], cw[:, ch * 4 + k:ch * 4 + k + 1], None, ALU.mult,
                 reads=[C.ident_bf, C.gcw], writes=[(dg, (ch, k))])
    PADG = 4
    pre = [P.sb(f"pre{j}", [128, PADG + S], BF16) for j in range(2)]
    for pr in pre:
        P.memset("pool", pr.t[:, 0:PADG], 0.0, writes=[(pr, "pad")])
    wch = [P.sb(f"wch{j}", [128, 8, 512], BF16) for j in range(2)]
    HS = min(2048, S)
    yhs = [P.sb(f"yh{j}", [128, HS], BF16) for j in range(2)]
    sshs = [P.sb(f"ssh{j}", [128, HS], F32) for j in range(2)]
    sqb = [P.sb(f"sqb{j}", [128, 512], BF16) for j in range(2)]
    ob = [P.sb(f"gob{j}", [128, HS], BF16) for j in range(2)]
    n = 0
    oi = 0
    for cg in range(8):
        w = wch[cg % 2]
        P.dma(w.t[:, :, :], Wg.t[:, cg * 512:(cg + 1) * 512].rearrange("(c p) f -> p c f", p=128), reads=[Wg], writes=[w])
        for ci in range(4):
            ch = cg * 4 + ci
            cs_ = slice(ci * 128, (ci + 1) * 128)
            r0 = (ch % 8) * 128
            if ch < 24:
                pr = pre[ch % 2]
                for s0 in range(0, S, 512):
                    k = n % 2
                    n += 1
                    ps = C.PS[k]
                    for c in range(8):
                        P.mm(ps.t[:, :], w.t[:, c, cs_], xn.t[:, c, s0:s0 + 512], start=(c == 0), stop=(c == 7), reads=[w, xn], writes=[ps])
                    P.copy("act", pr.t[:, PADG + s0:PADG + s0 + 512], ps.t[:, :], reads=[ps], writes=[(pr, s0)])
                dst = (C.qT_g, C.kT_g, C.vT_g)[ch // 8]
                for h0 in range(0, S, HS):
                    o = ob[oi % 2]
                    yh = yhs[oi % 2]
                    ssh = sshs[oi % 2]
                    oi += 1
                    for s0 in range(h0, h0 + HS, 512):
                        k = n % 2
                        n += 1
                        pc = C.PS[2 + k]
                        ls = slice(s0 - h0, s0 - h0 + 512)
                        for kk in range(4):
                            P.mm(pc.t[:, :], dg.t[:, ch, kk, :], pr.t[:, s0 + 1 + kk:s0 + 1 + kk + 512], start=(kk == 0), stop=(kk == 3),
                                 reads=[dg, pr], writes=[pc])
                        if ch >= 16:
                            P.act(o.t[:, ls], pc.t[:, :], AF.Silu, reads=[pc], writes=[(o, s0)])
                        else:
                            sq_ = sqb[k]
                            pn = C.PS[4 + k]
                            P.act(yh.t[:, ls], pc.t[:, :], AF.Silu, reads=[pc], writes=[(yh, s0)])
                            P.tt("dve", sq_.t[:, :], yh.t[:, ls], yh.t[:, ls], ALU.mult, reads=[(yh, s0)], writes=[sq_])
                            P.mm(pn.t[:, :], C.ones_bf.t[:, :], sq_.t[:, :], reads=[sq_, C.ones_bf], writes=[pn])
                            P.copy("dve", ssh.t[:, ls], pn.t[:, :], reads=[pn], writes=[(ssh, s0)])
                    if ch < 16:
                        P.act(ssh.t[:, :], ssh.t[:, :], AF.Ln, reads=[ssh, C.cst], writes=[ssh], bias=C.cst.t[:, 0:1])
                        P.act(ssh.t[:, :], ssh.t[:, :], AF.Exp, reads=[ssh], writes=[ssh], scale=-0.5)
                        if ch < 8:
                            P.stt("dve", o.t[:, :], yh.t[:, :], QSCALE, ssh.t[:, :], ALU.mult, ALU.mult, reads=[yh, ssh], writes=[o])
                        else:
                            P.tt("dve", o.t[:, :], yh.t[:, :], ssh.t[:, :], ALU.mult, reads=[yh, ssh], writes=[o])
                    P.dma(dst.t[r0:r0 + 128, h0:h0 + HS], o.t[:, :], reads=[o], writes=[(dst, (ch % 8, h0))])
            else:
                for h0 in range(0, S, HS):
                    o = ob[oi % 2]
                    oi += 1
                    for s0 in range(h0, h0 + HS, 512):
                        k = n % 2
                        n += 1
                        ps = C.PS[k]
                        for c in range(8):
                            P.mm(ps.t[:, :], w.t[:, c, cs_], xn.t[:, c, s0:s0 + 512], start=(c == 0), stop=(c == 7), reads=[w, xn], writes=[ps])
                        P.act(o.t[:, s0 - h0:s0 - h0 + 512], ps.t[:, :], AF.Silu, reads=[ps], writes=[(o, s0)])
                    P.dma(C.szT_g.t[r0:r0 + 128, h0:h0 + HS], o.t[:, :], reads=[o], writes=[(C.szT_g, (ch % 8, h0))])

    P.barrier()
    P.sb_release(mA)
    if GDN_STOP == 1:
        P.sb_release(m)
        return
    wba = P.sb("wba", [128, 8, 16], BF16)
    P.dma(wba.t[:, :, :], Wg.t[:, 4096:4112].rearrange("(c p) f -> p c f", p=128), reads=[Wg], writes=[wba])
    ba = P.sb("ba", [128, NB, 16], F32)
    x_ = P.sb("gx", [128, NB, 8], F32)
    e_ = P.sb("ge", [128, NB, 8], F32)
    g_ = P.sb("gg", [128, NB, 8], F32)
    gls = P.sb("gls", [128, NB, 8], F32)
    gsm = P.sb("gsm", [128, 16], F32)
    nA = P.sb("nA", [128, 8], F32)
    P.dma(gsm.t[:, :], C.gsm_row_in[0:1, i * 16:(i + 1) * 16].to_broadcast([128, 16]), writes=[gsm])
    P.act(nA.t[:, :], gsm.t[:, 0:8], AF.Exp, reads=[gsm], writes=[nA])
    P.ts("dve", nA.t[:, :], nA.t[:, :], -1.0, None, ALU.mult, reads=[nA], writes=[nA])
    if GDN_STOP == 21:
        P.barrier()
        P.sb_release(m)
        return
    pba = C.PS[6]
    for blk in range(NB):
        for c in range(8):
            P.mm(pba.t[:, blk * 16:(blk + 1) * 16], xn.t[:, c, blk * 128:(blk + 1) * 128], wba.t[:, c, :], start=(c == 0), stop=(c == 7),
                 reads=[wba, xn], writes=[(pba, blk)])
    P.copy("dve", ba.t[:, :, :], pba.t[:, 0:NB * 16].rearrange("p (b f) -> p b f", f=16), reads=[pba], writes=[ba])

    if GDN_STOP == 22:
        P.barrier()
        P.sb_release(m)
        return
    def bc8(ap):
        return ap.unsqueeze(1).to_broadcast([128, NB, 8])

    P.act(sc.beta.t[:, :, :], ba.t[:, :, 0:8], AF.Sigmoid, reads=[ba], writes=[sc.beta])
    P.tt("dve", x_.t[:, :, :], ba.t[:, :, 8:16], bc8(gsm.t[:, 8:16]), ALU.add, reads=[ba, gsm], writes=[x_])
    P.ts("dve", e_.t[:, :, :], x_.t[:, :, :], 30.0, None, ALU.min, reads=[x_], writes=[e_])
    P.act(e_.t[:, :, :], e_.t[:, :, :], AF.Exp, reads=[e_], writes=[e_])
    P.act(e_.t[:, :, :], e_.t[:, :, :], AF.Ln, reads=[e_, C.cst], writes=[e_], bias=C.cst.t[:, 2:3])
    P.tt("dve", e_.t[:, :, :], e_.t[:, :, :], x_.t[:, :, :], ALU.max, reads=[e_, x_], writes=[e_])
    P.tt("dve", g_.t[:, :, :], e_.t[:, :, :], bc8(nA.t[:, :]), ALU.mult, reads=[e_, nA], writes=[g_])
    if GDN_STOP == 23:
        P.barrier()
        P.sb_release(m)
        return
    gflat = g_.t[:, :, :].rearrange("p b h -> p (b h)")
    W8 = NB * 8
    pgc, pl0, pl1 = C.PS[5], C.PS[4], C.PS[3]
    P.mm(pgc.t[:, 0:W8], C.consts.t[:, 1024:1152], gflat, reads=[g_, C.consts], writes=[pgc])
    P.mm(pl0.t[:, 0:W8], C.consts.t[:, 1152:1280], gflat, reads=[g_, C.consts], writes=[pl0])
    P.mm(pl1.t[:, 0:W8], C.consts.t[:, 1280:1408], gflat, reads=[g_, C.consts], writes=[pl1])

    def v3(ap):
        return ap.rearrange("p (b h) -> p b h", h=8)
    if GDN_STOP == 24:
        P.barrier()
        P.sb_release(m)
        return

    P.copy("dve", sc.gc.t[:, :, :], v3(pgc.t[:, 0:W8]), reads=[pgc], writes=[sc.gc])
    P.copy("dve", gls.t[0:64, :, :], v3(pl0.t[0:64, 0:W8]), reads=[pl0], writes=[(gls, 0)])
    P.copy("dve", gls.t[64:128, :, :], v3(pl1.t[64:128, 0:W8]), reads=[pl1], writes=[(gls, 1)])
    if GDN_STOP == 25:
        P.barrier()
        P.sb_release(m)
        return
    P.copy("dve", sc.egl.t[:, 0, :, :], v3(pl0.t[:, 0:W8]), reads=[pl0], writes=[(sc.egl, 0)])
    P.copy("dve", sc.egl.t[:, 1, :, :], v3(pl1.t[:, 0:W8]), reads=[pl1], writes=[(sc.egl, 1)])
    P.act(sc.egl.t[:, :, :, :], sc.egl.t[:, :, :, :], AF.Exp, reads=[sc.egl], writes=[sc.egl])
    if GDN_STOP == 26:
        P.barrier()
        P.sb_release(m)
        return
    P.tt("dve", gls.t[:, :, :], gls.t[:, :, :], sc.gc.t[:, :, :], ALU.subtract, reads=[gls, sc.gc], writes=[gls])
    P.act(sc.kd.t[:, :, :], gls.t[:, :, :], AF.Exp, reads=[gls], writes=[sc.kd])
    P.act(sc.kbg.t[:, :, :], sc.gc.t[:, :, :], AF.Exp, reads=[sc.gc], writes=[sc.kbg])
    P.tt("dve", sc.kbg.t[:, :, :], sc.kbg.t[:, :, :], sc.beta.t[:, :, :], ALU.mult, reads=[sc.kbg, sc.beta], writes=[sc.kbg])

    if GDN_STOP == 2:
        P.barrier()
        P.sb_release(m)
        return
    aT = P.sb("aT", [8, S], F32)
    eT = P.sb("eT", [8, S], F32)
    mk = P.sb("mk", [8, S], F32)
    gcol8 = P.sb("gcol8", [8, 4], F32)
    P.dma(gcol8.t[:, 0:2], C.gsm_col_in[:, i * 2:(i + 1) * 2], writes=[gcol8])
    P.act(gcol8.t[:, 2:3], gcol8.t[:, 0:1], AF.Exp, reads=[gcol8], writes=[gcol8])
    P.ts("dve", gcol8.t[:, 2:3], gcol8.t[:, 2:3], -1.0, None, ALU.mult, reads=[gcol8], writes=[gcol8])
    for s0 in range(0, S, 512):
        k = n % 2
        n += 1
        ps = C.PS[k]
        for c in range(8):
            P.mm(ps.t[0:8, :], wba.t[:, c, 8:16], xn.t[:, c, s0:s0 + 512], start=(c == 0), stop=(c == 7), reads=[wba, xn], writes=[ps])
        P.ts("dve", aT.t[:, s0:s0 + 512], ps.t[0:8, :], gcol8.t[:, 1:2], None, ALU.add, reads=[ps, gcol8], writes=[(aT, s0)])
    P.ts("dve", eT.t[:, :], aT.t[:, :], 30.0, None, ALU.min, reads=[aT], writes=[eT])
    P.act(eT.t[:, :], eT.t[:, :], AF.Exp, reads=[eT], writes=[eT])
    P.act(eT.t[:, :], eT.t[:, :], AF.Ln, reads=[eT, C.cst], writes=[eT], bias=C.cst.t[0:8, 2:3])
    P.tt("dve", eT.t[:, :], eT.t[:, :], aT.t[:, :], ALU.max, reads=[eT, aT], writes=[eT])
    P.ts("dve", eT.t[:, :], eT.t[:, :], gcol8.t[:, 2:3], None, ALU.mult, reads=[eT, gcol8], writes=[eT])
    P.memset("dve", mk.t[:, :], 1.0, writes=[mk])
    P.memset("dve", mk.t[:, 0:S:64], 0.0, writes=[mk])
    P.add("dve", lambda e: e.tensor_tensor_scan(out=aT.t[:, :], data0=mk.t[:, :], data1=eT.t[:, :], initial=0.0, op0=ALU.mult, op1=ALU.add),
          reads=[mk, eT], writes=[aT])
    P.dma(C.gc_d.t[:, :], aT.t[:, :], reads=[aT], writes=[C.gc_d])
    P.barrier()
    P.sb_release(m)


def gdn_core_phase(P, C, S, i, sc):
    P.phase = "gdn_core_phase"
    m = P.sb_mark()
    NB = S // 128
    LT = 256
    BPL = LT // 128

    def ld(name, dt):
        return [P.sb(f"{name}{j}", [128, 8, LT], dt) for j in range(2)]

    qT, kT, vT, szT, gcB = ld("gq", BF16), ld("gk", BF16), ld("gv", BF16), ld("gsz", BF16), ld("gcB", F32)
    Sst = P.sb("Sst", [128, 8, 128], F32)
    Sbf = P.sb("Sbf", [128, 8, 128], BF16)
    P.memset("pool", Sst.t[:, :, :], 0.0, writes=[Sst])
    P.memset("pool", Sbf.t[:, :, :], 0.0, writes=[Sbf])

    class Wset:
        pass

    Ws = []
    for wi_ in range(2):
        W = Wset()
        for nm, dt in (("egc", F32), ("qd", BF16), ("kbg", BF16), ("kd", BF16), ("vb", BF16), ("E", F32), ("En", F32),
                       ("attnT", BF16), ("X", BF16), ("XT", BF16), ("Pa", BF16), ("Pb", BF16), ("PTa", BF16), ("PTb", BF16),
                       ("TTa", BF16), ("TTb", BF16), ("u", F32), ("wT", BF16), ("vn", BF16), ("oT", F32), ("osq", BF16),
                       ("rn", F32), ("ob", BF16)):
            setattr(W, nm, P.sb(f"w{wi_}_" + nm, [128, 8, 128], dt))
        Ws.append(W)

    maskU = C.consts.t[:, 1024:1152].unsqueeze(1).to_broadcast([128, 8, 128])
    maskLn = C.consts.t[:, 1408:1536].unsqueeze(1).to_broadcast([128, 8, 128])
    identb = C.ident_bf.t[:, :].unsqueeze(1).to_broadcast([128, 8, 128])
    ngcol = C.gng.t[:, i:i + 1]

    def bcf(ap):
        return ap.unsqueeze(2).to_broadcast([128, 8, 128])

    def h4(ps, lo=0, hi=128):
        return ps.t[lo:hi, :].rearrange("p (h d) -> p h d", h=4)

    def loads(lt):
        li = lt % 2
        tsl = slice(lt * LT, (lt + 1) * LT)
        for dstl, srcd in ((qT, C.qT_g), (kT, C.kT_g), (vT, C.vT_g), (szT, C.szT_g)):
            P.dma(dstl[li].t[:, :, :], srcd.t[:, tsl].rearrange("(h p) t -> p h t", p=128), reads=[srcd], writes=[dstl[li]])
        P.dma(gcB[li].t[:, :, :], C.gc_d.t[:, tsl].unsqueeze(0).to_broadcast([128, 8, LT]), reads=[C.gc_d], writes=[gcB[li]])

    def block_pre(b, W):
        P.phase = "gdn_pre"
        lt = b // BPL
        li = lt % 2
        off = (b % BPL) * 128
        bsl = slice(off, off + 128)
        q_, k_, v_, gb_ = qT[li], kT[li], vT[li], gcB[li]
        pq_ = b % 2
        P.act(W.egc.t[:, :, :], gb_.t[:, :, bsl], AF.Exp, reads=[gb_], writes=[W.egc])
        P.tt("dve", W.qd.t[:, :, :], q_.t[:, :, bsl], W.egc.t[:, :, :], ALU.mult, reads=[q_, W.egc], writes=[W.qd])
        pk = C.PS[0].t[:, :].bitcast(BF16)
        for h in range(8):
            P.tr(pk[:, h * 128:(h + 1) * 128], k_.t[:, h, bsl], C.ident_bf.t[:, :], reads=[k_, C.ident_bf], writes=[(C.PS[0], h)])
        pk3 = pk.rearrange("p (h d) -> p h d", h=8)
        P.tt("dve", W.kbg.t[:, :, :], pk3, bcf(sc.kbg.t[:, b, :]), ALU.mult, reads=[C.PS[0], sc.kbg], writes=[W.kbg])
        P.tt("dve", W.kd.t[:, :, :], pk3, bcf(sc.kd.t[:, b, :]), ALU.mult, reads=[C.PS[0], sc.kd], writes=[W.kd])
        pv = C.PS[1].t[:, :].bitcast(BF16)
        for h in range(8):
            P.tr(pv[:, h * 128:(h + 1) * 128], v_.t[:, h, bsl], C.ident_bf.t[:, :], reads=[v_, C.ident_bf], writes=[(C.PS[1], h)])
        P.tt("dve", W.vb.t[:, :, :], pv.rearrange("p (h d) -> p h d", h=8), bcf(sc.beta.t[:, b, :]), ALU.mult,
             reads=[C.PS[1], sc.beta], writes=[W.vb])
        yield
        for half in range(2):
            pg, pq = C.PS[2 + half], C.PS[4 + half]
            for hh in range(4):
                h = half * 4 + hh
                P.mm(pg.t[:, hh * 128:(hh + 1) * 128], k_.t[:, h, bsl], k_.t[:, h, bsl], reads=[k_], writes=[(pg, hh)])
            for hh in range(4):
                h = half * 4 + hh
                P.mm(pq.t[:, hh * 128:(hh + 1) * 128], k_.t[:, h, bsl], q_.t[:, h, bsl], reads=[k_, q_], writes=[(pq, hh)])
        P.tt("pool", W.E.t[:, :, :], gb_.t[:, :, bsl], bcf(sc.gc.t[:, b, :]), ALU.subtract, reads=[gb_, sc.gc], writes=[W.E])
        P.ts("dve", W.En.t[:, :, :], W.E.t[:, :, :], -1.0, 0.0, ALU.mult, ALU.min, reads=[W.E], writes=[W.En])
        P.ts("dve", W.E.t[:, :, :], W.E.t[:, :, :], 0.0, None, ALU.min, reads=[W.E], writes=[W.E])
        P.act(W.E.t[:, :, :], W.E.t[:, :, :], AF.Exp, reads=[W.E], writes=[W.E])
        P.act(W.En.t[:, :, :], W.En.t[:, :, :], AF.Exp, reads=[W.En], writes=[W.En])
        P.tt("pool", W.E.t[:, :, :], W.E.t[:, :, :], maskU, ALU.mult, reads=[W.E, C.consts], writes=[W.E])
        P.tt("pool", W.En.t[:, :, :], W.En.t[:, :, :], maskLn, ALU.mult, reads=[W.En, C.consts], writes=[W.En])
        P.tt("pool", W.En.t[:, :, :], W.En.t[:, :, :], bcf(sc.beta.t[:, b, :]), ALU.mult, reads=[W.En, sc.beta], writes=[W.En])
        for half in range(2):
            hs = slice(half * 4, half * 4 + 4)
            P.tt("dve", W.attnT.t[:, hs, :], h4(C.PS[4 + half]), W.E.t[:, hs, :], ALU.mult, reads=[C.PS[4 + half], W.E], writes=[(W.attnT, half)])
            P.tt("dve", W.X.t[:, hs, :], h4(C.PS[2 + half]), W.En.t[:, hs, :], ALU.mult, reads=[C.PS[2 + half], W.En], writes=[(W.X, half)])
        yield
        px = C.PS[pq_].t[:, :].bitcast(BF16)
        for h in range(8):
            P.tr(px[:, h * 128:(h + 1) * 128], W.X.t[:, h, :], C.ident_bf.t[:, :], reads=[W.X, C.ident_bf], writes=[(C.PS[pq_], h)])
        P.copy("act", W.XT.t[:, :, :], px.rearrange("p (h d) -> p h d", h=8), reads=[C.PS[pq_]], writes=[W.XT])
        P.tt("pool", W.TTa.t[:, :, :], W.XT.t[:, :, :], identb, ALU.add, reads=[W.XT, C.ident_bf], writes=[W.TTa])
        yield
        Pc, PTc, TTc = W.X, W.XT, W.TTa
        pbufs = [(W.Pa, W.PTa), (W.Pb, W.PTb)]
        ttbufs = [W.TTb, W.TTa]
        for kk in range(1, 6):
            Pn, PTn = pbufs[kk % 2]
            TTn = ttbufs[(kk - 1) % 2]
            for half in range(2):
                hs = slice(half * 4, half * 4 + 4)
                pp = C.PS[2 + half]
                for hh in range(4):
                    h = half * 4 + hh
                    P.mm(pp.t[:, hh * 128:(hh + 1) * 128], PTc.t[:, h, :], Pc.t[:, h, :], reads=[PTc, Pc], writes=[(pp, hh)])
                P.copy("act", Pn.t[:, hs, :], h4(pp), reads=[pp], writes=[(Pn, half)])
                if kk < 5:
                    ppt = C.PS[4 + half]
                    for hh in range(4):
                        h = half * 4 + hh
                        P.mm(ppt.t[:, hh * 128:(hh + 1) * 128], Pc.t[:, h, :], PTc.t[:, h, :], reads=[PTc, Pc], writes=[(ppt, hh)])
                    P.copy("act" if half == 0 else "dve", PTn.t[:, hs, :], h4(ppt), reads=[ppt], writes=[(PTn, half)])
            yield
            for half in range(2):
                hs = slice(half * 4, half * 4 + 4)
                pt = C.PS[6 + half]
                for hh in range(4):
                    h = half * 4 + hh
                    P.mm(pt.t[:, hh * 128:(hh + 1) * 128], Pn.t[:, h, :], TTc.t[:, h, :], start=True, stop=False, reads=[Pn, TTc], writes=[(pt, hh)])
                    P.mm(pt.t[:, hh * 128:(hh + 1) * 128], C.ident_bf.t[:, :], TTc.t[:, h, :], start=False, stop=True, reads=[C.ident_bf, TTc], writes=[(pt, hh)])
                P.copy("act" if half == 0 else "dve", TTn.t[:, hs, :], h4(pt), reads=[pt], writes=[(TTn, half)])
            Pc, PTc, TTc = Pn, PTn, TTn
            yield
        for half in range(2):
            hs = slice(half * 4, half * 4 + 4)
            pu = C.PS[6 + half]
            for hh in range(4):
                h = half * 4 + hh
                P.mm(pu.t[:, hh * 128:(hh + 1) * 128], TTc.t[:, h, :], W.vb.t[:, h, :], reads=[TTc, W.vb], writes=[(pu, hh)])
            P.copy("act", W.u.t[:, hs, :], h4(pu), reads=[pu], writes=[(W.u, half)])
        yield
        for half in range(2):
            hs = slice(half * 4, half * 4 + 4)
            pw = C.PS[6 + half]
            for hh in range(4):
                h = half * 4 + hh
                P.mm(pw.t[:, hh * 128:(hh + 1) * 128], W.kbg.t[:, h, :], TTc.t[:, h, :], reads=[TTc, W.kbg], writes=[(pw, hh)])
            P.copy("dve", W.wT.t[:, hs, :], h4(pw), reads=[pw], writes=[(W.wT, half)])
        yield

    def block_scan(b, W):
        P.phase = "gdn_scan"
        lt = b // BPL
        li = lt % 2
        off = (b % BPL) * 128
        bsl = slice(off, off + 128)
        sz_ = szT[li]
        for cix in range(2):
            lo, hi = cix * 64, cix * 64 + 64
            cs = slice(lo, hi)
            for half in range(2):
                hs = slice(half * 4, half * 4 + 4)
                pW = C.PS[2 + half]
                for hh in range(4):
                    h = half * 4 + hh
                    P.mm(pW.t[cs, hh * 128:(hh + 1) * 128], W.wT.t[:, h, cs], Sbf.t[:, h, :], reads=[W.wT, Sbf], writes=[(pW, hh)])
                P.tt("dve", W.vn.t[cs, hs, :], W.u.t[cs, hs, :], h4(pW, lo, hi), ALU.subtract, reads=[pW, W.u], writes=[(W.vn, (cix, half))])
            pO = C.PS[1]
            for h in range(8):
                P.mm(pO.t[:, h * 64:(h + 1) * 64], Sbf.t[:, h, :], W.qd.t[:, h, cs], start=True, stop=False, reads=[Sbf, W.qd], writes=[(pO, h)])
                P.mm(pO.t[:, h * 64:(h + 1) * 64], W.vn.t[cs, h, :], W.attnT.t[cs, h, cs], start=False, stop=True, reads=[W.vn, W.attnT], writes=[(pO, h)])
            P.copy("act", W.oT.t[:, :, cs], pO.t[:, :].rearrange("p (h t) -> p h t", h=8), reads=[pO], writes=[(W.oT, cix)])
            for half in range(2):
                pS = C.PS[4 + half]
                for hh in range(4):
                    h = half * 4 + hh
                    P.mm(pS.t[:, hh * 128:(hh + 1) * 128], W.kd.t[cs, h, :], W.vn.t[cs, h, :], reads=[W.kd, W.vn], writes=[(pS, hh)])
            P.tt("dve", Sst.t[:, :, :], Sst.t[:, :, :], bcf(sc.egl.t[:, cix, b, :]), ALU.mult, reads=[Sst, sc.egl], writes=[Sst])
            for half in range(2):
                hs = slice(half * 4, half * 4 + 4)
                P.tt("dve", Sst.t[:, hs, :], Sst.t[:, hs, :], h4(C.PS[4 + half]), ALU.add, reads=[C.PS[4 + half], Sst], writes=[Sst])
            P.copy("act", Sbf.t[:, :, :], Sst.t[:, :, :], reads=[Sst], writes=[Sbf])
        P.tt("pool", W.osq.t[:, :, :], W.oT.t[:, :, :], W.oT.t[:, :, :], ALU.mult, reads=[W.oT], writes=[W.osq])
        for half in range(2):
            hs = slice(half * 4, half * 4 + 4)
            pn = C.PS[6 + half]
            P.mm(pn.t[:, :], C.ones_bf.t[:, :], W.osq.t[:, hs, :], reads=[W.osq, C.ones_bf], writes=[pn])
            P.act(W.rn.t[:, hs, :], h4(pn), AF.Ln, reads=[pn, C.cst], writes=[(W.rn, half)], scale=1.0 / 128, bias=C.cst.t[:, 0:1])
        P.act(W.rn.t[:, :, :], W.rn.t[:, :, :], AF.Exp, reads=[W.rn], writes=[W.rn], scale=-0.5)
        P.stt("dve", W.rn.t[:, :, :], W.oT.t[:, :, :], ngcol, W.rn.t[:, :, :], ALU.mult, ALU.mult, reads=[W.oT, W.rn, C.gng], writes=[W.rn])
        P.tt("pool", W.ob.t[:, :, :], W.rn.t[:, :, :], sz_.t[:, :, bsl], ALU.mult, reads=[W.rn, sz_], writes=[W.ob])
        P.dma(C.catT_d.t[:, b * 128:(b + 1) * 128].rearrange("(h p) t -> p h t", p=128), W.ob.t[:, :, :], reads=[W.ob], writes=[(C.catT_d, ("g", b))])

    loads(0)
    for pr_ in range(0, NB, 2):
        blocks = [b for b in (pr_, pr_ + 1) if b < NB]
        if (pr_ // BPL + 1) * LT < S and pr_ % BPL == 0:
            loads(pr_ // BPL + 1)
        gens = [block_pre(b, Ws[b % 2]) for b in blocks]
        alive = list(gens)
        while alive:
            for g_ in list(alive):
                try:
                    P.phase = "gdn_pre"
                    next(g_)
                except StopIteration:
                    alive.remove(g_)
        for b in blocks:
            block_scan(b, Ws[b % 2])
    P.barrier()
    P.sb_release(m)


def gdn_layer(P, C, S, T, gmix, i):
    m = P.sb_mark()
    sc = gdn_alloc_scalars(P, C, S)
    gdn_proj_phase(P, C, S, gmix, i, sc, C.Wg[i])
    if GDN_STOP == 0 or GDN_STOP > 3:
        gdn_core_phase(P, C, S, i, sc)
    P.sb_release(m)
    mix_out_phase(P, C, S, T, C.Wgo[i])

def build(S=4096, depth=4, T=1024, mixers=True, dbg=None):
    nc = bass.Bass("TRN2", target_bir_lowering=False)
    P = Prog(nc)
    C = Ctx()
    C.S = S
    NE = (depth + 1) // 2
    NO = depth // 2

    def din(name, shape, dt=F32):
        return nc.dram_tensor(name, list(shape), dt, kind="ExternalInput")

    xT = din("xT", [D, S])
    pos = din("pos", [1, S], I32)
    gains_in = din("gains", [128, depth * 24 + 8])
    consts_in = din("consts", [128, 2048])
    ffn_w_in = [[din(f"ffn{j + 1}_w_in_{l}", [D, 2 * FF]) for j in range(2)] for l in range(depth)]
    ffn_w_out = [[din(f"ffn{j + 1}_w_out_{l}", [FF, D]) for j in range(2)] for l in range(depth)]
    if mixers:
        hyb_w_in = [din(f"hyb_w_in_{i}", [D, HYB_IN]) for i in range(NE)]
        hyb_w_out = [din(f"hyb_w_out_{i}", [D, D]) for i in range(NE)]
        hyb_dww_in = din("hyb_dww", [128, NE * 4 * CONVW])
        hvec_in = din("hvec", [128, NE * 12])
        gdn_w_in = [din(f"gdn_w_in_{i}", [D, GDN_IN]) for i in range(NO)]
        gdn_w_out = [din(f"gdn_w_out_{i}", [D, D]) for i in range(NO)]
        gcw_in = din("gcw", [128, max(NO, 1) * 96])
        gng_in = din("gng", [128, max(NO, 1)])
        C.gsm_row_in = din("gsm_row", [1, max(NO, 1) * 16])
        C.gsm_col_in = din("gsm_col", [8, max(NO, 1) * 2])
    C.outT = P.dram("outT", [D, S], F32, kind="ExternalOutput")
    C.hT = P.dram("hT", [D, S], F32)
    C.w_in_bf = [[P.dram(f"w_in_bf_{l}_{j}", [D, 2 * FF], BF16) for j in range(2)] for l in range(depth)]
    C.w_out_bf = [[P.dram(f"w_out_bf_{l}_{j}", [FF, D], BF16) for j in range(2)] for l in range(depth)]
    if mixers:
        C.Wh = [P.dram(f"Wh_{i}", [D, HYB_IN], BF16) for i in range(NE)]
        C.Whp = [P.dram(f"Whp_{i}", [D, 1024], BF16) for i in range(NE)]
        C.Who = [P.dram(f"Who_{i}", [D, D], BF16) for i in range(NE)]
        C.qT_d = P.dram("qT_d", [AW, S], BF16)
        C.kT_d = P.dram("kT_d", [AW, S], BF16)
        C.v_d = P.dram("v_d", [S, AW], BF16)
        C.gluT_d = P.dram("gluT_d", [512, PADL + S], BF16)
        C.catT_d = P.dram("catT_d", [D, S], BF16)
        C.cosT = P.dram("cosT", [128, S], F32)
        C.sinT = P.dram("sinT", [128, S], F32)
        C.den_d = P.dram("den_d", [1, S], F32)
        C.Wg = [P.dram(f"Wg_{i}", [D, GDN_IN], BF16) for i in range(NO)]
        C.Wgo = [P.dram(f"Wgo_{i}", [D, D], BF16) for i in range(NO)]
        C.qT_g = P.dram("qT_g", [D, S], BF16)
        C.kT_g = P.dram("kT_g", [D, S], BF16)
        C.vT_g = P.dram("vT_g", [D, S], BF16)
        C.szT_g = P.dram("szT_g", [D, S], BF16)
        C.gc_d = P.dram("gc_d", [8, S], F32)
    if dbg:
        C.dbg = {k: P.dram("dbg_" + k, shp, dt, kind="ExternalOutput") for k, (shp, dt) in dbg.items()}

    C.PS = [P.ps(f"ps{i}", [128, 512], F32) for i in range(8)]
    C.gains = P.sb("gains", [128, depth * 24 + 8], F32)
    C.consts = P.sb("consts", [128, 2048], F32)
    C.ones_bf = P.sb("ones_bf", [128, 128], BF16)
    C.ident_bf = P.sb("ident_bf", [128, 128], BF16)
    C.mask4 = P.sb("mask4", [128, 512], BF16)
    C.cst = P.sb("cst", [128, 8], F32)
    P.dma(C.gains.t[:, :], gains_in[:, :], writes=[C.gains])
    P.dma(C.consts.t[:, :], consts_in[:, :], writes=[C.consts])
    P.memset("pool", C.cst.t[:, 0:1], EPS, writes=[C.cst])
    P.memset("pool", C.cst.t[:, 1:2], -float(np.pi), writes=[C.cst])
    P.memset("pool", C.cst.t[:, 2:3], 1.0, writes=[C.cst])
    P.memset("pool", C.ones_bf.t[:, :], 1.0, writes=[C.ones_bf])
    P.copy("pool", C.ident_bf.t[:, :], C.consts.t[:, 0:128], reads=[C.consts], writes=[C.ident_bf])
    P.copy("pool", C.mask4.t[:, :], C.consts.t[:, 128:640], reads=[C.consts], writes=[C.mask4])
    if mixers:
        C.hyb_dww = P.sb("hyb_dww", [128, NE * 4 * CONVW], F32)
        C.hvec = P.sb("hvec", [128, NE * 12], F32)
        P.dma(C.hyb_dww.t[:, :], hyb_dww_in[:, :], writes=[C.hyb_dww])
        P.dma(C.hvec.t[:, :], hvec_in[:, :], writes=[C.hvec])
        C.gcw = P.sb("gcw", [128, max(NO, 1) * 96], F32)
        C.gng = P.sb("gng", [128, max(NO, 1)], F32)
        P.dma(C.gcw.t[:, :], gcw_in[:, :], writes=[C.gcw])
        P.dma(C.gng.t[:, :], gng_in[:, :], writes=[C.gng])
    persist = P.sb_mark()

    def alloc_cv():
        C.cv32 = [P.sb(f"cv32_{i}", [128, 1024], F32) for i in range(3)]
        C.cv16 = [P.sb(f"cv16_{i}", [128, 1024], BF16) for i in range(2)]

    alloc_cv()

    P.dma(C.hT.t[:, :], xT[:, :], writes=[C.hT])

    C.cvi = 0
    C.cvj = 0

    def jobs_ffn(l, j):
        return cast_jobs(ffn_w_in[l][j], C.w_in_bf[l][j], D, 2 * FF) + cast_jobs(ffn_w_out[l][j], C.w_out_bf[l][j], FF, D)

    def jobs_mixer(l):
        if not mixers:
            return []
        i = l // 2
        if l % 2 == 0:
            return cast_jobs(hyb_w_in[i], C.Wh[i], D, HYB_IN) + perm_jobs(hyb_w_in[i], C.Whp[i]) + cast_jobs(hyb_w_out[i], C.Who[i], D, D)
        return cast_jobs(gdn_w_in[i], C.Wg[i], D, GDN_IN) + cast_jobs(gdn_w_out[i], C.Wgo[i], D, D)

    run_cast_jobs(P, C, jobs_ffn(0, 0))

    if mixers and NE > 0:
        zt = C.cv16[0]
        P.memset("pool", zt.t[:, 0:PADL], 0.0, writes=[zt])
        for cc in range(4):
            P.dma(C.gluT_d.t[cc * 128:(cc + 1) * 128, 0:PADL], zt.t[:, 0:PADL], reads=[zt], writes=[(C.gluT_d, ("pad", cc))])
        P.barrier()
        rope_tables(P, C, S, pos)

    def alloc_ffn():
        alloc_cv()
        C.htiles = [P.sb(f"htile{i}", [128, 8, T], F32) for i in range(2)]
        C.xn = P.sb("xn", [128, 8, T], BF16)
        C.gT = P.sb("gT", [128, NFC, T], BF16)
        C.sq = P.sb("sq", [128, 8, 512], BF16)
        C.rstd = P.sb("rstd", [128, 512], F32)
        C.sg = [P.sb(f"sg{i}", [128, 512], F32) for i in range(2)]
        C.wg = [P.sb(f"wg{i}", [128, 8, 256], BF16) for i in range(2)]
        C.wu = [P.sb(f"wu{i}", [128, 8, 256], BF16) for i in range(2)]
        C.wo = [P.sb(f"wo{i}", [128, NFC, 256], BF16) for i in range(2)]

    for l in range(depth):
        P.barrier()
        P.sb_release(persist)
        alloc_ffn()
        ffn_phase(P, C, S, T, C.gains.t[:, l * 24:l * 24 + 8], C.w_in_bf[l][0], C.w_out_bf[l][0],
                  jobs=jobs_mixer(l) + jobs_ffn(l, 1))
        if mixers:
            P.barrier()
            P.sb_release(persist)
            gmix = C.gains.t[:, l * 24 + 8:l * 24 + 16]
            if l % 2 == 0:
                i = l // 2
                hyb_proj_phase(P, C, S, T, gmix, C.Wh[i], C.Whp[i], None, l)
                hyb_attn_phase(P, C, S, co=hyb_conv_gen(P, C, S, C.hvec.t[:, i * 12:(i + 1) * 12],
                                                        C.hyb_dww.t[:, i * 4 * CONVW:(i + 1) * 4 * CONVW]))
                mix_out_phase(P, C, S, T, C.Who[i])
            else:
                gdn_layer(P, C, S, T, gmix, l // 2)
            P.sb_release(persist)
            alloc_ffn()
        ffn_phase(P, C, S, T, C.gains.t[:, l * 24 + 16:l * 24 + 24], C.w_in_bf[l][1], C.w_out_bf[l][1],
                  jobs=(jobs_ffn(l + 1, 0) if l + 1 < depth else None))

    P.barrier()
    P.sb_release(persist)
    C.htile = P.sb("htile", [128, 8, T], F32)
    C.fo = P.sb("fo", [128, 8, T], F32)
    C.sq = P.sb("sq", [128, 8, 512], BF16)
    C.rstd = P.sb("rstd", [128, 512], F32)
    final_phase(P, C, S, T, C.gains.t[:, depth * 24:depth * 24 + 8], C.outT)
    P.emit()
    return nc, P


def make_consts():
    f32 = np.float32
    c = np.zeros((128, 2048), f32)
    c[:, 0:128] = np.eye(128, dtype=f32)
    k = np.arange(128)[:, None]
    q = np.arange(128)[None, :]
    cur = (q >= k).astype(f32)
    prev = (q <= k).astype(f32)
    c[:, 128:640] = np.concatenate([cur, prev, cur, prev], axis=1)
    inv = np.power(f32(500000.0), -np.arange(0, 16, 2, dtype=f32) / f32(16)).astype(f32)
    for p in range(128):
        i = p % 64
        c[p, 640] = inv[i % 8] if i < 16 else 0.0
    same = (k // 64) == (q // 64)
    c[:, 1024:1152] = ((q >= k) & same).astype(f32)
    c[:, 1152:1280] = (k < 64).astype(f32) * np.ones((1, 128), f32)
    c[:, 1280:1408] = (k >= 64).astype(f32) * np.ones((1, 128), f32)
    c[:, 1408:1536] = -((k > q) & same).astype(f32)
    return c


def host_inputs(inputs, S=4096, depth=4, mixers=True):
    f32 = np.float32
    NE = (depth + 1) // 2
    x = np.asarray(inputs["x"], dtype=f32)
    B = x.shape[0]
    gains = np.zeros((128, depth * 24 + 8), f32)
    for l in range(depth):
        for j, nm in enumerate(("ffn1_norm", "mix_norm", "ffn2_norm")):
            gains[:, l * 24 + j * 8:l * 24 + j * 8 + 8] = np.asarray(inputs[nm][l], f32).reshape(8, 128).T
    gains[:, depth * 24:] = np.asarray(inputs["final_norm"], f32).reshape(8, 128).T
    shared = {"gains": gains, "consts": make_consts()}
    for l in range(depth):
        for j in range(2):
            shared[f"ffn{j + 1}_w_in_{l}"] = np.ascontiguousarray(inputs[f"ffn{j + 1}_w_in"][l], dtype=f32)
            shared[f"ffn{j + 1}_w_out_{l}"] = np.ascontiguousarray(inputs[f"ffn{j + 1}_w_out"][l], dtype=f32)
    if mixers:
        dww = np.zeros((128, NE * 4 * CONVW), f32)
        hvec = np.zeros((128, NE * 12), f32)
        for i in range(NE):
            shared[f"hyb_w_in_{i}"] = np.ascontiguousarray(inputs["hyb_w_in"][i], dtype=f32)
            shared[f"hyb_w_out_{i}"] = np.ascontiguousarray(inputs["hyb_w_out"][i], dtype=f32)
            w = np.asarray(inputs["hyb_dw_w"][i], f32)
            for cc in range(4):
                dww[:, (i * 4 + cc) * CONVW:(i * 4 + cc + 1) * CONVW] = w[:, cc * 128:(cc + 1) * 128].T
            for j, nm in enumerate(("hyb_dw_b", "hyb_ln_g", "hyb_ln_b")):
                hvec[:, i * 12 + j * 4:i * 12 + j * 4 + 4] = np.asarray(inputs[nm][i], f32).reshape(4, 128).T
        shared["hyb_dww"] = dww
        shared["hvec"] = hvec
        NO = depth // 2
        gcw = np.zeros((128, max(NO, 1) * 96), f32)
        gng = np.zeros((128, max(NO, 1)), f32)
        grow = np.zeros((1, max(NO, 1) * 16), f32)
        gcolm = np.zeros((8, max(NO, 1) * 2), f32)
        for i in range(NO):
            shared[f"gdn_w_in_{i}"] = np.ascontiguousarray(inputs["gdn_w_in"][i], dtype=f32)
            shared[f"gdn_w_out_{i}"] = np.ascontiguousarray(inputs["gdn_w_out"][i], dtype=f32)
            cw = np.asarray(inputs["gdn_conv_w"][i], f32)
            for ch in range(24):
                gcw[:, i * 96 + ch * 4:i * 96 + ch * 4 + 4] = cw[:, ch * 128:(ch + 1) * 128].T
            gng[:, i] = np.asarray(inputs["gdn_norm_g"][i], f32)
            grow[0, i * 16:i * 16 + 8] = np.asarray(inputs["gdn_A_log"][i], f32)
            grow[0, i * 16 + 8:i * 16 + 16] = np.asarray(inputs["gdn_dt_bias"][i], f32)
            gcolm[:, i * 2] = np.asarray(inputs["gdn_A_log"][i], f32)
            gcolm[:, i * 2 + 1] = np.asarray(inputs["gdn_dt_bias"][i], f32)
        shared["gcw"] = gcw
        shared["gng"] = gng
        shared["gsm_row"] = grow
        shared["gsm_col"] = gcolm
    maps = []
    for b in range(B):
        m = dict(shared)
        m["xT"] = np.ascontiguousarray(x[b, :S].T)
        m["pos"] = np.ascontiguousarray(np.asarray(inputs["positions"])[b:b + 1, :S].astype(np.int32))
        maps.append(m)
    return maps


_CACHE = {}


def kernel(**inputs):
    S, depth = 4096, 4
    maps = host_inputs(inputs, S, depth)
    nc, P = build(S, depth)
    res = run_bass_kernel_spmd(nc, maps, core_ids=list(range(NCORES)))
    outs = [np.asarray(r["outT"]) for r in res.results]
    out = np.stack([o.T for o in outs], axis=0)
    return np.ascontiguousarray(out.astype(np.float32))
```
